# Optimizing a Trainium2 kernel written in Bass

```python
import jax
import jax.numpy as jnp
from jax import lax
import numpy as np

D_MODEL = 1024
BATCH = 32
SEQ = 256
DEPTH = 1
DEC_BATCH = 2
DEC_SEQ = 2048
PAST_LEN = 512

GRID_W = 64
N_HEADS = 8
HEAD_DIM = 64
D_NA = N_HEADS * HEAD_DIM
D_POOL = 512
POOL_WINDOWS = (2, 4, 8, 16)
N_POOL_GROUPS = len(POOL_WINDOWS)
POOL_GROUP = D_POOL // N_POOL_GROUPS
WIN_H = 8
WIN_W = 16
QB_W = 16
KB_W = 2 * QB_W
NQB = GRID_W // QB_W
D_FF = 2816
CONV_W = 3
Q_BLOCK = 128
EPS = 1e-6
NEG_INF = -1e30
D_IN = D_POOL + 3 * D_NA + 2 * D_MODEL
IN_SPLITS = (D_POOL, D_POOL + D_NA, D_POOL + 2 * D_NA, D_POOL + 3 * D_NA, D_POOL + 3 * D_NA + D_MODEL)

kernel_name = "hybrid_pool_natten_dit_step"


def _rms(x, g):
    xf = x.astype(jnp.float32)
    y = xf * lax.rsqrt(jnp.mean(xf * xf, axis=-1, keepdims=True) + EPS)
    return (y * g.astype(jnp.float32)).astype(x.dtype)


def _modulation(cond, w_mod, b_mod):
    m = (jax.nn.silu(cond) @ w_mod + b_mod)[:, None, :]
    return jnp.split(m, 6, axis=-1)


def _modulated_norm(x, g, shift, scale):
    return _rms(x, g) * (1 + scale) + shift


def _mixer_inputs(h, w_in, q_g, k_g):
    B, L, _ = h.shape
    z = h @ w_in
    p, q, k, v, gp, gn = jnp.split(z, IN_SPLITS, axis=-1)

    def heads(t):
        return t.reshape(B, L, N_HEADS, HEAD_DIM).transpose(0, 2, 1, 3)

    q = _rms(heads(q), q_g)
    k = _rms(heads(k), k_g)
    return p, q, k, heads(v), gp, gn


def _pool_mixer(p, w_pool, s_pool):
    B, L, _ = p.shape
    pf = p.astype(jnp.float32)
    cs = jnp.concatenate([jnp.zeros((B, 1, D_POOL), jnp.float32), jnp.cumsum(pf, axis=1)], axis=1)
    t = np.arange(L)
    groups = []
    for gi, w in enumerate(POOL_WINDOWS):
        lo = np.clip(t - w // 2, 0, L)
        hi = np.clip(t + w - w // 2, 0, L)
        cnt = jnp.asarray((hi - lo).astype(np.float32))[None, :, None]
        sl = slice(gi * POOL_GROUP, (gi + 1) * POOL_GROUP)
        seg = cs[:, :, sl]
        groups.append((seg[:, hi] - seg[:, lo]) / cnt - pf[:, :, sl])
    pooled = jnp.stack(groups, axis=2).astype(p.dtype)
    mixed = jnp.einsum("blgc,gcd->blgd", pooled, w_pool).reshape(B, L, D_POOL)
    return mixed * s_pool


def _context_attention(q, k, v):
    B, H, Lc, dh = q.shape
    nb = Lc // Q_BLOCK
    scale = HEAD_DIM ** -0.5
    qb = q.reshape(B, H, nb, Q_BLOCK, dh).transpose(2, 0, 1, 3, 4)

    def blk(qi):
        s = jnp.einsum("bhqd,bhkd->bhqk", qi, k).astype(jnp.float32) * scale
        pr = jax.nn.softmax(s, axis=-1).astype(v.dtype)
        return jnp.einsum("bhqk,bhkd->bhqd", pr, v)

    o = lax.map(blk, qb)
    return o.transpose(1, 3, 0, 2, 4).reshape(B, Lc, H * dh)


def _na_geometry(rows):
    kh = min(WIN_H, rows)
    r = np.arange(rows)
    row_start = np.clip(r - kh // 2, 0, rows - kh)
    row_idx = row_start[:, None] + np.arange(kh)[None, :]
    c0 = np.arange(NQB) * QB_W
    band_start = np.clip(c0 - WIN_W // 2, 0, GRID_W - KB_W)
    col_idx = band_start[:, None] + np.arange(KB_W)[None, :]
    qc = c0[:, None] + np.arange(QB_W)[None, :]
    win_start = np.clip(qc - WIN_W // 2, 0, GRID_W - WIN_W)
    kc = col_idx[:, None, :]
    valid = (kc >= win_start[:, :, None]) & (kc < win_start[:, :, None] + WIN_W)
    dr_idx = row_idx - r[:, None] + (WIN_H - 1)
    dc_idx = np.clip(kc - qc[:, :, None] + (WIN_W - 1), 0, 2 * WIN_W - 2)
    return kh, row_idx, col_idx, valid, dr_idx, dc_idx


def _latent_attention(q, k, v, k_ctx, v_ctx, rpb):
    B, H, L, dh = q.shape
    rows = L // GRID_W
    kh, row_idx, col_idx, valid, dr_idx, dc_idx = _na_geometry(rows)
    nk = kh * KB_W
    scale = HEAD_DIM ** -0.5
    qb = q.reshape(B, H, rows, NQB, QB_W, dh)

    def band(t):
        t = t.reshape(B, H, rows, GRID_W, dh)
        t = jnp.take(t, row_idx, axis=2)
        t = jnp.take(t, col_idx, axis=4)
        return t.transpose(0, 1, 2, 4, 3, 5, 6).reshape(B, H, rows, NQB, nk, dh)

    kb, vb = band(k), band(v)
    bias = rpb[:, dr_idx[:, None, None, :, None], dc_idx[None, :, :, None, :]]
    bias = bias.reshape(H, rows, NQB, QB_W, nk).astype(jnp.float32)
    mask = np.broadcast_to(valid[:, :, None, :], (NQB, QB_W, kh, KB_W)).reshape(NQB, QB_W, nk)
    s_loc = jnp.einsum("bhrnqd,bhrnkd->bhrnqk", qb, kb).astype(jnp.float32) * scale + bias
    s_loc = jnp.where(mask, s_loc, NEG_INF)
    s_ctx = jnp.einsum("bhrnqd,bhkd->bhrnqk", qb, k_ctx).astype(jnp.float32) * scale
    prob = jax.nn.softmax(jnp.concatenate([s_loc, s_ctx], axis=-1), axis=-1).astype(v.dtype)
    o = (jnp.einsum("bhrnqk,bhrnkd->bhrnqd", prob[..., :nk], vb)
         + jnp.einsum("bhrnqk,bhkd->bhrnqd", prob[..., nk:], v_ctx))
    return o.transpose(0, 2, 3, 4, 1, 5).reshape(B, L, H * dh)


def _merge(pool_out, na_out, gp, gn, w_pool_proj, w_na_proj, w_o):
    a = pool_out @ w_pool_proj
    b = na_out @ w_na_proj
    return (jax.nn.sigmoid(gp) * a + jax.nn.sigmoid(gn) * b) @ w_o


def _conv_ffn(h, w_up, conv_w, conv_b, w_down):
    u = h @ w_up
    up = jnp.pad(u, ((0, 0), (1, 1), (0, 0)))
    u = up[:, :-2] * conv_w[0] + up[:, 1:-1] * conv_w[1] + up[:, 2:] * conv_w[2] + conv_b
    a, g = jnp.split(u, 2, axis=-1)
    return (jax.nn.silu(a) * g) @ w_down


def _layer(x, cond, attend, norm_mix_g, norm_ffn_g, w_mod, b_mod, w_in, q_norm_g, k_norm_g,
           pool_w, pool_scale, w_pool_proj, w_na_proj, w_o, w_up, ffn_conv_w, ffn_conv_b, w_down):
    sa, ca, ga, sf, cf, gf = _modulation(cond, w_mod, b_mod)
    h = _modulated_norm(x, norm_mix_g, sa, ca)
    p, q, k, v, gp, gn = _mixer_inputs(h, w_in, q_norm_g, k_norm_g)
    na = attend(q, k, v)
    pool = _pool_mixer(p, pool_w, pool_scale)
    x = x + ga * _merge(pool, na, gp, gn, w_pool_proj, w_na_proj, w_o)
    h = _modulated_norm(x, norm_ffn_g, sf, cf)
    x = x + gf * _conv_ffn(h, w_up, ffn_conv_w, ffn_conv_b, w_down)
    return x, k, v


def setup_inputs(seed: int = 0) -> dict:
    key = jax.random.key(seed)
    ks = jax.random.split(key, 24)

    def nrm(k, shape, scale):
        return jax.random.normal(k, shape, jnp.float32) * scale

    return {
        "x_prompt": nrm(ks[0], (BATCH, SEQ, D_MODEL), 1.0),
        "x_sample": nrm(ks[1], (DEC_BATCH, DEC_SEQ, D_MODEL), 1.0),
        "cache_k": nrm(ks[2], (DEC_BATCH, DEPTH, N_HEADS, PAST_LEN, HEAD_DIM), 1.0),
        "cache_v": nrm(ks[3], (DEC_BATCH, DEPTH, N_HEADS, PAST_LEN, HEAD_DIM), 1.0),
        "c": nrm(ks[4], (DEC_BATCH, D_MODEL), 1.0),
        "c_ctx": nrm(ks[5], (D_MODEL,), 1.0),
        "norm_mix_g": 1.0 + nrm(ks[6], (DEPTH, D_MODEL), 0.1),
        "norm_ffn_g": 1.0 + nrm(ks[7], (DEPTH, D_MODEL), 0.1),
        "w_mod": nrm(ks[8], (DEPTH, D_MODEL, 6 * D_MODEL), 0.5 * D_MODEL ** -0.5),
        "b_mod": nrm(ks[9], (DEPTH, 6 * D_MODEL), 0.02),
        "w_in": nrm(ks[10], (DEPTH, D_MODEL, D_IN), D_MODEL ** -0.5),
        "q_norm_g": 1.0 + nrm(ks[11], (DEPTH, HEAD_DIM), 0.1),
        "k_norm_g": 1.0 + nrm(ks[12], (DEPTH, HEAD_DIM), 0.1),
        "pool_w": nrm(ks[13], (DEPTH, N_POOL_GROUPS, POOL_GROUP, POOL_GROUP), POOL_GROUP ** -0.5),
        "pool_scale": 1.0 + nrm(ks[14], (DEPTH, D_POOL), 0.1),
        "na_rpb": nrm(ks[15], (DEPTH, N_HEADS, 2 * WIN_H - 1, 2 * WIN_W - 1), 0.1),
        "w_pool_proj": nrm(ks[16], (DEPTH, D_POOL, D_MODEL), D_POOL ** -0.5),
        "w_na_proj": nrm(ks[17], (DEPTH, D_NA, D_MODEL), D_NA ** -0.5),
        "w_o": nrm(ks[18], (DEPTH, D_MODEL, D_MODEL), D_MODEL ** -0.5),
        "w_up": nrm(ks[19], (DEPTH, D_MODEL, 2 * D_FF), D_MODEL ** -0.5),
        "ffn_conv_w": nrm(ks[20], (DEPTH, CONV_W, 2 * D_FF), CONV_W ** -0.5),
        "ffn_conv_b": nrm(ks[21], (DEPTH, 2 * D_FF), 0.02),
        "w_down": nrm(ks[22], (DEPTH, D_FF, D_MODEL), D_FF ** -0.5),
    }


def reference(x_prompt, x_sample, cache_k, cache_v, c, c_ctx, norm_mix_g, norm_ffn_g, w_mod, b_mod,
              w_in, q_norm_g, k_norm_g, pool_w, pool_scale, na_rpb, w_pool_proj, w_na_proj, w_o,
              w_up, ffn_conv_w, ffn_conv_b, w_down):
    x = x_prompt
    ks, vs = [], []
    for l in range(DEPTH):
        x, k, v = _layer(x, c_ctx[None, :], _context_attention,
                         norm_mix_g[l], norm_ffn_g[l], w_mod[l], b_mod[l], w_in[l], q_norm_g[l],
                         k_norm_g[l], pool_w[l], pool_scale[l], w_pool_proj[l], w_na_proj[l], w_o[l],
                         w_up[l], ffn_conv_w[l], ffn_conv_b[l], w_down[l])
        ks.append(k)
        vs.append(v)
    y_prompt = x
    new_k = jnp.stack(ks, axis=1)
    new_v = jnp.stack(vs, axis=1)

    x = x_sample
    for l in range(DEPTH):
        k_ctx, v_ctx, rpb = cache_k[:, l], cache_v[:, l], na_rpb[l]

        def attend(q, k, v, k_ctx=k_ctx, v_ctx=v_ctx, rpb=rpb):
            return _latent_attention(q, k, v, k_ctx, v_ctx, rpb)

        x, _, _ = _layer(x, c, attend,
                         norm_mix_g[l], norm_ffn_g[l], w_mod[l], b_mod[l], w_in[l], q_norm_g[l],
                         k_norm_g[l], pool_w[l], pool_scale[l], w_pool_proj[l], w_na_proj[l], w_o[l],
                         w_up[l], ffn_conv_w[l], ffn_conv_b[l], w_down[l])
    y_sample = x
    return (y_prompt, y_sample, new_k, new_v)
```

```python
import numpy as np
from contextlib import ExitStack
import concourse.bass as bass
import concourse.mybir as mybir
from concourse.bass_utils import run_bass_kernel_spmd

F32 = mybir.dt.float32
BF16 = mybir.dt.bfloat16
AF = mybir.ActivationFunctionType
ALU = mybir.AluOpType
AX = mybir.AxisListType

D = 1024
DFF = 2816
NEG = -30000.0
EPS = 1e-6
NRING = 5
NE = 26
POOL_W = (2, 4, 8, 16)
DEBUG_STOP = None
DEBUG_LVL = 99


class Prog:
    ENG = ("pe", "act", "dve", "pool", "sp")

    def __init__(self, nc, es):
        self.nc = nc
        self.es = es
        self.ops = {e: [] for e in self.ENG}
        self.cnt = {e: 0 for e in self.ENG}
        self.sem = {e: es.enter_context(nc.semaphore("s_" + e)) for e in self.ENG}
        self.res = {}
        self.seen = {e: {} for e in self.ENG}
        self.dsem = {}
        self.dcnt = {}

    def _deps(self, eng, r, w):
        deps = {}

        def add(d):
            if d is None:
                return
            s, v = d
            if s == "pe" and eng == "pe":
                return
            if deps.get(s, 0) < v:
                deps[s] = v

        for k in r:
            st = self.res.get(k)
            if st:
                add(st["w"])
        for k in w:
            st = self.res.get(k)
            if st:
                add(st["w"])
                for d in st["r"]:
                    add(d)
        waits = []
        for s, v in deps.items():
            if self.seen[eng].get(s, 0) >= v:
                continue
            self.seen[eng][s] = v
            waits.append((s, v))
        return waits

    def _mark(self, tok, r, w):
        for k in r:
            st = self.res.setdefault(k, {"w": None, "r": []})
            if len(st["r"]) > 64:
                best = {}
                for s, v in st["r"]:
                    if best.get(s, 0) < v:
                        best[s] = v
                st["r"] = list(best.items())
            st["r"].append(tok)
        for k in w:
            self.res[k] = {"w": tok, "r": []}

    def op(self, eng, fn, r=(), w=(), inc=True):
        waits = self._deps(eng, r, w)
        val = self.cnt[eng] + 1
        if inc:
            self.cnt[eng] = val
        self.ops[eng].append((waits, fn, ("e", eng) if inc else None))
        self._mark((eng, val), r, w)

    def dma(self, q, skey, fn, r=(), w=()):
        if skey not in self.dsem:
            self.dsem[skey] = self.es.enter_context(self.nc.semaphore("d_" + skey))
            self.dcnt[skey] = 0
        waits = self._deps(q, r, w)
        self.dcnt[skey] += 16
        self.ops[q].append((waits, fn, ("d", skey)))
        self._mark(("d:" + skey, self.dcnt[skey]), r, w)

    def alias(self, newkeys, oldkeys):
        acc = []
        for k in oldkeys:
            st = self.res.get(k)
            if st:
                if st["w"]:
                    acc.append(st["w"])
                acc.extend(st["r"])
        for k in newkeys:
            st = self.res.setdefault(k, {"w": None, "r": []})
            st["r"].extend(acc)

    def _semof(self, s):
        if s.startswith("d:"):
            return self.dsem[s[2:]]
        return self.sem[s]

    def finish_waits(self, eng):
        waits = [("d:" + k, c) for k, c in self.dcnt.items()]
        self.ops[eng].append((waits, None, None))

    def emit(self):
        nc = self.nc
        hw = {"pe": "tensor", "act": "scalar", "dve": "vector", "pool": "gpsimd", "sp": "sync"}
        with nc.Block() as block:
            for e in self.ENG:
                lst = self.ops[e]

                def body(engine, lst=lst):
                    for waits, fn, inc in lst:
                        for s, v in waits:
                            engine.wait_ge(self._semof(s), v)
                        if fn is None:
                            continue
                        ins = fn(engine)
                        if inc is not None:
                            if inc[0] == "e":
                                ins.then_inc(self.sem[inc[1]], 1)
                            else:
                                ins.then_inc(self.dsem[inc[1]], 16)

                getattr(block, hw[e])(body)


class Builder:
    def __init__(self):
        self.nc = bass.Bass("TRN2", target_bir_lowering=False)
        self.es = ExitStack()

    def mm(self, out, lhsT, rhs, start, stop, r, w, inc=None, skip=False):
        inc = stop if inc is None else inc
        self.P.op("pe", lambda e, o=out, l=lhsT, rr=rhs, s=start, t=stop, sk=skip:
                  e.matmul(o, lhsT=l, rhs=rr, start=s, stop=t, skip_group_check=sk), r=r, w=w, inc=inc)

    def tr(self, out, in_, ident, r, w):
        self.P.op("pe", lambda e, o=out, i=in_, d=ident: e.transpose(o, i, d), r=r, w=w)

    def act(self, out, in_, func, r, w, scale=None, bias=None, accum=None):
        kw = {}
        if scale is not None:
            kw["scale"] = scale
        if bias is not None:
            kw["bias"] = bias
        if accum is not None:
            kw["accum_out"] = accum
        self.P.op("act", lambda e, o=out, i=in_, f=func, kw=kw: e.activation(out=o, in_=i, func=f, **kw),
                  r=r, w=w)

    def tt(self, out, a, b, op, r, w, eng="dve"):
        self.P.op(eng, lambda e, o=out, a=a, b=b, op=op: e.tensor_tensor(out=o, in0=a, in1=b, op=op), r=r, w=w)

    def ts(self, out, a, s1, s2, op0, op1, r, w, eng="dve"):
        if s2 is None:
            self.P.op(eng, lambda e, o=out, a=a, s1=s1, op0=op0:
                      e.tensor_scalar(out=o, in0=a, scalar1=s1, scalar2=None, op0=op0), r=r, w=w)
        else:
            self.P.op(eng, lambda e, o=out, a=a, s1=s1, s2=s2, op0=op0, op1=op1:
                      e.tensor_scalar(out=o, in0=a, scalar1=s1, scalar2=s2, op0=op0, op1=op1), r=r, w=w)

    def stt(self, out, a, s, b, op0, op1, r, w):
        self.P.op("dve", lambda e, o=out, a=a, s=s, b=b, op0=op0, op1=op1:
                  e.scalar_tensor_tensor(out=o, in0=a, scalar=s, in1=b, op0=op0, op1=op1), r=r, w=w)

    def cp(self, out, in_, r, w, eng="dve"):
        self.P.op(eng, lambda e, o=out, i=in_: e.tensor_copy(out=o, in_=i), r=r, w=w)

    def recip(self, out, in_, r, w):
        self.P.op("dve", lambda e, o=out, i=in_: e.reciprocal(out=o, in_=i), r=r, w=w)

    def memset(self, ap, val, w, eng="dve"):
        self.P.op(eng, lambda e, a=ap, v=val: e.memset(a, v), w=w)

    def ld(self, skey, out, in_, w, r=(), q="sp"):
        if skey in ("c", "cp"):
            self._cn = getattr(self, "_cn", 0) + 1
            skey = "%s%d" % (skey, self._cn)
        self.P.dma(q, skey, lambda e, o=out, i=in_: e.dma_start(out=o, in_=i), r=r, w=w)

    def st(self, skey, out, in_, r, q="sp"):
        self.P.dma(q, skey, lambda e, o=out, i=in_: e.dma_start(out=o, in_=i), r=r, w=())

    def bank(self, pool):
        lst, idx = self.pools[pool]
        b = lst[idx % len(lst)]
        self.pools[pool][1] = idx + 1
        return b

    def unit(self, loads, slot=None):
        if slot is None:
            s = self.uidx % NRING
            self.uidx += 1
        else:
            s = slot
        R = self.R[s]
        for dfn, src in loads:
            self.ld("R%d" % s, dfn(R), src, w=[("R", s)], q="pool")
        return s

    def sb(self, name, shape, dt):
        return self.es.enter_context(self.nc.sbuf_tensor("sb_" + name, shape, dt))

    def build(self):
        nc, es = self.nc, self.es
        self.P = P = Prog(nc, es)
        din = lambda n, s: nc.dram_tensor(n, s, F32, kind="ExternalInput").ap()
        dout = lambda n, s: nc.dram_tensor(n, s, F32, kind="ExternalOutput").ap()
        self.xp = din("xp", [1024, D])
        self.xs = din("xs", [1152, D])
        self.ck = din("ck", [8, 512, 64])
        self.cv = din("cv", [8, 512, 64])
        self.w_mod = din("w_mod", [D, 6 * D])
        self.w_in = din("w_in", [D, 4096])
        self.pool_w = din("pool_w", [4, 128, 128])
        self.w_pp = din("w_pp", [512, D])
        self.w_np = din("w_np", [512, D])
        self.w_o = din("w_o", [D, D])
        self.w_up = din("w_up", [D, 2 * DFF])
        self.w_down = din("w_down", [DFF, D])
        self.vecF_d = din("vecF", [128, 260])
        self.gqk_d = din("gqk", [128, 128])
        self.bgagf_d = din("bgagf", [2, 2048])
        self.mscr = nc.dram_tensor("mscr", [1, 2048], F32, kind="Internal").ap()
        self.ident_d = din("ident", [128, 128])
        self.half_d = din("half", [128, 128])
        self.t3_d = din("t3", [8, 128, NE * 64])
        self.negm_d = din("negm", [128, 90])
        self.ptab_d = din("ptab", [128, 128])
        self.pmask_d = din("pmask", [128, 656])
        self.hval_d = din("hval", [128, 2])
        self.ones_d = din("ones1", [1, 128])
        self.yp = dout("yp", [1024, D])
        self.ys = dout("ys", [512, D])
        self.nk = dout("nk", [4, 8, 256, 64])
        self.nv = dout("nv", [4, 8, 256, 64])

        sb = self.sb
        self.R = [sb("ring%d" % i, [128, 8, 512], BF16) for i in range(NRING)]
        self.xres = sb("xres", [128, 6, D], F32)
        self.xh = sb("xh", [128, 2, D], BF16)
        self.kbq = sb("kbq", [128, 2, 512], BF16)
        self.hT = sb("hT", [128, 8, 1152], BF16)
        self.ATT = sb("ATT", [128, 13312], BF16)
        self.KT = self.ATT[:, 0:6656].rearrange("p (j c) -> p j c", j=4)
        self.Qz = self.ATT[:, 6656:11776].rearrange("p (h c) -> p h c", h=8)
        self.PTr = self.ATT[:, 11776:13312].rearrange("p (j c) -> p j c", j=4)
        self.PTr6 = self.ATT[:, 11776:13312].rearrange("p (j c) -> p j c", j=6)
        self.actT = self.ATT[:, 0:11264].rearrange("p (j c) -> p j c", j=22)
        self.Vaug = sb("Vaug", [128, 13, 8, 65], BF16)
        self.sq = sb("sq", [128, 3, 512], F32)
        self.kf = sb("kf", [128, 4, 512], F32)
        self.kb = sb("kb", [128, 2, 512], BF16)
        self.EB = sb("EB", [128, 2, NE * 64], BF16)
        self.na = sb("na", [128, 6, 512], BF16)
        self.naT = sb("naT", [128, 4, 640], BF16)
        self.PF = sb("PF", [128, 3936], F32)
        self.pT = self.PF[:, 0:2624].rearrange("p (g c) -> p g c", g=4)
        self.T1 = self.PF[:, 2624:3280]
        self.T2 = self.PF[:, 3280:3936]
        self.tab = self.PF[:, 0:2048].rearrange("p (s c) -> p s c", s=4)
        self.ybuf = self.PF[:, 0:3072].rearrange("p (s c) -> p s c", s=6)
        self.pmask = sb("pmask", [128, 656], BF16)
        self.pooledT = sb("pooledT", [128, 4, 640], BF16)
        self.pooloutT = self.pooledT
        self.mergedT = sb("mergedT", [128, 8, 640], BF16)
        self.GA = sb("GA", [128, D], F32)
        self.GF = sb("GF", [128, D], F32)
        self.identF = sb("identF", [128, 128], F32)
        self.identB = sb("identB", [128, 128], BF16)
        self.vecF = sb("vecF", [128, 260], F32)
        self.gqk = sb("gqk", [128, 128], F32)
        self.scb = sb("scb", [128, 8, 2], BF16)
        self.modF = sb("modF", [128, 2, 48], F32)
        self.G1F = sb("G1F", [128, 2, 8], F32)
        self.G2F = sb("G2F", [128, 2, 8], F32)
        self.poolw = sb("poolw", [128, 4, 128], BF16)
        self.half = sb("halfb", [128, 128], BF16)
        self.negm = sb("negmb", [128, 90], BF16)
        self.ptab = sb("ptab", [128, 128], F32)
        self.hval = sb("hval", [128, 2], F32)
        self.ones1 = sb("ones1s", [1, 128], F32)
        self.mrow = self.sq[0:1, 2, :]
        self.brow = self.kf[0:1, 3, :]
        self.small = sb("small", [128, 64], F32)
        self.tmpE = sb("tmpE", [128, 64], F32)
        self.ps = [es.enter_context(nc.psum_tensor("ps%d" % i, [128, 512], F32)) for i in range(8)]
        self.psb = [p.bitcast(BF16) for p in self.ps]
        self.pools = {"main": [[0, 1, 2, 3, 4, 5], 0], "tr": [[6, 7], 0], "S": [[0, 1, 2, 3], 0], "S6": [[0, 1, 2, 3, 6, 7], 0],
                      "O": [[4, 5], 0], "all8": [[0, 1, 2, 3, 4, 5, 6, 7], 0]}
        self.uidx = 0
        self.small_i = 0

        self.row_cond = [None, None]
        self.prologue()
        def pf_steps(kind, cond):
            G1n = self.G1F[:, cond, :]
            S1n = self.modF[:, cond, 0:8]

            def land(c):
                if kind == "p":
                    ls = 4 + c % 2
                    return self.xres[:, ls, :], [("x", ls)], "x%d" % ls, self.xp[512 + c * 128: 512 + (c + 1) * 128, :]
                sl = c % 2
                xl = self.kf[:, 2 * sl:2 * sl + 2, :].rearrange("p a c -> p (a c)")
                return xl, [("kf", 2 * sl), ("kf", 2 * sl + 1)], "kf%d" % (2 * sl), self.xs[c * 128:(c + 1) * 128, :]

            def load(c):
                ap, keys, sem, src = land(c)
                self.ld(sem, ap, src, w=keys)

            def n1(c):
                ap, keys, sem, src = land(c)
                self.norm_A(ap, keys, 128, c % 2, False)

            def n2(c):
                self.norm_B(128, c % 2, G1n, S1n, c * 128, [("hT", c)])

            return [lambda: (load(0), load(1), n1(0)),
                    lambda: (n1(1), n2(0), load(2)),
                    lambda: (n1(2), n2(1), load(3)),
                    lambda: (n1(3), n2(2)),
                    lambda: n2(3)]

        def pre_units():
            win = self.w_in.rearrange("(kc p) c -> p kc c", p=128)
            return (self.unit([(lambda R: R[:, :, :], win[:, :, 1024:1536])]),
                    self.unit([(lambda R: R[:, :, :], win[:, :, 1536:2048])]),
                    self.unit([(lambda R: R[:, :, :], win[:, :, 512:1024])]))

        G0 = dict(kind="p", g=0, cond=0)
        G1 = dict(kind="p", g=1, cond=0, prefetched=True)
        G2 = dict(kind="s", cond=1, prefetched=True)
        G0["prefetch_steps"] = pf_steps("p", 0)
        G1["prefetch_steps"] = pf_steps("s", 1)
        G0["post_f2"] = lambda: G1.__setitem__("pre_units", pre_units())
        G1["post_f2"] = lambda: G2.__setitem__("pre_units", pre_units())
        self.group(G0)
        self.group(G1)
        self.group(G2)
        P.finish_waits("sp")
        P.emit()
        return nc

    def dbg_stop(self, tag):
        if DEBUG_STOP != tag:
            return False
        dbg = self.nc.dram_tensor("dbg", [4, 128, D], F32, kind="ExternalOutput").ap()
        for i in range(4):
            self.st("dbg", dbg[i], self.xres[:, i, :], r=[("x", i)])
        return True

    def stat(self, n=1):
        i = self.small_i
        if i + n > 64:
            i = 0
        self.small_i = i + n
        return i

    def prologue(self):
        P = self.P
        ld = self.ld
        ld("c", self.vecF[:], self.vecF_d, w=["vecF"])
        ld("c", self.identF[:], self.ident_d, w=["identF"])
        ld("c", self.gqk[:], self.gqk_d, w=["gqk"])
        ld("c", self.ptab[:], self.ptab_d, w=["ptab"])
        ld("cp", self.pmask[:], self.pmask_d, w=["pmask"], q="pool")
        ld("c", self.hval[:], self.hval_d, w=["hval"])
        ld("c", self.ones1[:], self.ones_d, w=["ones1"])
        ld("cp", self.half[:], self.half_d, w=["half"], q="pool")
        ld("cp", self.negm[:], self.negm_d, w=["negm"], q="pool")
        ld("cp", self.poolw[:], self.pool_w.rearrange("g c d -> c g d"), w=["poolw"], q="pool")
        self.cp(self.identB[:], self.identF[:], r=["identF"], w=["identB"])
        self.memset(self.PF[:], 0.0, w=["pT", "T1", "T2", "tab", "ybuf"])
        self.memset(self.Vaug[:].rearrange("p a b c -> p (a b c)"), 1.0, w=[("V", c) for c in range(13)])
        self.memset(self.ATT[:, 11776:13312], 1.0, w=[("PTr", c) for c in range(6)])
        self.memset(self.mergedT[:].rearrange("p a b -> p (a b)"), 0.0, w=[("mT", c) for c in range(8)])
        self.memset(self.tmpE[:, 63:64], EPS, w=["epsT"])
        condF = self.vecF[:, 244:260].rearrange("p (k c) -> p k c", c=2)
        self.act(self.scb[:], condF, AF.Silu, r=["vecF"], w=["scb"])
        self.mod_late_done = False
        self.mod_feat((0, 1, 2, 3))
        self.mod_G(1)
        cvv = self.cv.rearrange("h k d -> k h d")
        ckk = self.ck.rearrange("h k d -> k h d")
        for cc in range(4):
            self.ld("cp", self.Vaug[:, 9 + cc, :, 0:64], cvv[cc * 128:(cc + 1) * 128], w=[("V", 9 + cc)], q="pool")

    def mod_feat(self, units):
        wm = self.w_mod.rearrange("(kc p) c -> p kc c", p=128)
        for u in units:
            s = self.unit([(lambda R: R[:, :, :], wm[:, :, u * 512:(u + 1) * 512])])
            b = self.bank("main")
            for cc in range(4):
                for kc in range(8):
                    self.mm(self.ps[b][:, cc * 2:cc * 2 + 2], self.R[s][:, kc, cc * 128:(cc + 1) * 128],
                            self.scb[:, kc, :], kc == 0, kc == 7, r=[("R", s), "scb"], w=[("ps", b)])
            for cond in range(2):
                pv = self.ps[b][:, 0:8].rearrange("p (c k) -> p c k", k=2)[:, :, cond]
                self.tt(self.modF[:, cond, 4 * u:4 * u + 4], pv, self.vecF[:, 16 + 4 * u:16 + 4 * u + 4], ALU.add,
                        r=[("ps", b), "vecF"], w=["modF"])

    def mod_G(self, which):
        for cond in range(2):
            if which == 1:
                self.ts(self.G1F[:, cond, :], self.modF[:, cond, 8:16], 1.0, None, ALU.add, None, r=["modF"], w=["G1F"])
                self.tt(self.G1F[:, cond, :], self.G1F[:, cond, :], self.vecF[:, 0:8], ALU.mult, r=["G1F", "vecF"], w=["G1F"])
            else:
                self.ts(self.G2F[:, cond, :], self.modF[:, cond, 32:40], 1.0, None, ALU.add, None, r=["modF"], w=["G2F"])
                self.tt(self.G2F[:, cond, :], self.G2F[:, cond, :], self.vecF[:, 8:16], ALU.mult, r=["G2F", "vecF"], w=["G2F"])

    def k_transposes(self, src, m, col0, srckey, kchunk, dst="KT", dst_ap=None, wkeys=None):
        b = self.bank("tr")
        pb = self.psb[b][:, 0:512].rearrange("p (j c) -> p j c", j=4)
        for j in range(4):
            self.tr(pb[:, j, 0:m], src[0:m, j * 128:(j + 1) * 128], self.identB[0:m, 0:m],
                    r=[srckey, "identB"], w=[("ps", b)])
        tgt = self.KT if dst == "KT" else (self.QT if dst == "QT" else self.naT)
        if dst_ap is None:
            dst_ap = tgt[:, :, col0:col0 + m]
        self.act(dst_ap, pb[:, :, 0:m], AF.Identity, r=[("ps", b)], w=wkeys or [(dst, kchunk)])

    def mod_rows(self, cond, only):
        wm = self.w_mod.rearrange("(kc p) c -> p kc c", p=128)
        for which, u0, dst, key in ((0, 4, self.GA, "GA"), (1, 10, self.GF, "GF")):
            if which != only:
                continue
            if self.row_cond[which] == cond:
                continue
            first = self.row_cond[which] is None
            self.row_cond[which] = cond
            mrow2 = self.sq[0:2, 2, :]
            brow2 = self.kf[0:2, 3, :]
            for uu in range(2):
                o_ = which * 1024 + uu * 512
                if first:
                    u = u0 + uu
                    s = self.unit([(lambda R: R[:, :, :], wm[:, :, u * 512:(u + 1) * 512])])
                    b = self.bank("main")
                    for kc in range(8):
                        self.mm(self.ps[b][0:2, :], self.scb[:, kc, 0:2], self.R[s][:, kc, :],
                                kc == 0, kc == 7, r=[("R", s), "scb"], w=[("ps", b)])
                    self.ld("kf3", brow2, self.bgagf_d[:, o_:o_ + 512], w=[("kf", 3)])
                    self.tt(mrow2, self.ps[b][0:2, :], brow2, ALU.add, r=[("ps", b), ("kf", 3)], w=[("sq", 2)])
                    oc_ = 1 - cond
                    self.P.dma("sp", "mscr", lambda e, o=self.mscr[0:1, o_:o_ + 512], i=self.sq[oc_:oc_ + 1, 2, :]: e.dma_start(out=o, in_=i),
                               r=[("sq", 2)], w=[("mscr", which, uu)])
                    src_row = self.sq[cond:cond + 1, 2, :]
                    if cond != 0:
                        raise NotImplementedError
                else:
                    self.P.dma("sp", "mscr", lambda e, o=self.sq[0:1, 2, :], i=self.mscr[0:1, o_:o_ + 512]: e.dma_start(out=o, in_=i),
                               r=[("mscr", which, uu)], w=[("sq", 2)])
                    src_row = self.sq[0:1, 2, :]
                b2 = self.bank("main")
                self.mm(self.ps[b2][:, :], self.ones1[:, :], src_row, True, True, r=["ones1", ("sq", 2)], w=[("ps", b2)])
                self.cp(dst[:, uu * 512:(uu + 1) * 512], self.ps[b2][:, :], r=[("ps", b2)], w=[key])

    def rstd_from_ssq(self, ssq_ap, out_ap, n, m, width, keys):
        self.act(out_ap, ssq_ap, AF.Sqrt, r=keys + ["epsT"], w=keys, scale=1.0 / n, bias=self.tmpE[0:m, 63:64])
        self.recip(out_ap, out_ap, r=keys, w=keys)

    def pipeline(self, n, stages, lag=1):
        ns = len(stages)
        for t in range(n + (ns - 1) * lag):
            for si, f in enumerate(stages):
                c = t - si * lag
                if 0 <= c < n:
                    f(c)

    def schedule(self, items):
        T = max(st + len(fs) for st, fs in items)
        for t in range(T):
            for st, fs in items:
                k = t - st
                if 0 <= k < len(fs):
                    fs[k]()

    def norm_A(self, x_ap, xkey, m, slot, inplace):
        xkeys = xkey if isinstance(xkey, list) else [xkey]
        si = self.stat(1)
        st = self.small[0:m, si:si + 1]
        skey = ("small", si)
        xh = self.xh[0:m, slot, :]
        hkey = ("xh", slot)
        self.act(xh, x_ap, AF.Square, r=xkeys, w=[skey, hkey], accum=st)
        self.rstd_from_ssq(st, st, float(D), m, 1, [skey])
        if inplace:
            self.ts(xh, xh, st, None, ALU.mult, None, r=[skey, hkey], w=[hkey])
        else:
            self.ts(xh, x_ap, st, None, ALU.mult, None, r=[skey] + xkeys, w=[hkey])

    def norm_B(self, m, slot, GF_, SF_, dcol, dkeys):
        xh = self.xh[0:m, slot, :]
        hkey = ("xh", slot)
        for f4 in range(2):
            b = self.bank("tr")
            pb = self.psb[b][:, 0:512].rearrange("p (j c) -> p j c", j=4)
            for q in range(4):
                fc = 4 * f4 + q
                self.tr(pb[:, q, 0:m], xh[:, fc * 128:(fc + 1) * 128], self.identB[0:m, 0:m],
                        r=[hkey, "identB"], w=[("ps", b)])
            for q in range(4):
                fc = 4 * f4 + q
                if fc % 2 == 0:
                    self.act(self.hT[:, fc, dcol:dcol + m], pb[:, q, 0:m], AF.Identity, r=[("ps", b), "G1F", "G2F", "modF"],
                             w=dkeys, scale=GF_[:, fc:fc + 1], bias=SF_[:, fc:fc + 1])
                else:
                    self.ts(self.hT[:, fc, dcol:dcol + m], pb[:, q, 0:m], GF_[:, fc:fc + 1], SF_[:, fc:fc + 1],
                            ALU.mult, ALU.add, r=[("ps", b), "G1F", "G2F", "modF"], w=dkeys)

    def norm_to_T(self, x_ap, xkey, m, slot, GF_, SF_, dstT, dcol, dkey, inplace):
        si = self.stat(1)
        st = self.small[0:m, si:si + 1]
        skey = ("small", si)
        self.act(self.junk[0:m, :], x_ap, AF.Square, r=[xkey], w=["junk", skey], accum=st)
        self.rstd_from_ssq(st, st, float(D), m, 1, [skey])
        xh = self.xh[0:m, slot, :]
        hkey = ("xh", slot)
        if inplace:
            self.ts(xh, xh, st, None, ALU.mult, None, r=[skey, hkey], w=[hkey])
        else:
            self.ts(xh, x_ap, st, None, ALU.mult, None, r=[skey, xkey], w=[hkey])
        for fc in range(8):
            b = self.bank("tr")
            self.tr(self.ps[b][:, 0:m], xh[:, fc * 128:(fc + 1) * 128], self.identF[0:m, 0:m],
                    r=[hkey, "identF"], w=[("ps", b)])
            if fc % 2 == 0:
                self.act(dstT[:, fc, dcol:dcol + m], self.ps[b][:, 0:m], AF.Identity, r=[("ps", b), "G1F", "G2F", "modF"],
                         w=[dkey], scale=GF_[:, fc:fc + 1], bias=SF_[:, fc:fc + 1])
            else:
                self.ts(dstT[:, fc, dcol:dcol + m], self.ps[b][:, 0:m], GF_[:, fc:fc + 1], SF_[:, fc:fc + 1],
                        ALU.mult, ALU.add, r=[("ps", b), "G1F", "G2F", "modF"], w=[dkey])

    def qk_A(self, s, m, hcol, hkeys, slot3):
        b = self.bank("main")
        for kc in range(8):
            self.mm(self.ps[b][0:m, :], self.hT[:, kc, hcol:hcol + m], self.R[s][:, kc, :], kc == 0, kc == 7,
                    r=[("R", s)] + hkeys, w=[("ps", b)])
        pk = self.ps[b][0:m, :]
        si = self.stat(8)
        st = self.small[0:m, si:si + 8]
        skey = ("small", si)
        sq = self.sq[0:m, slot3, :]
        self.act(sq, pk, AF.Square, r=[("ps", b)], w=[("sq", slot3)])
        self.P.op("dve", lambda e, o=st, i=sq.rearrange("p (h d) -> p h d", h=8): e.tensor_reduce(out=o, in_=i, axis=AX.X, op=ALU.add),
                  r=[("sq", slot3)], w=[skey])
        self.rstd_from_ssq(st, st, 64.0, m, 8, [skey])
        return dict(b=b, si=si, slot3=slot3, m=m)

    def qk_B(self, cx, gcol, kslot, out_dram=None, fslot=0, kbuf=None, kname="kb"):
        b, si, slot3, m = cx["b"], cx["si"], cx["slot3"], cx["m"]
        kbuf = self.kb if kbuf is None else kbuf
        pk = self.ps[b][0:m, :]
        st = self.small[0:m, si:si + 8]
        skey = ("small", si)
        pk3 = pk.rearrange("p (h d) -> p h d", h=8)
        sq3 = self.sq[0:m, slot3, :].rearrange("p (h d) -> p h d", h=8)
        self.tt(sq3, pk3, st.unsqueeze(2).broadcast_to([m, 8, 64]), ALU.mult, r=[("ps", b), skey], w=[("sq", slot3)])
        gb = self.gqk[0:m, gcol:gcol + 64].unsqueeze(1).broadcast_to([m, 8, 64])
        kb = kbuf[0:m, kslot, :]
        if out_dram is not None:
            kf = self.kf[0:m, fslot, :]
            self.tt(kf.rearrange("p (h d) -> p h d", h=8), sq3, gb, ALU.mult, r=[("sq", slot3), "gqk"], w=[("kf", fslot)])
            self.st("kf%d" % fslot, out_dram, kf.rearrange("p (h d) -> p h d", h=8), r=[("kf", fslot)])
            self.act(kb, kf, AF.Identity, r=[("kf", fslot)], w=[(kname, kslot)])
        else:
            self.tt(kb.rearrange("p (h d) -> p h d", h=8), sq3, gb, ALU.mult, r=[("sq", slot3), "gqk"], w=[(kname, kslot)])
        return kb

    def qk_block(self, s, m, hcol, hkeys, gcol, slot, out_dram=None):
        b = self.bank("main")
        for kc in range(8):
            self.mm(self.ps[b][0:m, :], self.hT[:, kc, hcol:hcol + m], self.R[s][:, kc, :], kc == 0, kc == 7,
                    r=[("R", s)] + hkeys, w=[("ps", b)])
        pk = self.ps[b][0:m, :]
        sq = self.sq[0:m, slot, :]
        self.act(sq, pk, AF.Square, r=[("ps", b)], w=[("sq", slot)])
        if DEBUG_LVL <= 0:
            return None
        si = self.stat(8)
        st = self.small[0:m, si:si + 8]
        skey = ("small", si)
        self.P.op("dve", lambda e, o=st, i=sq.rearrange("p (h d) -> p h d", h=8): e.tensor_reduce(out=o, in_=i, axis=AX.X, op=ALU.add),
                  r=[("sq", slot)], w=[skey])
        if DEBUG_LVL <= 1:
            return None
        self.rstd_from_ssq(st, st, 64.0, m, 8, [skey])
        if DEBUG_LVL <= 2:
            return None
        pk3 = pk.rearrange("p (h d) -> p h d", h=8)
        sq3 = sq.rearrange("p (h d) -> p h d", h=8)
        self.tt(sq3, pk3, st.unsqueeze(2).broadcast_to([m, 8, 64]), ALU.mult, r=[("ps", b), skey], w=[("sq", slot)])
        if DEBUG_LVL <= 3:
            return None
        gb = self.gqk[0:m, gcol:gcol + 64].unsqueeze(1).broadcast_to([m, 8, 64])
        kb = self.kb[0:m, slot, :]
        if out_dram is not None:
            kf = self.kf[0:m, slot, :]
            self.tt(kf.rearrange("p (h d) -> p h d", h=8), sq3, gb, ALU.mult, r=[("sq", slot), "gqk"], w=[("kf", slot)])
            if DEBUG_LVL <= 4:
                return None
            self.st("kf%d" % slot, out_dram, kf.rearrange("p (h d) -> p h d", h=8), r=[("kf", slot)])
            if DEBUG_LVL <= 5:
                return None
            self.act(kb, kf, AF.Identity, r=[("kf", slot)], w=[("kb", slot)])
        else:
            self.tt(kb.rearrange("p (h d) -> p h d", h=8), sq3, gb, ALU.mult, r=[("sq", slot), "gqk"], w=[("kb", slot)])
        return kb

    def group(self, G):
        P = self.P
        kind, cond = G["kind"], G["cond"]
        win = self.w_in.rearrange("(kc p) c -> p kc c", p=128)
        if kind == "p":
            g = G["g"]
            nkv = 4
            XOFF = 0
            NX = 512
            xch = [(i * 128, 128) for i in range(4)]
            own = [0, 1, 2, 3]
            ntiles = [dict(x0=0, n=256, ch=[0, 1], kv=[0, 1], ctx=[], local=False),
                      dict(x0=256, n=256, ch=[2, 3], kv=[2, 3], ctx=[], local=False)]
            mtiles = [(0, 512)]
        else:
            nkv = 9
            XOFF = 256
            NX = 640
            xch = [(0, 64), (64, 128), (192, 128), (320, 128), (448, 128), (576, 64)]
            own = [1, 2, 3, 4]
            ntiles = [dict(x0=0, n=320, ch=[0, 1, 2], kv=list(range(0, 7)), ctx=[9, 10, 11, 12], local=True, r0=0, lo=63, hi=320),
                      dict(x0=320, n=320, ch=[3, 4, 5], kv=list(range(2, 9)), ctx=[9, 10, 11, 12], local=True, r0=5, lo=0, hi=257)]
            mtiles = [(63, 257), (320, 257)]
        G1 = self.G1F[:, cond, :]
        S1 = self.modF[:, cond, 0:8]
        G2 = self.G2F[:, cond, :]
        S2 = self.modF[:, cond, 24:32]
        hkey = lambda c0, n: [("hT", c) for c in range(c0 // 128, (c0 + n - 1) // 128 + 1)]

        if kind == "s":
            for cc in range(4):
                self.k_transposes(self.na[:, cc, :], 128, 1152 + cc * 128, ("na", cc), 9 + cc)
        if G.get("pre_units") is not None:
            s_k, s_v, s_q = G["pre_units"]
        else:
            s_k = self.unit([(lambda R: R[:, :, :], win[:, :, 1024:1536])])
            s_v = self.unit([(lambda R: R[:, :, :], win[:, :, 1536:2048])])
            s_q = self.unit([(lambda R: R[:, :, :], win[:, :, 512:1024])])
        s_p = self.unit([(lambda R: R[:, :, :], win[:, :, 0:512])])
        kctx = {}

        NPF = 4 if G.get("prefetched", False) else 0

        def N1(c):
            slot = c % 2
            if kind == "p":
                self.ld("x%d" % c, self.xres[:, c, :], self.xp[g * 512 + c * 128: g * 512 + (c + 1) * 128, :], w=[("x", c)])
                if c >= NPF:
                    self.norm_A(self.xres[:, c, :], ("x", c), 128, slot, False)
            else:
                xl = self.kf[:, 2 * slot:2 * slot + 2, :].rearrange("p a c -> p (a c)")
                if c >= NPF:
                    self.ld("kf%d" % (2 * slot), xl, self.xs[c * 128:(c + 1) * 128, :], w=[("kf", 2 * slot), ("kf", 2 * slot + 1)])
                    self.norm_A(xl, [("kf", 2 * slot), ("kf", 2 * slot + 1)], 128, slot, False)

        def N2(c):
            self.norm_B(128, c % 2, G1, S1, c * 128, [("hT", c)])

        def K1(c):
            kctx[c] = self.qk_A(s_k, 128, c * 128, [("hT", c)], c % 3)

        def V1(c):
            b = self.bank("main")
            for kc in range(8):
                self.mm(self.ps[b][:, :], self.hT[:, kc, c * 128:(c + 1) * 128], self.R[s_v][:, kc, :], kc == 0, kc == 7,
                        r=[("R", s_v), ("hT", c)], w=[("ps", b)])
            pv3 = self.ps[b][:, :].rearrange("p (h d) -> p h d", h=8)
            self.cp(self.Vaug[:, c, :, 0:64], pv3, r=[("ps", b)], w=[("V", c)])
            if kind == "p":
                fs = 2 + c % 2
                bl = 2 * g + c // 2
                od = self.nv[bl].rearrange("h s d -> s h d")[(c % 2) * 128:(c % 2) * 128 + 128]
                self.cp(self.kf[:, fs, :], self.ps[b][:, :], r=[("ps", b)], w=[("kf", fs)])
                self.st("kf%d" % fs, od, self.kf[:, fs, :].rearrange("p (h d) -> p h d", h=8), r=[("kf", fs)])

        def K2(c):
            od = None
            if kind == "p":
                bl = 2 * g + c // 2
                od = self.nk[bl].rearrange("h s d -> s h d")[(c % 2) * 128:(c % 2) * 128 + 128]
            self.qk_B(kctx[c], 64, c % 2, od, c % 2)

        def K3(c):
            self.k_transposes(self.kb[:, c % 2, :], 128, c * 128, ("kb", c % 2), c)

        qctx = {}

        def Q1(i):
            xc, m = xch[i]
            qctx[i] = self.qk_A(s_q, m, XOFF + xc, hkey(XOFF + xc, m), i % 3)

        def Q2(i):
            self.qk_B(qctx[i], 0, i % 2, None, kbuf=self.kbq, kname="kbq")

        def Q3(i):
            xc, m = xch[i]
            bq = self.bank("tr")
            pb = self.psb[bq][:, 0:512].rearrange("p (j c) -> p j c", j=4)
            for j in range(4):
                self.tr(pb[:, j, 0:m], self.kbq[0:m, i % 2, j * 128:(j + 1) * 128], self.identB[0:m, 0:m],
                        r=[("kbq", i % 2), "identB"], w=[("ps", bq)])
            if kind == "p":
                bb, blk = i // 2, i % 2
                csl = slice(bb * 256 + blk, bb * 256 + 256, 2)
                wk = [("QT", 2 * bb), ("QT", 2 * bb + 1)]
            else:
                csl = slice(xc, xc + m)
                wk = [("QT", i)]
            self.cp(self.Qz[0:64, 0:8:2, csl], pb[0:64, :, 0:m], r=[("ps", bq)], w=wk)
            self.act(self.Qz[64:128, 1:8:2, csl], pb[64:128, :, 0:m], AF.Identity, r=[("ps", bq)], w=wk)

        items = []
        for c in range(nkv):
            if c < NPF:
                items.append((c, [lambda c=c: (N1(c), K1(c), V1(c)), lambda c=c: K2(c), lambda c=c: K3(c)]))
            else:
                items.append((c - 2 if NPF else c,
                              [lambda c=c: N1(c), lambda c=c: N2(c), lambda c=c: (K1(c), V1(c)), lambda c=c: K2(c), lambda c=c: K3(c)]))
        for i, (xc, m) in enumerate(xch):
            cl = (XOFF + xc + m - 1) // 128
            items.append((cl + (0 if NPF else 2), [lambda i=i: Q1(i), lambda i=i: Q2(i), lambda i=i: Q3(i)]))
        def zero_q():
            wk_ = [("QT", i) for i in range(6)]
            self.memset(self.Qz[64:128, 0:8:2, 0:NX], 0.0, w=wk_)
            self.memset(self.Qz[0:64, 1:8:2, 0:NX], 0.0, w=wk_)
        items.append((1 if kind == "s" else 0, [zero_q]))
        items.sort(key=lambda it: it[0])
        self.schedule(items)
        if kind == "s":
            for i, (xc, m) in enumerate(xch):
                self.ld("x%d" % i, self.xres[0:m, i, :], self.xs[XOFF + xc: XOFF + xc + m, :], w=[("x", i)])
        if self.dbg_stop("S3"):
            return
        chains = self.pool_stage(G, s_p, XOFF, NX, hkey)
        self.attention(G, xch, ntiles, chains)
        for i, (xc, m) in enumerate(xch):
            self.k_transposes(self.na[:, i, :], m, xc, ("na", i), i, dst="naT")
        if self.dbg_stop("S4"):
            return
        self.pool_mix(G)
        if self.dbg_stop("S5"):
            return
        wpp = self.w_pp.rearrange("(g p) c -> p g c", p=128)
        wnp = self.w_np.rearrange("(g p) c -> p g c", p=128)
        for j in range(2):
            s_pn = self.unit([(lambda R: R[:, 0:4, :], wpp[:, :, j * 512:(j + 1) * 512]),
                              (lambda R: R[:, 4:8, :], wnp[:, :, j * 512:(j + 1) * 512])])
            s_gp = self.unit([(lambda R: R[:, :, :], win[:, :, 2048 + j * 512: 2048 + (j + 1) * 512])])
            s_gn = self.unit([(lambda R: R[:, :, :], win[:, :, 3072 + j * 512: 3072 + (j + 1) * 512])])
            for oc4 in range(4):
                oc = 4 * j + oc4
                osl = slice(oc4 * 128, (oc4 + 1) * 128)
                for (x0, n) in mtiles:
                    bA, bB, bC, bD = [self.bank("all8") for _ in range(4)]
                    ti_ = mtiles.index((x0, n))
                    for gg in range(4):
                        self.mm(self.ps[bA][:, 0:n], self.R[s_pn][:, gg, osl], self.pooloutT[:, gg, x0:x0 + n], gg == 0, gg == 3,
                                r=[("R", s_pn), ("pl", gg, ti_)], w=[("ps", bA)])
                    for kc in range(8):
                        self.mm(self.ps[bB][:, 0:n], self.R[s_gp][:, kc, osl], self.hT[:, kc, XOFF + x0:XOFF + x0 + n], kc == 0, kc == 7,
                                r=[("R", s_gp)] + hkey(XOFF + x0, n), w=[("ps", bB)])
                    for gg in range(4):
                        self.mm(self.ps[bC][:, 0:n], self.R[s_pn][:, 4 + gg, osl], self.naT[:, gg, x0:x0 + n], gg == 0, gg == 3,
                                r=[("R", s_pn)] + [("naT", i) for i in range(len(xch))], w=[("ps", bC)])
                    for kc in range(8):
                        self.mm(self.ps[bD][:, 0:n], self.R[s_gn][:, kc, osl], self.hT[:, kc, XOFF + x0:XOFF + x0 + n], kc == 0, kc == 7,
                                r=[("R", s_gn)] + hkey(XOFF + x0, n), w=[("ps", bD)])
                    t1 = self.sq[:, 0, 0:n]
                    t2 = self.sq[:, 1, 0:n]
                    self.act(t1, self.ps[bB][:, 0:n], AF.Sigmoid, r=[("ps", bB)], w=[("sq", 0)])
                    self.act(t2, self.ps[bD][:, 0:n], AF.Sigmoid, r=[("ps", bD)], w=[("sq", 1)])
                    self.tt(t1, self.ps[bA][:, 0:n], t1, ALU.mult, r=[("ps", bA), ("sq", 0)], w=[("sq", 0)])
                    self.tt(t2, self.ps[bC][:, 0:n], t2, ALU.mult, r=[("ps", bC), ("sq", 1)], w=[("sq", 1)])
                    self.tt(self.mergedT[:, oc, x0:x0 + n], t1, t2, ALU.add, r=[("sq", 0), ("sq", 1)], w=[("mT", oc)])
        if not self.mod_late_done:
            self.mod_late_done = True
            self.mod_feat((6, 7, 8, 9))
            self.mod_G(2)
        self.mod_rows(cond, 0)
        wo = self.w_o.rearrange("(kc p) c -> p kc c", p=128)
        s_o = [self.unit([(lambda R: R[:, :, :], wo[:, :, j * 512:(j + 1) * 512])]) for j in range(2)]
        h2T = self.hT
        tcnt = [0]

        def O1(i):
            xc, m = xch[i]
            for j in range(2):
                b = self.bank("main")
                for kc in range(8):
                    self.mm(self.ps[b][0:m, :], self.mergedT[:, kc, xc:xc + m], self.R[s_o[j]][:, kc, :], kc == 0, kc == 7,
                            r=[("R", s_o[j]), ("mT", kc)], w=[("ps", b)])
                sl = tcnt[0] % 3
                tcnt[0] += 1
                tmp = self.sq[0:m, sl, :]
                self.tt(tmp, self.ps[b][0:m, :], self.GA[0:m, j * 512:(j + 1) * 512], ALU.mult, r=[("ps", b), "GA"], w=[("sq", sl)])
                xr = self.xres[0:m, i, j * 512:(j + 1) * 512]
                self.tt(xr, xr, tmp, ALU.add, r=[("x", i), ("sq", sl)], w=[("x", i)])

        def N1b(i):
            xc, m = xch[i]
            self.norm_A(self.xres[0:m, i, :], ("x", i), m, i % 2, False)

        def N2b(i):
            xc, m = xch[i]
            self.norm_B(m, i % 2, G2, S2, xc, [("hT", c) for c in range(xc // 128, (xc + m - 1) // 128 + 1)])

        self.pipeline(len(xch), [O1, N1b, N2b])
        if kind == "s":
            for col, hv in ((63, 0), (576, 1)):
                ap = self.hT[:, :, col:col + 1]
                self.ts(ap, ap, self.hval[:, hv:hv + 1], None, ALU.mult, None, r=[("hT", col // 128), "hval"], w=[("hT", col // 128)])
        if kind == "p" and G["g"] == 1:
            ckk = self.ck.rearrange("h k d -> k h d")
            for cc in range(4):
                kc_t = self.na[:, cc, :].rearrange("p (h d) -> p h d", h=8)
                self.ld("cp", kc_t, ckk[cc * 128:(cc + 1) * 128], w=[("na", cc)], q="pool")
        actkeys = [("actT", i) for i in range(22)]
        P.alias([("ta", 0), ("ta", 1), ("tg", 0), ("tg", 1)], ["pT", "T1", "T2"])
        P.alias(actkeys, [("KT", c) for c in range(13)] + [("QT", c) for c in range(6)] + [("PTr", c) for c in range(6)])
        if kind == "p":
            ftiles = [dict(c0=0, n=512, segs=[(0, 256), (256, 256)], o0=0, no=512, oc0=0)]
        else:
            ftiles = [dict(c0=63, n=258, segs=[(0, 258)], o0=1, no=256, oc0=0),
                      dict(c0=319, n=258, segs=[(0, 258)], o0=1, no=256, oc0=256)]
        wup = self.w_up.rearrange("(kc p) c -> p kc c", p=128)
        cv = self.vecF[:, 68:244].rearrange("p (c k) -> p c k", k=4)
        tslot = 0
        for i2 in range(11):
            s = self.unit([(lambda R: R[:, :, 0:256], wup[:, :, i2 * 256:(i2 + 1) * 256]),
                           (lambda R: R[:, :, 256:512], wup[:, :, DFF + i2 * 256: DFF + (i2 + 1) * 256])])
            for ii in range(2):
                hi = 2 * i2 + ii
                for ft in ftiles:
                    c0, n = ft["c0"], ft["n"]
                    hk = hkey(c0, n)
                    bA = self.bank("main")
                    bG = self.bank("main")
                    for kc in range(8):
                        self.mm(self.ps[bA][:, 0:n], self.R[s][:, kc, ii * 128:(ii + 1) * 128], h2T[:, kc, c0:c0 + n], kc == 0, kc == 7,
                                r=[("R", s)] + hk, w=[("ps", bA)])
                    for kc in range(8):
                        self.mm(self.ps[bG][:, 0:n], self.R[s][:, kc, 256 + ii * 128: 256 + (ii + 1) * 128], h2T[:, kc, c0:c0 + n], kc == 0, kc == 7,
                                r=[("R", s)] + hk, w=[("ps", bG)])
                    ts_ = tslot % 2
                    tslot += 1
                    ta = self.tab[:, ts_, 0:n]
                    tg = self.tab[:, 2 + ts_, 0:n]
                    for (t_, bb, ch, tk) in ((ta, bA, hi, ("ta", ts_)), (tg, bG, 22 + hi, ("tg", ts_))):
                        pa = self.ps[bb][:, 0:n]
                        self.act(t_, pa, AF.Identity, r=[("ps", bb), "vecF"], w=[tk], scale=cv[:, ch, 1:2], bias=cv[:, ch, 3:4])
                        if len(ft["segs"]) == 2:
                            L = 256
                            p3 = pa.rearrange("p (s l) -> p s l", s=2)
                            t3 = t_.rearrange("p (s l) -> p s l", s=2)
                            self.stt(t3[:, :, 1:L], p3[:, :, 0:L - 1], cv[:, ch, 0:1], t3[:, :, 1:L], ALU.mult, ALU.add,
                                     r=[("ps", bb), tk, "vecF"], w=[tk])
                            self.stt(t3[:, :, 0:L - 1], p3[:, :, 1:L], cv[:, ch, 2:3], t3[:, :, 0:L - 1], ALU.mult, ALU.add,
                                     r=[("ps", bb), tk, "vecF"], w=[tk])
                        else:
                            self.stt(t_[:, 1:n], pa[:, 0:n - 1], cv[:, ch, 0:1], t_[:, 1:n], ALU.mult, ALU.add,
                                     r=[("ps", bb), tk, "vecF"], w=[tk])
                            self.stt(t_[:, 0:n - 1], pa[:, 1:n], cv[:, ch, 2:3], t_[:, 0:n - 1], ALU.mult, ALU.add,
                                     r=[("ps", bb), tk, "vecF"], w=[tk])
                    self.act(ta, ta, AF.Silu, r=[("ta", ts_)], w=[("ta", ts_)])
                    o0, no, oc0 = ft["o0"], ft["no"], ft["oc0"]
                    self.tt(self.actT[:, hi, oc0:oc0 + no], ta[:, o0:o0 + no], tg[:, o0:o0 + no], ALU.mult,
                            r=[("ta", ts_), ("tg", ts_)], w=[("actT", hi)])
        self.mod_rows(cond, 1)
        wd = self.w_down.rearrange("(kc p) c -> p kc c", p=128)
        ykeys = ["ybuf%d" % i for i in range(6)]
        P.alias(ykeys, ["pT", "T1", "T2", ("ta", 0), ("ta", 1), ("tg", 0), ("tg", 1)])
        yslot = 0
        f2i = [0]
        pfs = list(G.get("prefetch_steps", []))
        if pfs:
            pfs.pop(0)()
        for j in range(2):
            banks = [self.bank("main") for _ in own]
            for u in range(3):
                nk_ = 8 if u < 2 else 6
                f2i[0] += 1
                su = self.unit([(lambda R, nk_=nk_: R[:, 0:nk_, :], wd[:, 8 * u:8 * u + nk_, j * 512:(j + 1) * 512])])
                for oi in range(len(own)):
                    b = banks[oi]
                    for kk in range(nk_):
                        k = 8 * u + kk
                        self.mm(self.ps[b][:, :], self.actT[:, k, oi * 128:(oi + 1) * 128], self.R[su][:, kk, :], k == 0, k == 21,
                                r=[("R", su), ("actT", k)], w=[("ps", b)], inc=(kk == nk_ - 1))
                if pfs:
                    pfs.pop(0)()
            for oi, xi in enumerate(own):
                b = banks[oi]
                ys = yslot % 6
                yslot += 1
                yb = self.ybuf[:, ys, :]
                self.tt(yb, self.ps[b][:, :], self.GF[:, j * 512:(j + 1) * 512], ALU.mult, r=[("ps", b), "GF"], w=["ybuf%d" % ys])
                self.tt(yb, yb, self.xres[:, xi, j * 512:(j + 1) * 512], ALU.add, r=["ybuf%d" % ys, ("x", xi)], w=["ybuf%d" % ys])
                if kind == "p":
                    dst = self.yp[G["g"] * 512 + oi * 128: G["g"] * 512 + (oi + 1) * 128, j * 512:(j + 1) * 512]
                else:
                    dst = self.ys[oi * 128:(oi + 1) * 128, j * 512:(j + 1) * 512]
                self.st("yb%d" % ys, dst, yb, r=["ybuf%d" % ys])
        if G.get("post_f2") is not None:
            G["post_f2"]()
        P.alias([("KT", c) for c in range(9)] + [("QT", c) for c in range(6)] + [("PTr", c) for c in range(6)], actkeys)
        P.alias(["pT", "T1", "T2", ("ta", 0), ("ta", 1), ("tg", 0), ("tg", 1)], ykeys + [("ta", 0), ("ta", 1), ("tg", 0), ("tg", 1)])

    def _norm2(self, i, xc, m, G2, S2):
        keys = [("hT", c) for c in range(xc // 128, (xc + m - 1) // 128 + 1)]
        x_ap = self.xres[0:m, i, :]
        si = self.stat(1)
        st = self.small[0:m, si:si + 1]
        skey = ("small", si)
        slot = i % 2
        self.act(self.junk[0:m, :], x_ap, AF.Square, r=[("x", i)], w=["junk", skey], accum=st)
        self.rstd_from_ssq(st, st, float(D), m, 1, [skey])
        xh = self.xh[0:m, slot, :]
        hk = ("xh", slot)
        self.ts(xh, x_ap, st, None, ALU.mult, None, r=[skey, ("x", i)], w=[hk])
        for fc in range(8):
            b = self.bank("tr")
            self.tr(self.ps[b][:, 0:m], xh[:, fc * 128:(fc + 1) * 128], self.identF[0:m, 0:m], r=[hk, "identF"], w=[("ps", b)])
            if fc % 2 == 0:
                self.act(self.hT[:, fc, xc:xc + m], self.ps[b][:, 0:m], AF.Identity, r=[("ps", b), "G2F", "modF"], w=keys,
                         scale=G2[:, fc:fc + 1], bias=S2[:, fc:fc + 1])
            else:
                self.ts(self.hT[:, fc, xc:xc + m], self.ps[b][:, 0:m], G2[:, fc:fc + 1], S2[:, fc:fc + 1], ALU.mult, ALU.add,
                        r=[("ps", b), "G2F", "modF"], w=keys)

    def attention(self, G, xch, ntiles, inter=()):
        kind = G["kind"]
        inter = list(inter)
        units, flat = [], []
        for h in range(8):
            for ti, T in enumerate(ntiles):
                chunks = [(c, True) for c in T["kv"]] if T["local"] else [(c, False) for c in T["kv"]]
                chunks += [(c, False) for c in T["ctx"]]
                u = dict(h=h, T=T, chunks=chunks, nck=len(chunks), first=True, bO=None, last_tile=(ti == len(ntiles) - 1))
                units.append(u)
                for ci in range(len(chunks)):
                    flat.append((u, ci))
        eb_loaded = set()
        slots = {}
        ptk = [("PTr", c) for c in range(6)]
        self.P.alias(ptk, ptk)
        if kind == "p":
            PTv, nslots, spool, LA = self.PTr6, 6, "S6", 5
        else:
            PTv, nslots, spool, LA = self.PTr, 4, "S", 3

        def emit_S(k):
            u, ci = flat[k]
            h, T = u["h"], u["T"]
            j = h // 2
            x0, n = T["x0"], T["n"]
            qkeys = [("QT", i) for i in T["ch"]]
            c, loc = u["chunks"][ci]
            if kind == "s" and h not in eb_loaded:
                eb_loaded.add(h)
                es_ = h % 2
                self.ld("EB%d" % es_, self.EB[:, es_, 4 * 64:22 * 64], self.t3_d[h][:, 4 * 64:22 * 64], w=[("EB", es_)], q="pool")
                self.act(self.EB[:, es_, 4 * 64:22 * 64], self.EB[:, es_, 4 * 64:22 * 64], AF.Exp, r=[("EB", es_)], w=[("EB", es_)])
            b = self.bank(spool)
            lo, hi = T.get("lo", 0), T.get("hi", n)
            if loc and c == 8:
                lo = 256
            self.mm(self.ps[b][:, lo:hi], self.KT[:, j, c * 128:(c + 1) * 128], self.Qz[:, h, x0 + lo:x0 + hi],
                    True, not loc, r=[("KT", c)] + qkeys, w=[("ps", b)], skip=loc)
            if loc:
                r0 = T["r0"]
                rhs = self.negm[:, c * 10 + r0: c * 10 + r0 + 5].unsqueeze(2).broadcast_to([128, 5, 64])
                self.mm(self.ps[b][:, 0:n], self.half[:, :], rhs, False, True, r=["half", "negm"], w=[("ps", b)], skip=True)
            slot = self.pt_i % nslots
            self.pt_i += 1
            pt = PTv[:, slot, lo:hi]
            self.act(pt, self.ps[b][:, lo:hi], AF.Exp, r=[("ps", b)], w=[("PTr", slot)], scale=0.125)
            if loc:
                e0 = T["r0"] - 2 * c + 16
                es_ = h % 2
                self.tt(pt, pt, self.EB[:, es_, e0 * 64 + lo:e0 * 64 + hi], ALU.mult, r=[("PTr", slot), ("EB", es_)], w=[("PTr", slot)])
            slots[k] = slot

        def emit_PV(k):
            u, ci = flat[k]
            h, T = u["h"], u["T"]
            x0 = T["x0"]
            c, loc = u["chunks"][ci]
            slot = slots.pop(k)
            if u["bO"] is None:
                u["bO"] = self.bank("O")
            bO = u["bO"]
            order = sorted(range(len(T["ch"])), key=lambda q: -xch[T["ch"][q]][1])
            if loc and c == 8:
                order = [q for q in order if T["ch"][q] == 5]
            for oi_, qi in enumerate(order):
                xc, m = xch[T["ch"][qi]]
                last = oi_ == len(order) - 1
                self.mm(self.ps[bO][:, qi * 65:(qi + 1) * 65], PTv[:, slot, xc - x0: xc - x0 + 128], self.Vaug[:, c, h, :],
                        u["first"], (ci == u["nck"] - 1),
                        r=[("PTr", slot), ("V", c)], w=[("ps", bO)], inc=last, skip=True)
                u["first"] = False
            if ci == u["nck"] - 1:
                for qi, xi in enumerate(T["ch"]):
                    xc, m = xch[xi]
                    si = self.stat(1)
                    rc = self.small[0:m, si:si + 1]
                    self.recip(rc, self.ps[bO][0:m, qi * 65 + 64: qi * 65 + 65], r=[("ps", bO)], w=[("small", si)])
                    self.ts(self.na[0:m, xi, h * 64:(h + 1) * 64], self.ps[bO][0:m, qi * 65: qi * 65 + 64], rc, None, ALU.mult, None,
                            r=[("ps", bO), ("small", si)], w=[("na", xi)])

        nf = len(flat)
        rate = len(inter) / max(1.0, nf - 6.0)
        acc = 0.0
        for k in range(min(LA, nf)):
            emit_S(k)
        for k in range(nf):
            if k + LA < nf:
                emit_S(k + LA)
            emit_PV(k)
            acc += rate
            while acc >= 1.0 and inter:
                inter.pop(0)()
                acc -= 1.0
        while inter:
            inter.pop(0)()

    def pool_stage(self, G, s, XOFF, NX, hkey):
        kind = G["kind"]
        if kind == "p":
            nseg, L, LP = 2, 256, 272
            tabL, tabR = 0, 1
        else:
            nseg, L, LP = 1, 640, 656
            tabL, tabR = 2, 3
        pT4 = self.PF[:, 0:4 * nseg * LP].rearrange("p (g s c) -> p g s c", g=4, s=nseg)
        if kind == "p":
            for g in range(4):
                self.memset(pT4[:, g, :, 0:8], 0.0, w=["pT"])
                self.memset(pT4[:, g, :, 8 + L:LP], 0.0, w=["pT"])
        for g in range(4):
            if kind == "p":
                b = self.bank("main")
                for kc in range(8):
                    self.mm(self.ps[b][:, :], self.R[s][:, kc, g * 128:(g + 1) * 128], self.hT[:, kc, 0:512], kc == 0, kc == 7,
                            r=[("R", s)] + hkey(0, 512), w=[("ps", b)])
                self.act(pT4[:, g, :, 8:8 + L], self.ps[b][:, :].rearrange("p (s l) -> p s l", s=2), AF.Identity, r=[("ps", b)], w=["pT"])
            else:
                for nt in range(2):
                    b = self.bank("main")
                    c0 = XOFF - 8 + nt * 328
                    for kc in range(8):
                        self.mm(self.ps[b][:, 0:328], self.R[s][:, kc, g * 128:(g + 1) * 128], self.hT[:, kc, c0:c0 + 328], kc == 0, kc == 7,
                                r=[("R", s)] + hkey(c0, 328), w=[("ps", b)])
                    self.tt(pT4[:, g, 0, nt * 328:(nt + 1) * 328], self.ps[b][:, 0:328], self.pmask[:, nt * 328:(nt + 1) * 328], ALU.mult,
                            r=[("ps", b), "pmask"], w=["pT"])
        T1 = self.PF[:, 2624:2624 + nseg * LP].rearrange("p (s c) -> p s c", s=nseg)
        T2 = self.PF[:, 3280:3280 + nseg * LP].rearrange("p (s c) -> p s c", s=nseg)
        tabv = self.ptab[:, :].rearrange("p (t g d) -> p t g d", t=4, g=4)

        from functools import partial
        ops = []

        def chain(g):
            p = pT4[:, g]
            w = POOL_W[g]
            ops.append(partial(self.tt, T1[:, :, 1:LP], p[:, :, 0:LP - 1], p[:, :, 1:LP], ALU.add, r=["pT"], w=["T1"]))
            W = T1
            wk = "T1"
            if g >= 1:
                ops.append(partial(self.tt, T2[:, :, 2:LP - 1], T1[:, :, 1:LP - 2], T1[:, :, 3:LP], ALU.add, r=["T1"], w=["T2"]))
                W, wk = T2, "T2"
            if g >= 2:
                ops.append(partial(self.tt, T1[:, :, 4:LP - 3], T2[:, :, 2:LP - 5], T2[:, :, 6:LP - 1], ALU.add, r=["T2"], w=["T1"]))
                W, wk = T1, "T1"
            if g >= 3:
                ops.append(partial(self.tt, T2[:, :, 8:LP - 7], T1[:, :, 4:LP - 11], T1[:, :, 12:LP - 3], ALU.add, r=["T1"], w=["T2"]))
                W, wk = T2, "T2"
            po = self.pooledT[:, g, 0:nseg * L].rearrange("p (s l) -> p s l", s=nseg)
            ops.append(partial(self.stt, po, W[:, :, 8:8 + L], 1.0 / w, p[:, :, 8:8 + L], ALU.mult, ALU.subtract,
                               r=[wk, "pT"], w=[("pl", g, 0), ("pl", g, 1)]))
            if kind == "p":
                edges = [(0, tabL), (L - 8, tabR)]
            else:
                edges = [(64, tabL), (568, tabR)]
            for (lc, tb) in edges:
                te = self.tmpE[:, 0:nseg * 8].rearrange("p (s d) -> p s d", s=nseg)
                tv = tabv[:, tb, g, :].unsqueeze(1).broadcast_to([128, nseg, 8])
                ops.append(partial(self.tt, te, W[:, :, 8 + lc:8 + lc + 8], tv, ALU.mult, r=[wk, "ptab"], w=["tmpE"]))
                ops.append(partial(self.tt, po[:, :, lc:lc + 8], te, p[:, :, 8 + lc:8 + lc + 8], ALU.subtract,
                                   r=["tmpE", "pT"], w=[("pl", g, 0), ("pl", g, 1)]))

        for g in range(4):
            chain(g)
        return ops

    def pool_mix(self, G):
        kind = G["kind"]
        mt = [(0, 512)] if kind == "p" else [(63, 257), (320, 257)]
        for g in range(4):
            for ti, (x0, n) in enumerate(mt):
                b = self.bank("main")
                self.mm(self.ps[b][:, 0:n], self.poolw[:, g, :], self.pooledT[:, g, x0:x0 + n], True, True, r=["poolw", ("pl", g, ti)], w=[("ps", b)])
                self.act(self.pooloutT[:, g, x0:x0 + n], self.ps[b][:, 0:n], AF.Identity, r=[("ps", b), "vecF"], w=[("pl", g, ti)],
                         scale=self.vecF[:, 64 + g:65 + g])


Builder.pt_i = 0


def _feat(v):
    v = np.asarray(v, np.float32)
    return np.ascontiguousarray(v.reshape(-1, 128).T)


def _geometry(j):
    R0 = 8 * j
    negm = np.full((2, 9, 10), NEG, np.float32)
    for c in range(9):
        for hf in range(2):
            ak = R0 - 5 + 2 * c + hf
            for r in range(10):
                aq = R0 - 1 + r
                if not (0 <= aq < 32) or not (0 <= ak < 32):
                    continue
                st = min(max(aq - 4, 0), 24)
                if st <= ak < st + 8:
                    negm[hf, c, r] = 0.0
    ptab = np.zeros((4, 4, 8), np.float32)
    for g, w in enumerate(POOL_W):
        for d in range(8):
            ptab[0, g, d] = 1.0 / (w // 2 + min(d, w // 2))
            dd = 7 - d
            ptab[1, g, d] = 1.0 / (w // 2 + min(w // 2, dd + 1))
            ptab[2, g, d] = ptab[0, g, d] if j == 0 else 1.0 / w
            ptab[3, g, d] = ptab[1, g, d] if j == 3 else 1.0 / w
    tok = (R0 - 5) * 64 + 248 + np.arange(656)
    pmask = ((tok >= 0) & (tok < 2048)).astype(np.float32)
    hval = np.array([1.0 if j > 0 else 0.0, 1.0 if j < 3 else 0.0], np.float32)
    return negm, ptab, pmask, hval


def _t3(rpb):
    t3 = np.full((8, 2, 64, NE, 64), NEG, np.float32)
    kc = np.arange(64)[:, None]
    qc = np.arange(64)[None, :]
    ws = np.clip(qc - 8, 0, 48)
    colv = (kc >= ws) & (kc < ws + 16)
    dc = np.clip(kc - qc + 15, 0, 30)
    for hf in range(2):
        for e in range(NE):
            i = hf + 19 - e
            if 0 <= i <= 14:
                vals = rpb[:, i][:, dc]
                t3[:, hf, :, e, :] = np.where(colv[None], vals, NEG)
    return np.ascontiguousarray(t3.reshape(8, 128, NE * 64))


_NC_CACHE = {}


def make_in_maps(x_prompt, x_sample, cache_k, cache_v, c, c_ctx, norm_mix_g, norm_ffn_g, w_mod, b_mod,
                 w_in, q_norm_g, k_norm_g, pool_w, pool_scale, na_rpb, w_pool_proj, w_na_proj, w_o,
                 w_up, ffn_conv_w, ffn_conv_b, w_down):
    f = lambda a: np.ascontiguousarray(np.asarray(a, np.float32))
    x_prompt, x_sample, cache_k, cache_v = f(x_prompt), f(x_sample), f(cache_k), f(cache_v)
    c, c_ctx = f(c), f(c_ctx)
    shared = dict(w_mod=f(w_mod[0]), w_in=f(w_in[0]), pool_w=f(pool_w[0]), w_pp=f(w_pool_proj[0]), w_np=f(w_na_proj[0]),
                  w_o=f(w_o[0]), w_up=f(w_up[0]), w_down=f(w_down[0]))
    convF = np.stack([_feat(ffn_conv_w[0, 0]), _feat(ffn_conv_w[0, 1]), _feat(ffn_conv_w[0, 2]), _feat(ffn_conv_b[0])], axis=2)
    gqk = np.ascontiguousarray(np.broadcast_to(np.concatenate([f(q_norm_g[0]), f(k_norm_g[0])])[None, :], (128, 128)))
    bm = f(b_mod[0])
    bgagf = np.ascontiguousarray(np.broadcast_to(np.concatenate([bm[2048:3072], bm[5120:6144]])[None, :], (2, 2048)))
    ident = np.eye(128, dtype=np.float32)
    half = np.zeros((128, 128), np.float32)
    half[0:2] = np.repeat(np.eye(2, dtype=np.float32), 64, axis=1)
    t3 = _t3(f(na_rpb[0]))
    ones1 = np.ones((1, 128), np.float32)
    in_maps = []
    for i in range(8):
        b, j = i // 4, i % 4
        R0 = 8 * j
        lo, hi = (R0 - 5) * 64, (R0 + 13) * 64
        xs = np.zeros((1152, D), np.float32)
        a, e = max(lo, 0), min(hi, 2048)
        xs[a - lo:e - lo] = x_sample[b, a:e]
        negm, ptab, pmask, hval = _geometry(j)
        condF = np.stack([_feat(c_ctx), _feat(c[b])], axis=2)
        vecF = np.concatenate([_feat(norm_mix_g[0]), _feat(norm_ffn_g[0]), _feat(bm), _feat(pool_scale[0]),
                               convF.reshape(128, 176), condF.reshape(128, 16)], axis=1)
        m = dict(shared)
        m.update(xp=np.ascontiguousarray(x_prompt[4 * i:4 * i + 4].reshape(1024, D)), xs=xs,
                 ck=cache_k[b, 0], cv=cache_v[b, 0], vecF=np.ascontiguousarray(vecF, dtype=np.float32), gqk=gqk, bgagf=bgagf,
                 ident=ident, half=half, t3=t3, negm=np.ascontiguousarray(np.concatenate([negm.reshape(2, 90), np.zeros((126, 90), np.float32)], axis=0)),
                 ptab=np.ascontiguousarray(np.broadcast_to(ptab.reshape(1, 128), (128, 128))),
                 pmask=np.ascontiguousarray(np.broadcast_to(pmask[None, :], (128, 656))),
                 hval=np.ascontiguousarray(np.broadcast_to(hval[None, :], (128, 2))), ones1=ones1)
        in_maps.append(m)
    return in_maps


def kernel(**inputs):
    in_maps = make_in_maps(**inputs)
    if "nc" not in _NC_CACHE:
        _NC_CACHE["nc"] = Builder().build()
    nc = _NC_CACHE["nc"]
    res = run_bass_kernel_spmd(nc, in_maps, core_ids=list(range(8)))
    rs = res.results
    y_prompt = np.concatenate([r["yp"].reshape(4, 256, D) for r in rs], axis=0)
    y_sample = np.stack([np.concatenate([rs[4 * b + j]["ys"] for j in range(4)], axis=0) for b in range(2)], axis=0)
    new_k = np.concatenate([r["nk"] for r in rs], axis=0)[:, None]
    new_v = np.concatenate([r["nv"] for r in rs], axis=0)[:, None]
    return (y_prompt.astype(np.float32), y_sample.astype(np.float32),
            np.ascontiguousarray(new_k, dtype=np.float32), np.ascontiguousarray(new_v, dtype=np.float32))
```

```python
import numpy as np
from contextlib import ExitStack
import concourse.bass as bass
import concourse.mybir as mybir
from concourse.bass_utils import run_bass_kernel_spmd

F32 = mybir.dt.float32
BF16 = mybir.dt.bfloat16
AF = mybir.ActivationFunctionType
ALU = mybir.AluOpType
AX = mybir.AxisListType

D = 1024
DFF = 2816
NEG = -30000.0
EPS = 1e-6
NRING = 5
NE = 26
POOL_W = (2, 4, 8, 16)
DEBUG_STOP = None
DEBUG_LVL = 99


class Prog:
    ENG = ("pe", "act", "dve", "pool", "sp")

    def __init__(self, nc, es):
        self.nc = nc
        self.es = es
        self.ops = {e: [] for e in self.ENG}
        self.cnt = {e: 0 for e in self.ENG}
        self.sem = {e: es.enter_context(nc.semaphore("s_" + e)) for e in self.ENG}
        self.res = {}
        self.seen = {e: {} for e in self.ENG}
        self.dsem = {}
        self.dcnt = {}

    def _deps(self, eng, r, w):
        deps = {}

        def add(d):
            if d is None:
                return
            s, v = d
            if s == "pe" and eng == "pe":
                return
            if deps.get(s, 0) < v:
                deps[s] = v

        for k in r:
            st = self.res.get(k)
            if st:
                add(st["w"])
        for k in w:
            st = self.res.get(k)
            if st:
                add(st["w"])
                for d in st["r"]:
                    add(d)
        waits = []
        for s, v in deps.items():
            if self.seen[eng].get(s, 0) >= v:
                continue
            self.seen[eng][s] = v
            waits.append((s, v))
        return waits

    def _mark(self, tok, r, w):
        for k in r:
            st = self.res.setdefault(k, {"w": None, "r": []})
            if len(st["r"]) > 64:
                best = {}
                for s, v in st["r"]:
                    if best.get(s, 0) < v:
                        best[s] = v
                st["r"] = list(best.items())
            st["r"].append(tok)
        for k in w:
            self.res[k] = {"w": tok, "r": []}

    def op(self, eng, fn, r=(), w=(), inc=True):
        waits = self._deps(eng, r, w)
        val = self.cnt[eng] + 1
        if inc:
            self.cnt[eng] = val
        self.ops[eng].append((waits, fn, ("e", eng) if inc else None))
        self._mark((eng, val), r, w)

    def dma(self, q, skey, fn, r=(), w=()):
        if skey not in self.dsem:
            self.dsem[skey] = self.es.enter_context(self.nc.semaphore("d_" + skey))
            self.dcnt[skey] = 0
        waits = self._deps(q, r, w)
        self.dcnt[skey] += 16
        self.ops[q].append((waits, fn, ("d", skey)))
        self._mark(("d:" + skey, self.dcnt[skey]), r, w)

    def alias(self, newkeys, oldkeys):
        acc = []
        for k in oldkeys:
            st = self.res.get(k)
            if st:
                if st["w"]:
                    acc.append(st["w"])
                acc.extend(st["r"])
        for k in newkeys:
            st = self.res.setdefault(k, {"w": None, "r": []})
            st["r"].extend(acc)

    def _semof(self, s):
        if s.startswith("d:"):
            return self.dsem[s[2:]]
        return self.sem[s]

    def finish_waits(self, eng):
        waits = [("d:" + k, c) for k, c in self.dcnt.items()]
        self.ops[eng].append((waits, None, None))

    def emit(self):
        nc = self.nc
        hw = {"pe": "tensor", "act": "scalar", "dve": "vector", "pool": "gpsimd", "sp": "sync"}
        with nc.Block() as block:
            for e in self.ENG:
                lst = self.ops[e]

                def body(engine, lst=lst):
                    for waits, fn, inc in lst:
                        for s, v in waits:
                            engine.wait_ge(self._semof(s), v)
                        if fn is None:
                            continue
                        ins = fn(engine)
                        if inc is not None:
                            if inc[0] == "e":
                                ins.then_inc(self.sem[inc[1]], 1)
                            else:
                                ins.then_inc(self.dsem[inc[1]], 16)

                getattr(block, hw[e])(body)


class Builder:
    def __init__(self):
        self.nc = bass.Bass("TRN2", target_bir_lowering=False)
        self.es = ExitStack()

    def mm(self, out, lhsT, rhs, start, stop, r, w, inc=None, skip=False):
        inc = stop if inc is None else inc
        self.P.op("pe", lambda e, o=out, l=lhsT, rr=rhs, s=start, t=stop, sk=skip:
                  e.matmul(o, lhsT=l, rhs=rr, start=s, stop=t, skip_group_check=sk), r=r, w=w, inc=inc)

    def tr(self, out, in_, ident, r, w):
        self.P.op("pe", lambda e, o=out, i=in_, d=ident: e.transpose(o, i, d), r=r, w=w)

    def act(self, out, in_, func, r, w, scale=None, bias=None, accum=None):
        kw = {}
        if scale is not None:
            kw["scale"] = scale
        if bias is not None:
            kw["bias"] = bias
        if accum is not None:
            kw["accum_out"] = accum
        self.P.op("act", lambda e, o=out, i=in_, f=func, kw=kw: e.activation(out=o, in_=i, func=f, **kw),
                  r=r, w=w)

    def tt(self, out, a, b, op, r, w, eng="dve"):
        self.P.op(eng, lambda e, o=out, a=a, b=b, op=op: e.tensor_tensor(out=o, in0=a, in1=b, op=op), r=r, w=w)

    def ts(self, out, a, s1, s2, op0, op1, r, w, eng="dve"):
        if s2 is None:
            self.P.op(eng, lambda e, o=out, a=a, s1=s1, op0=op0:
                      e.tensor_scalar(out=o, in0=a, scalar1=s1, scalar2=None, op0=op0), r=r, w=w)
        else:
            self.P.op(eng, lambda e, o=out, a=a, s1=s1, s2=s2, op0=op0, op1=op1:
                      e.tensor_scalar(out=o, in0=a, scalar1=s1, scalar2=s2, op0=op0, op1=op1), r=r, w=w)

    def stt(self, out, a, s, b, op0, op1, r, w):
        self.P.op("dve", lambda e, o=out, a=a, s=s, b=b, op0=op0, op1=op1:
                  e.scalar_tensor_tensor(out=o, in0=a, scalar=s, in1=b, op0=op0, op1=op1), r=r, w=w)

    def cp(self, out, in_, r, w, eng="dve"):
        self.P.op(eng, lambda e, o=out, i=in_: e.tensor_copy(out=o, in_=i), r=r, w=w)

    def recip(self, out, in_, r, w):
        self.P.op("dve", lambda e, o=out, i=in_: e.reciprocal(out=o, in_=i), r=r, w=w)

    def memset(self, ap, val, w, eng="dve"):
        self.P.op(eng, lambda e, a=ap, v=val: e.memset(a, v), w=w)

    def ld(self, skey, out, in_, w, r=(), q="sp"):
        if skey in ("c", "cp"):
            self._cn = getattr(self, "_cn", 0) + 1
            skey = "%s%d" % (skey, self._cn)
        self.P.dma(q, skey, lambda e, o=out, i=in_: e.dma_start(out=o, in_=i), r=r, w=w)

    def st(self, skey, out, in_, r, q="sp"):
        self.P.dma(q, skey, lambda e, o=out, i=in_: e.dma_start(out=o, in_=i), r=r, w=())

    def bank(self, pool):
        lst, idx = self.pools[pool]
        b = lst[idx % len(lst)]
        self.pools[pool][1] = idx + 1
        return b

    def unit(self, loads, slot=None):
        if slot is None:
            s = self.uidx % NRING
            self.uidx += 1
        else:
            s = slot
        R = self.R[s]
        for dfn, src in loads:
            self.ld("R%d" % s, dfn(R), src, w=[("R", s)], q="pool")
        return s

    def sb(self, name, shape, dt):
        return self.es.enter_context(self.nc.sbuf_tensor("sb_" + name, shape, dt))

    def build(self):
        nc, es = self.nc, self.es
        self.P = P = Prog(nc, es)
        din = lambda n, s: nc.dram_tensor(n, s, F32, kind="ExternalInput").ap()
        dout = lambda n, s: nc.dram_tensor(n, s, F32, kind="ExternalOutput").ap()
        self.xp = din("xp", [1024, D])
        self.xs = din("xs", [1152, D])
        self.ck = din("ck", [8, 512, 64])
        self.cv = din("cv", [8, 512, 64])
        self.w_mod = din("w_mod", [D, 6 * D])
        self.w_in = din("w_in", [D, 4096])
        self.pool_w = din("pool_w", [4, 128, 128])
        self.w_pp = din("w_pp", [512, D])
        self.w_np = din("w_np", [512, D])
        self.w_o = din("w_o", [D, D])
        self.w_up = din("w_up", [D, 2 * DFF])
        self.w_down = din("w_down", [DFF, D])
        self.vecF_d = din("vecF", [128, 260])
        self.gqk_d = din("gqk", [128, 128])
        self.bgagf_d = din("bgagf", [2, 2048])
        self.mscr = nc.dram_tensor("mscr", [1, 2048], F32, kind="Internal").ap()
        self.ident_d = din("ident", [128, 128])
        self.half_d = din("half", [128, 128])
        self.t3_d = din("t3", [8, 128, NE * 64])
        self.negm_d = din("negm", [128, 90])
        self.ptab_d = din("ptab", [128, 128])
        self.pmask_d = din("pmask", [128, 656])
        self.hval_d = din("hval", [128, 2])
        self.ones_d = din("ones1", [1, 128])
        self.yp = dout("yp", [1024, D])
        self.ys = dout("ys", [512, D])
        self.nk = dout("nk", [4, 8, 256, 64])
        self.nv = dout("nv", [4, 8, 256, 64])

        sb = self.sb
        self.R = [sb("ring%d" % i, [128, 8, 512], BF16) for i in range(NRING)]
        self.xres = sb("xres", [128, 6, D], F32)
        self.xh = sb("xh", [128, 2, D], BF16)
        self.kbq = sb("kbq", [128, 2, 512], BF16)
        self.hT = sb("hT", [128, 8, 1152], BF16)
        self.ATT = sb("ATT", [128, 13312], BF16)
        self.KT = self.ATT[:, 0:6656].rearrange("p (j c) -> p j c", j=4)
        self.Qz = self.ATT[:, 6656:11776].rearrange("p (h c) -> p h c", h=8)
        self.PTr = self.ATT[:, 11776:13312].rearrange("p (j c) -> p j c", j=4)
        self.actT = self.ATT[:, 0:11264].rearrange("p (j c) -> p j c", j=22)
        self.Vaug = sb("Vaug", [128, 13, 8, 65], BF16)
        self.sq = sb("sq", [128, 3, 512], F32)
        self.kf = sb("kf", [128, 4, 512], F32)
        self.kb = sb("kb", [128, 2, 512], BF16)
        self.EB = sb("EB", [128, 2, NE * 64], BF16)
        self.na = sb("na", [128, 6, 512], BF16)
        self.naT = sb("naT", [128, 4, 640], BF16)
        self.PF = sb("PF", [128, 3936], F32)
        self.pT = self.PF[:, 0:2624].rearrange("p (g c) -> p g c", g=4)
        self.T1 = self.PF[:, 2624:3280]
        self.T2 = self.PF[:, 3280:3936]
        self.tab = self.PF[:, 0:2048].rearrange("p (s c) -> p s c", s=4)
        self.ybuf = self.PF[:, 0:3072].rearrange("p (s c) -> p s c", s=6)
        self.pmask = sb("pmask", [128, 656], BF16)
        self.pooledT = sb("pooledT", [128, 4, 640], BF16)
        self.pooloutT = self.pooledT
        self.mergedT = sb("mergedT", [128, 8, 640], BF16)
        self.GA = sb("GA", [128, D], F32)
        self.GF = sb("GF", [128, D], F32)
        self.identF = sb("identF", [128, 128], F32)
        self.identB = sb("identB", [128, 128], BF16)
        self.vecF = sb("vecF", [128, 260], F32)
        self.gqk = sb("gqk", [128, 128], F32)
        self.scb = sb("scb", [128, 8, 2], BF16)
        self.modF = sb("modF", [128, 2, 48], F32)
        self.G1F = sb("G1F", [128, 2, 8], F32)
        self.G2F = sb("G2F", [128, 2, 8], F32)
        self.poolw = sb("poolw", [128, 4, 128], BF16)
        self.half = sb("halfb", [128, 128], BF16)
        self.negm = sb("negmb", [128, 90], BF16)
        self.ptab = sb("ptab", [128, 128], F32)
        self.hval = sb("hval", [128, 2], F32)
        self.ones1 = sb("ones1s", [1, 128], F32)
        self.mrow = self.sq[0:1, 2, :]
        self.brow = self.kf[0:1, 3, :]
        self.small = sb("small", [128, 64], F32)
        self.tmpE = sb("tmpE", [128, 64], F32)
        self.ps = [es.enter_context(nc.psum_tensor("ps%d" % i, [128, 512], F32)) for i in range(8)]
        self.psb = [p.bitcast(BF16) for p in self.ps]
        self.pools = {"main": [[0, 1, 2, 3, 4, 5], 0], "tr": [[6, 7], 0], "S": [[0, 1, 2, 3], 0],
                      "O": [[4, 5], 0], "all8": [[0, 1, 2, 3, 4, 5, 6, 7], 0]}
        self.uidx = 0
        self.small_i = 0

        self.row_cond = [None, None]
        self.prologue()
        def pf_steps(kind, cond):
            G1n = self.G1F[:, cond, :]
            S1n = self.modF[:, cond, 0:8]

            def land(c):
                if kind == "p":
                    ls = 4 + c % 2
                    return self.xres[:, ls, :], [("x", ls)], "x%d" % ls, self.xp[512 + c * 128: 512 + (c + 1) * 128, :]
                sl = c % 2
                xl = self.kf[:, 2 * sl:2 * sl + 2, :].rearrange("p a c -> p (a c)")
                return xl, [("kf", 2 * sl), ("kf", 2 * sl + 1)], "kf%d" % (2 * sl), self.xs[c * 128:(c + 1) * 128, :]

            def load(c):
                ap, keys, sem, src = land(c)
                self.ld(sem, ap, src, w=keys)

            def n1(c):
                ap, keys, sem, src = land(c)
                self.norm_A(ap, keys, 128, c % 2, False)

            def n2(c):
                self.norm_B(128, c % 2, G1n, S1n, c * 128, [("hT", c)])

            return [lambda: (load(0), load(1), n1(0)),
                    lambda: (n1(1), n2(0), load(2)),
                    lambda: (n1(2), n2(1), load(3)),
                    lambda: (n1(3), n2(2)),
                    lambda: n2(3)]

        def pre_units():
            win = self.w_in.rearrange("(kc p) c -> p kc c", p=128)
            return (self.unit([(lambda R: R[:, :, :], win[:, :, 1024:1536])]),
                    self.unit([(lambda R: R[:, :, :], win[:, :, 1536:2048])]),
                    self.unit([(lambda R: R[:, :, :], win[:, :, 512:1024])]))

        G0 = dict(kind="p", g=0, cond=0)
        G1 = dict(kind="p", g=1, cond=0, prefetched=True)
        G2 = dict(kind="s", cond=1, prefetched=True)
        G0["prefetch_steps"] = pf_steps("p", 0)
        G1["prefetch_steps"] = pf_steps("s", 1)
        G0["post_f2"] = lambda: G1.__setitem__("pre_units", pre_units())
        G1["post_f2"] = lambda: G2.__setitem__("pre_units", pre_units())
        self.group(G0)
        self.group(G1)
        self.group(G2)
        P.finish_waits("sp")
        P.emit()
        return nc

    def dbg_stop(self, tag):
        if DEBUG_STOP != tag:
            return False
        dbg = self.nc.dram_tensor("dbg", [4, 128, D], F32, kind="ExternalOutput").ap()
        for i in range(4):
            self.st("dbg", dbg[i], self.xres[:, i, :], r=[("x", i)])
        return True

    def stat(self, n=1):
        i = self.small_i
        if i + n > 64:
            i = 0
        self.small_i = i + n
        return i

    def prologue(self):
        P = self.P
        ld = self.ld
        ld("c", self.vecF[:], self.vecF_d, w=["vecF"])
        ld("c", self.identF[:], self.ident_d, w=["identF"])
        ld("c", self.gqk[:], self.gqk_d, w=["gqk"])
        ld("c", self.ptab[:], self.ptab_d, w=["ptab"])
        ld("cp", self.pmask[:], self.pmask_d, w=["pmask"], q="pool")
        ld("c", self.hval[:], self.hval_d, w=["hval"])
        ld("c", self.ones1[:], self.ones_d, w=["ones1"])
        ld("cp", self.half[:], self.half_d, w=["half"], q="pool")
        ld("cp", self.negm[:], self.negm_d, w=["negm"], q="pool")
        ld("cp", self.poolw[:], self.pool_w.rearrange("g c d -> c g d"), w=["poolw"], q="pool")
        self.cp(self.identB[:], self.identF[:], r=["identF"], w=["identB"])
        self.memset(self.PF[:], 0.0, w=["pT", "T1", "T2", "tab", "ybuf"])
        self.memset(self.Vaug[:].rearrange("p a b c -> p (a b c)"), 1.0, w=[("V", c) for c in range(13)])
        self.memset(self.ATT[:, 11776:13312], 1.0, w=[("PTr", c) for c in range(4)])
        self.memset(self.mergedT[:].rearrange("p a b -> p (a b)"), 0.0, w=[("mT", c) for c in range(8)])
        self.memset(self.tmpE[:, 63:64], EPS, w=["epsT"])
        condF = self.vecF[:, 244:260].rearrange("p (k c) -> p k c", c=2)
        self.act(self.scb[:], condF, AF.Silu, r=["vecF"], w=["scb"])
        self.mod_late_done = False
        self.mod_feat((0, 1, 2, 3))
        self.mod_G(1)
        cvv = self.cv.rearrange("h k d -> k h d")
        ckk = self.ck.rearrange("h k d -> k h d")
        for cc in range(4):
            self.ld("cp", self.Vaug[:, 9 + cc, :, 0:64], cvv[cc * 128:(cc + 1) * 128], w=[("V", 9 + cc)], q="pool")

    def mod_feat(self, units):
        wm = self.w_mod.rearrange("(kc p) c -> p kc c", p=128)
        for u in units:
            s = self.unit([(lambda R: R[:, :, :], wm[:, :, u * 512:(u + 1) * 512])])
            b = self.bank("main")
            for cc in range(4):
                for kc in range(8):
                    self.mm(self.ps[b][:, cc * 2:cc * 2 + 2], self.R[s][:, kc, cc * 128:(cc + 1) * 128],
                            self.scb[:, kc, :], kc == 0, kc == 7, r=[("R", s), "scb"], w=[("ps", b)])
            for cond in range(2):
                pv = self.ps[b][:, 0:8].rearrange("p (c k) -> p c k", k=2)[:, :, cond]
                self.tt(self.modF[:, cond, 4 * u:4 * u + 4], pv, self.vecF[:, 16 + 4 * u:16 + 4 * u + 4], ALU.add,
                        r=[("ps", b), "vecF"], w=["modF"])

    def mod_G(self, which):
        for cond in range(2):
            if which == 1:
                self.ts(self.G1F[:, cond, :], self.modF[:, cond, 8:16], 1.0, None, ALU.add, None, r=["modF"], w=["G1F"])
                self.tt(self.G1F[:, cond, :], self.G1F[:, cond, :], self.vecF[:, 0:8], ALU.mult, r=["G1F", "vecF"], w=["G1F"])
            else:
                self.ts(self.G2F[:, cond, :], self.modF[:, cond, 32:40], 1.0, None, ALU.add, None, r=["modF"], w=["G2F"])
                self.tt(self.G2F[:, cond, :], self.G2F[:, cond, :], self.vecF[:, 8:16], ALU.mult, r=["G2F", "vecF"], w=["G2F"])

    def k_transposes(self, src, m, col0, srckey, kchunk, dst="KT", dst_ap=None, wkeys=None):
        b = self.bank("tr")
        pb = self.psb[b][:, 0:512].rearrange("p (j c) -> p j c", j=4)
        for j in range(4):
            self.tr(pb[:, j, 0:m], src[0:m, j * 128:(j + 1) * 128], self.identB[0:m, 0:m],
                    r=[srckey, "identB"], w=[("ps", b)])
        tgt = self.KT if dst == "KT" else (self.QT if dst == "QT" else self.naT)
        if dst_ap is None:
            dst_ap = tgt[:, :, col0:col0 + m]
        self.act(dst_ap, pb[:, :, 0:m], AF.Identity, r=[("ps", b)], w=wkeys or [(dst, kchunk)])

    def mod_rows(self, cond, only):
        wm = self.w_mod.rearrange("(kc p) c -> p kc c", p=128)
        for which, u0, dst, key in ((0, 4, self.GA, "GA"), (1, 10, self.GF, "GF")):
            if which != only:
                continue
            if self.row_cond[which] == cond:
                continue
            first = self.row_cond[which] is None
            self.row_cond[which] = cond
            mrow2 = self.sq[0:2, 2, :]
            brow2 = self.kf[0:2, 3, :]
            for uu in range(2):
                o_ = which * 1024 + uu * 512
                if first:
                    u = u0 + uu
                    s = self.unit([(lambda R: R[:, :, :], wm[:, :, u * 512:(u + 1) * 512])])
                    b = self.bank("main")
                    for kc in range(8):
                        self.mm(self.ps[b][0:2, :], self.scb[:, kc, 0:2], self.R[s][:, kc, :],
                                kc == 0, kc == 7, r=[("R", s), "scb"], w=[("ps", b)])
                    self.ld("kf3", brow2, self.bgagf_d[:, o_:o_ + 512], w=[("kf", 3)])
                    self.tt(mrow2, self.ps[b][0:2, :], brow2, ALU.add, r=[("ps", b), ("kf", 3)], w=[("sq", 2)])
                    oc_ = 1 - cond
                    self.P.dma("sp", "mscr", lambda e, o=self.mscr[0:1, o_:o_ + 512], i=self.sq[oc_:oc_ + 1, 2, :]: e.dma_start(out=o, in_=i),
                               r=[("sq", 2)], w=[("mscr", which, uu)])
                    src_row = self.sq[cond:cond + 1, 2, :]
                    if cond != 0:
                        raise NotImplementedError
                else:
                    self.P.dma("sp", "mscr", lambda e, o=self.sq[0:1, 2, :], i=self.mscr[0:1, o_:o_ + 512]: e.dma_start(out=o, in_=i),
                               r=[("mscr", which, uu)], w=[("sq", 2)])
                    src_row = self.sq[0:1, 2, :]
                b2 = self.bank("main")
                self.mm(self.ps[b2][:, :], self.ones1[:, :], src_row, True, True, r=["ones1", ("sq", 2)], w=[("ps", b2)])
                self.cp(dst[:, uu * 512:(uu + 1) * 512], self.ps[b2][:, :], r=[("ps", b2)], w=[key])

    def rstd_from_ssq(self, ssq_ap, out_ap, n, m, width, keys):
        self.act(out_ap, ssq_ap, AF.Sqrt, r=keys + ["epsT"], w=keys, scale=1.0 / n, bias=self.tmpE[0:m, 63:64])
        self.recip(out_ap, out_ap, r=keys, w=keys)

    def pipeline(self, n, stages, lag=1):
        ns = len(stages)
        for t in range(n + (ns - 1) * lag):
            for si, f in enumerate(stages):
                c = t - si * lag
                if 0 <= c < n:
                    f(c)

    def schedule(self, items):
        T = max(st + len(fs) for st, fs in items)
        for t in range(T):
            for st, fs in items:
                k = t - st
                if 0 <= k < len(fs):
                    fs[k]()

    def norm_A(self, x_ap, xkey, m, slot, inplace):
        xkeys = xkey if isinstance(xkey, list) else [xkey]
        si = self.stat(1)
        st = self.small[0:m, si:si + 1]
        skey = ("small", si)
        xh = self.xh[0:m, slot, :]
        hkey = ("xh", slot)
        self.act(xh, x_ap, AF.Square, r=xkeys, w=[skey, hkey], accum=st)
        self.rstd_from_ssq(st, st, float(D), m, 1, [skey])
        if inplace:
            self.ts(xh, xh, st, None, ALU.mult, None, r=[skey, hkey], w=[hkey])
        else:
            self.ts(xh, x_ap, st, None, ALU.mult, None, r=[skey] + xkeys, w=[hkey])

    def norm_B(self, m, slot, GF_, SF_, dcol, dkeys):
        xh = self.xh[0:m, slot, :]
        hkey = ("xh", slot)
        for f4 in range(2):
            b = self.bank("tr")
            pb = self.psb[b][:, 0:512].rearrange("p (j c) -> p j c", j=4)
            for q in range(4):
                fc = 4 * f4 + q
                self.tr(pb[:, q, 0:m], xh[:, fc * 128:(fc + 1) * 128], self.identB[0:m, 0:m],
                        r=[hkey, "identB"], w=[("ps", b)])
            for q in range(4):
                fc = 4 * f4 + q
                if fc % 2 == 0:
                    self.act(self.hT[:, fc, dcol:dcol + m], pb[:, q, 0:m], AF.Identity, r=[("ps", b), "G1F", "G2F", "modF"],
                             w=dkeys, scale=GF_[:, fc:fc + 1], bias=SF_[:, fc:fc + 1])
                else:
                    self.ts(self.hT[:, fc, dcol:dcol + m], pb[:, q, 0:m], GF_[:, fc:fc + 1], SF_[:, fc:fc + 1],
                            ALU.mult, ALU.add, r=[("ps", b), "G1F", "G2F", "modF"], w=dkeys)

    def norm_to_T(self, x_ap, xkey, m, slot, GF_, SF_, dstT, dcol, dkey, inplace):
        si = self.stat(1)
        st = self.small[0:m, si:si + 1]
        skey = ("small", si)
        self.act(self.junk[0:m, :], x_ap, AF.Square, r=[xkey], w=["junk", skey], accum=st)
        self.rstd_from_ssq(st, st, float(D), m, 1, [skey])
        xh = self.xh[0:m, slot, :]
        hkey = ("xh", slot)
        if inplace:
            self.ts(xh, xh, st, None, ALU.mult, None, r=[skey, hkey], w=[hkey])
        else:
            self.ts(xh, x_ap, st, None, ALU.mult, None, r=[skey, xkey], w=[hkey])
        for fc in range(8):
            b = self.bank("tr")
            self.tr(self.ps[b][:, 0:m], xh[:, fc * 128:(fc + 1) * 128], self.identF[0:m, 0:m],
                    r=[hkey, "identF"], w=[("ps", b)])
            if fc % 2 == 0:
                self.act(dstT[:, fc, dcol:dcol + m], self.ps[b][:, 0:m], AF.Identity, r=[("ps", b), "G1F", "G2F", "modF"],
                         w=[dkey], scale=GF_[:, fc:fc + 1], bias=SF_[:, fc:fc + 1])
            else:
                self.ts(dstT[:, fc, dcol:dcol + m], self.ps[b][:, 0:m], GF_[:, fc:fc + 1], SF_[:, fc:fc + 1],
                        ALU.mult, ALU.add, r=[("ps", b), "G1F", "G2F", "modF"], w=[dkey])

    def qk_A(self, s, m, hcol, hkeys, slot3):
        b = self.bank("main")
        for kc in range(8):
            self.mm(self.ps[b][0:m, :], self.hT[:, kc, hcol:hcol + m], self.R[s][:, kc, :], kc == 0, kc == 7,
                    r=[("R", s)] + hkeys, w=[("ps", b)])
        pk = self.ps[b][0:m, :]
        si = self.stat(8)
        st = self.small[0:m, si:si + 8]
        skey = ("small", si)
        sq = self.sq[0:m, slot3, :]
        self.act(sq, pk, AF.Square, r=[("ps", b)], w=[("sq", slot3)])
        self.P.op("dve", lambda e, o=st, i=sq.rearrange("p (h d) -> p h d", h=8): e.tensor_reduce(out=o, in_=i, axis=AX.X, op=ALU.add),
                  r=[("sq", slot3)], w=[skey])
        self.rstd_from_ssq(st, st, 64.0, m, 8, [skey])
        return dict(b=b, si=si, slot3=slot3, m=m)

    def qk_B(self, cx, gcol, kslot, out_dram=None, fslot=0, kbuf=None, kname="kb"):
        b, si, slot3, m = cx["b"], cx["si"], cx["slot3"], cx["m"]
        kbuf = self.kb if kbuf is None else kbuf
        pk = self.ps[b][0:m, :]
        st = self.small[0:m, si:si + 8]
        skey = ("small", si)
        pk3 = pk.rearrange("p (h d) -> p h d", h=8)
        sq3 = self.sq[0:m, slot3, :].rearrange("p (h d) -> p h d", h=8)
        self.tt(sq3, pk3, st.unsqueeze(2).broadcast_to([m, 8, 64]), ALU.mult, r=[("ps", b), skey], w=[("sq", slot3)])
        gb = self.gqk[0:m, gcol:gcol + 64].unsqueeze(1).broadcast_to([m, 8, 64])
        kb = kbuf[0:m, kslot, :]
        if out_dram is not None:
            kf = self.kf[0:m, fslot, :]
            self.tt(kf.rearrange("p (h d) -> p h d", h=8), sq3, gb, ALU.mult, r=[("sq", slot3), "gqk"], w=[("kf", fslot)])
            self.st("kf%d" % fslot, out_dram, kf.rearrange("p (h d) -> p h d", h=8), r=[("kf", fslot)])
            self.act(kb, kf, AF.Identity, r=[("kf", fslot)], w=[(kname, kslot)])
        else:
            self.tt(kb.rearrange("p (h d) -> p h d", h=8), sq3, gb, ALU.mult, r=[("sq", slot3), "gqk"], w=[(kname, kslot)])
        return kb

    def qk_block(self, s, m, hcol, hkeys, gcol, slot, out_dram=None):
        b = self.bank("main")
        for kc in range(8):
            self.mm(self.ps[b][0:m, :], self.hT[:, kc, hcol:hcol + m], self.R[s][:, kc, :], kc == 0, kc == 7,
                    r=[("R", s)] + hkeys, w=[("ps", b)])
        pk = self.ps[b][0:m, :]
        sq = self.sq[0:m, slot, :]
        self.act(sq, pk, AF.Square, r=[("ps", b)], w=[("sq", slot)])
        if DEBUG_LVL <= 0:
            return None
        si = self.stat(8)
        st = self.small[0:m, si:si + 8]
        skey = ("small", si)
        self.P.op("dve", lambda e, o=st, i=sq.rearrange("p (h d) -> p h d", h=8): e.tensor_reduce(out=o, in_=i, axis=AX.X, op=ALU.add),
                  r=[("sq", slot)], w=[skey])
        if DEBUG_LVL <= 1:
            return None
        self.rstd_from_ssq(st, st, 64.0, m, 8, [skey])
        if DEBUG_LVL <= 2:
            return None
        pk3 = pk.rearrange("p (h d) -> p h d", h=8)
        sq3 = sq.rearrange("p (h d) -> p h d", h=8)
        self.tt(sq3, pk3, st.unsqueeze(2).broadcast_to([m, 8, 64]), ALU.mult, r=[("ps", b), skey], w=[("sq", slot)])
        if DEBUG_LVL <= 3:
            return None
        gb = self.gqk[0:m, gcol:gcol + 64].unsqueeze(1).broadcast_to([m, 8, 64])
        kb = self.kb[0:m, slot, :]
        if out_dram is not None:
            kf = self.kf[0:m, slot, :]
            self.tt(kf.rearrange("p (h d) -> p h d", h=8), sq3, gb, ALU.mult, r=[("sq", slot), "gqk"], w=[("kf", slot)])
            if DEBUG_LVL <= 4:
                return None
            self.st("kf%d" % slot, out_dram, kf.rearrange("p (h d) -> p h d", h=8), r=[("kf", slot)])
            if DEBUG_LVL <= 5:
                return None
            self.act(kb, kf, AF.Identity, r=[("kf", slot)], w=[("kb", slot)])
        else:
            self.tt(kb.rearrange("p (h d) -> p h d", h=8), sq3, gb, ALU.mult, r=[("sq", slot), "gqk"], w=[("kb", slot)])
        return kb

    def group(self, G):
        P = self.P
        kind, cond = G["kind"], G["cond"]
        win = self.w_in.rearrange("(kc p) c -> p kc c", p=128)
        if kind == "p":
            g = G["g"]
            nkv = 4
            XOFF = 0
            NX = 512
            xch = [(i * 128, 128) for i in range(4)]
            own = [0, 1, 2, 3]
            ntiles = [dict(x0=0, n=256, ch=[0, 1], kv=[0, 1], ctx=[], local=False),
                      dict(x0=256, n=256, ch=[2, 3], kv=[2, 3], ctx=[], local=False)]
            mtiles = [(0, 512)]
        else:
            nkv = 9
            XOFF = 256
            NX = 640
            xch = [(0, 64), (64, 128), (192, 128), (320, 128), (448, 128), (576, 64)]
            own = [1, 2, 3, 4]
            ntiles = [dict(x0=0, n=320, ch=[0, 1, 2], kv=list(range(0, 7)), ctx=[9, 10, 11, 12], local=True, r0=0, lo=63, hi=320),
                      dict(x0=320, n=320, ch=[3, 4, 5], kv=list(range(2, 9)), ctx=[9, 10, 11, 12], local=True, r0=5, lo=0, hi=257)]
            mtiles = [(63, 257), (320, 257)]
        G1 = self.G1F[:, cond, :]
        S1 = self.modF[:, cond, 0:8]
        G2 = self.G2F[:, cond, :]
        S2 = self.modF[:, cond, 24:32]
        hkey = lambda c0, n: [("hT", c) for c in range(c0 // 128, (c0 + n - 1) // 128 + 1)]

        if kind == "s":
            for cc in range(4):
                self.k_transposes(self.na[:, cc, :], 128, 1152 + cc * 128, ("na", cc), 9 + cc)
        if G.get("pre_units") is not None:
            s_k, s_v, s_q = G["pre_units"]
        else:
            s_k = self.unit([(lambda R: R[:, :, :], win[:, :, 1024:1536])])
            s_v = self.unit([(lambda R: R[:, :, :], win[:, :, 1536:2048])])
            s_q = self.unit([(lambda R: R[:, :, :], win[:, :, 512:1024])])
        s_p = self.unit([(lambda R: R[:, :, :], win[:, :, 0:512])])
        kctx = {}

        NPF = 4 if G.get("prefetched", False) else 0

        def N1(c):
            slot = c % 2
            if kind == "p":
                self.ld("x%d" % c, self.xres[:, c, :], self.xp[g * 512 + c * 128: g * 512 + (c + 1) * 128, :], w=[("x", c)])
                if c >= NPF:
                    self.norm_A(self.xres[:, c, :], ("x", c), 128, slot, False)
            else:
                xl = self.kf[:, 2 * slot:2 * slot + 2, :].rearrange("p a c -> p (a c)")
                if c >= NPF:
                    self.ld("kf%d" % (2 * slot), xl, self.xs[c * 128:(c + 1) * 128, :], w=[("kf", 2 * slot), ("kf", 2 * slot + 1)])
                    self.norm_A(xl, [("kf", 2 * slot), ("kf", 2 * slot + 1)], 128, slot, False)

        def N2(c):
            self.norm_B(128, c % 2, G1, S1, c * 128, [("hT", c)])

        def K1(c):
            kctx[c] = self.qk_A(s_k, 128, c * 128, [("hT", c)], c % 3)

        def V1(c):
            b = self.bank("main")
            for kc in range(8):
                self.mm(self.ps[b][:, :], self.hT[:, kc, c * 128:(c + 1) * 128], self.R[s_v][:, kc, :], kc == 0, kc == 7,
                        r=[("R", s_v), ("hT", c)], w=[("ps", b)])
            pv3 = self.ps[b][:, :].rearrange("p (h d) -> p h d", h=8)
            self.cp(self.Vaug[:, c, :, 0:64], pv3, r=[("ps", b)], w=[("V", c)])
            if kind == "p":
                fs = 2 + c % 2
                bl = 2 * g + c // 2
                od = self.nv[bl].rearrange("h s d -> s h d")[(c % 2) * 128:(c % 2) * 128 + 128]
                self.cp(self.kf[:, fs, :], self.ps[b][:, :], r=[("ps", b)], w=[("kf", fs)])
                self.st("kf%d" % fs, od, self.kf[:, fs, :].rearrange("p (h d) -> p h d", h=8), r=[("kf", fs)])

        def K2(c):
            od = None
            if kind == "p":
                bl = 2 * g + c // 2
                od = self.nk[bl].rearrange("h s d -> s h d")[(c % 2) * 128:(c % 2) * 128 + 128]
            self.qk_B(kctx[c], 64, c % 2, od, c % 2)

        def K3(c):
            self.k_transposes(self.kb[:, c % 2, :], 128, c * 128, ("kb", c % 2), c)

        qctx = {}

        def Q1(i):
            xc, m = xch[i]
            qctx[i] = self.qk_A(s_q, m, XOFF + xc, hkey(XOFF + xc, m), i % 3)

        def Q2(i):
            self.qk_B(qctx[i], 0, i % 2, None, kbuf=self.kbq, kname="kbq")

        def Q3(i):
            xc, m = xch[i]
            bq = self.bank("tr")
            pb = self.psb[bq][:, 0:512].rearrange("p (j c) -> p j c", j=4)
            for j in range(4):
                self.tr(pb[:, j, 0:m], self.kbq[0:m, i % 2, j * 128:(j + 1) * 128], self.identB[0:m, 0:m],
                        r=[("kbq", i % 2), "identB"], w=[("ps", bq)])
            if kind == "p":
                bb, blk = i // 2, i % 2
                csl = slice(bb * 256 + blk, bb * 256 + 256, 2)
                wk = [("QT", 2 * bb), ("QT", 2 * bb + 1)]
            else:
                csl = slice(xc, xc + m)
                wk = [("QT", i)]
            self.cp(self.Qz[0:64, 0:8:2, csl], pb[0:64, :, 0:m], r=[("ps", bq)], w=wk)
            self.act(self.Qz[64:128, 1:8:2, csl], pb[64:128, :, 0:m], AF.Identity, r=[("ps", bq)], w=wk)

        items = []
        for c in range(nkv):
            if c < NPF:
                items.append((c, [lambda c=c: (N1(c), K1(c), V1(c)), lambda c=c: K2(c), lambda c=c: K3(c)]))
            else:
                items.append((c - 2 if NPF else c,
                              [lambda c=c: N1(c), lambda c=c: N2(c), lambda c=c: (K1(c), V1(c)), lambda c=c: K2(c), lambda c=c: K3(c)]))
        for i, (xc, m) in enumerate(xch):
            cl = (XOFF + xc + m - 1) // 128
            items.append((cl + (0 if NPF else 2), [lambda i=i: Q1(i), lambda i=i: Q2(i), lambda i=i: Q3(i)]))
        def zero_q():
            wk_ = [("QT", i) for i in range(6)]
            self.memset(self.Qz[64:128, 0:8:2, 0:NX], 0.0, w=wk_)
            self.memset(self.Qz[0:64, 1:8:2, 0:NX], 0.0, w=wk_)
        items.append((1 if kind == "s" else 0, [zero_q]))
        items.sort(key=lambda it: it[0])
        self.schedule(items)
        if kind == "s":
            for i, (xc, m) in enumerate(xch):
                self.ld("x%d" % i, self.xres[0:m, i, :], self.xs[XOFF + xc: XOFF + xc + m, :], w=[("x", i)])
        if self.dbg_stop("S3"):
            return
        chains = self.pool_stage(G, s_p, XOFF, NX, hkey)
        self.attention(G, xch, ntiles, chains)
        for i, (xc, m) in enumerate(xch):
            self.k_transposes(self.na[:, i, :], m, xc, ("na", i), i, dst="naT")
        if self.dbg_stop("S4"):
            return
        self.pool_mix(G)
        if self.dbg_stop("S5"):
            return
        wpp = self.w_pp.rearrange("(g p) c -> p g c", p=128)
        wnp = self.w_np.rearrange("(g p) c -> p g c", p=128)
        for j in range(2):
            s_pn = self.unit([(lambda R: R[:, 0:4, :], wpp[:, :, j * 512:(j + 1) * 512]),
                              (lambda R: R[:, 4:8, :], wnp[:, :, j * 512:(j + 1) * 512])])
            s_gp = self.unit([(lambda R: R[:, :, :], win[:, :, 2048 + j * 512: 2048 + (j + 1) * 512])])
            s_gn = self.unit([(lambda R: R[:, :, :], win[:, :, 3072 + j * 512: 3072 + (j + 1) * 512])])
            for oc4 in range(4):
                oc = 4 * j + oc4
                osl = slice(oc4 * 128, (oc4 + 1) * 128)
                for (x0, n) in mtiles:
                    bA, bB, bC, bD = [self.bank("all8") for _ in range(4)]
                    ti_ = mtiles.index((x0, n))
                    for gg in range(4):
                        self.mm(self.ps[bA][:, 0:n], self.R[s_pn][:, gg, osl], self.pooloutT[:, gg, x0:x0 + n], gg == 0, gg == 3,
                                r=[("R", s_pn), ("pl", gg, ti_)], w=[("ps", bA)])
                    for kc in range(8):
                        self.mm(self.ps[bB][:, 0:n], self.R[s_gp][:, kc, osl], self.hT[:, kc, XOFF + x0:XOFF + x0 + n], kc == 0, kc == 7,
                                r=[("R", s_gp)] + hkey(XOFF + x0, n), w=[("ps", bB)])
                    for gg in range(4):
                        self.mm(self.ps[bC][:, 0:n], self.R[s_pn][:, 4 + gg, osl], self.naT[:, gg, x0:x0 + n], gg == 0, gg == 3,
                                r=[("R", s_pn)] + [("naT", i) for i in range(len(xch))], w=[("ps", bC)])
                    for kc in range(8):
                        self.mm(self.ps[bD][:, 0:n], self.R[s_gn][:, kc, osl], self.hT[:, kc, XOFF + x0:XOFF + x0 + n], kc == 0, kc == 7,
                                r=[("R", s_gn)] + hkey(XOFF + x0, n), w=[("ps", bD)])
                    t1 = self.sq[:, 0, 0:n]
                    t2 = self.sq[:, 1, 0:n]
                    self.act(t1, self.ps[bB][:, 0:n], AF.Sigmoid, r=[("ps", bB)], w=[("sq", 0)])
                    self.act(t2, self.ps[bD][:, 0:n], AF.Sigmoid, r=[("ps", bD)], w=[("sq", 1)])
                    self.tt(t1, self.ps[bA][:, 0:n], t1, ALU.mult, r=[("ps", bA), ("sq", 0)], w=[("sq", 0)])
                    self.tt(t2, self.ps[bC][:, 0:n], t2, ALU.mult, r=[("ps", bC), ("sq", 1)], w=[("sq", 1)])
                    self.tt(self.mergedT[:, oc, x0:x0 + n], t1, t2, ALU.add, r=[("sq", 0), ("sq", 1)], w=[("mT", oc)])
        if not self.mod_late_done:
            self.mod_late_done = True
            self.mod_feat((6, 7, 8, 9))
            self.mod_G(2)
        self.mod_rows(cond, 0)
        wo = self.w_o.rearrange("(kc p) c -> p kc c", p=128)
        s_o = [self.unit([(lambda R: R[:, :, :], wo[:, :, j * 512:(j + 1) * 512])]) for j in range(2)]
        h2T = self.hT
        tcnt = [0]

        def O1(i):
            xc, m = xch[i]
            for j in range(2):
                b = self.bank("main")
                for kc in range(8):
                    self.mm(self.ps[b][0:m, :], self.mergedT[:, kc, xc:xc + m], self.R[s_o[j]][:, kc, :], kc == 0, kc == 7,
                            r=[("R", s_o[j]), ("mT", kc)], w=[("ps", b)])
                sl = tcnt[0] % 3
                tcnt[0] += 1
                tmp = self.sq[0:m, sl, :]
                self.tt(tmp, self.ps[b][0:m, :], self.GA[0:m, j * 512:(j + 1) * 512], ALU.mult, r=[("ps", b), "GA"], w=[("sq", sl)])
                xr = self.xres[0:m, i, j * 512:(j + 1) * 512]
                self.tt(xr, xr, tmp, ALU.add, r=[("x", i), ("sq", sl)], w=[("x", i)])

        def N1b(i):
            xc, m = xch[i]
            self.norm_A(self.xres[0:m, i, :], ("x", i), m, i % 2, False)

        def N2b(i):
            xc, m = xch[i]
            self.norm_B(m, i % 2, G2, S2, xc, [("hT", c) for c in range(xc // 128, (xc + m - 1) // 128 + 1)])

        self.pipeline(len(xch), [O1, N1b, N2b])
        if kind == "s":
            for col, hv in ((63, 0), (576, 1)):
                ap = self.hT[:, :, col:col + 1]
                self.ts(ap, ap, self.hval[:, hv:hv + 1], None, ALU.mult, None, r=[("hT", col // 128), "hval"], w=[("hT", col // 128)])
        if kind == "p" and G["g"] == 1:
            ckk = self.ck.rearrange("h k d -> k h d")
            for cc in range(4):
                kc_t = self.na[:, cc, :].rearrange("p (h d) -> p h d", h=8)
                self.ld("cp", kc_t, ckk[cc * 128:(cc + 1) * 128], w=[("na", cc)], q="pool")
        actkeys = [("actT", i) for i in range(22)]
        P.alias([("ta", 0), ("ta", 1), ("tg", 0), ("tg", 1)], ["pT", "T1", "T2"])
        P.alias(actkeys, [("KT", c) for c in range(13)] + [("QT", c) for c in range(6)] + [("PTr", c) for c in range(4)])
        if kind == "p":
            ftiles = [dict(c0=0, n=512, segs=[(0, 256), (256, 256)], o0=0, no=512, oc0=0)]
        else:
            ftiles = [dict(c0=63, n=258, segs=[(0, 258)], o0=1, no=256, oc0=0),
                      dict(c0=319, n=258, segs=[(0, 258)], o0=1, no=256, oc0=256)]
        wup = self.w_up.rearrange("(kc p) c -> p kc c", p=128)
        cv = self.vecF[:, 68:244].rearrange("p (c k) -> p c k", k=4)
        tslot = 0
        for i2 in range(11):
            s = self.unit([(lambda R: R[:, :, 0:256], wup[:, :, i2 * 256:(i2 + 1) * 256]),
                           (lambda R: R[:, :, 256:512], wup[:, :, DFF + i2 * 256: DFF + (i2 + 1) * 256])])
            for ii in range(2):
                hi = 2 * i2 + ii
                for ft in ftiles:
                    c0, n = ft["c0"], ft["n"]
                    hk = hkey(c0, n)
                    bA = self.bank("main")
                    bG = self.bank("main")
                    for kc in range(8):
                        self.mm(self.ps[bA][:, 0:n], self.R[s][:, kc, ii * 128:(ii + 1) * 128], h2T[:, kc, c0:c0 + n], kc == 0, kc == 7,
                                r=[("R", s)] + hk, w=[("ps", bA)])
                    for kc in range(8):
                        self.mm(self.ps[bG][:, 0:n], self.R[s][:, kc, 256 + ii * 128: 256 + (ii + 1) * 128], h2T[:, kc, c0:c0 + n], kc == 0, kc == 7,
                                r=[("R", s)] + hk, w=[("ps", bG)])
                    ts_ = tslot % 2
                    tslot += 1
                    ta = self.tab[:, ts_, 0:n]
                    tg = self.tab[:, 2 + ts_, 0:n]
                    for (t_, bb, ch, tk) in ((ta, bA, hi, ("ta", ts_)), (tg, bG, 22 + hi, ("tg", ts_))):
                        pa = self.ps[bb][:, 0:n]
                        self.act(t_, pa, AF.Identity, r=[("ps", bb), "vecF"], w=[tk], scale=cv[:, ch, 1:2], bias=cv[:, ch, 3:4])
                        if len(ft["segs"]) == 2:
                            L = 256
                            p3 = pa.rearrange("p (s l) -> p s l", s=2)
                            t3 = t_.rearrange("p (s l) -> p s l", s=2)
                            self.stt(t3[:, :, 1:L], p3[:, :, 0:L - 1], cv[:, ch, 0:1], t3[:, :, 1:L], ALU.mult, ALU.add,
                                     r=[("ps", bb), tk, "vecF"], w=[tk])
                            self.stt(t3[:, :, 0:L - 1], p3[:, :, 1:L], cv[:, ch, 2:3], t3[:, :, 0:L - 1], ALU.mult, ALU.add,
                                     r=[("ps", bb), tk, "vecF"], w=[tk])
                        else:
                            self.stt(t_[:, 1:n], pa[:, 0:n - 1], cv[:, ch, 0:1], t_[:, 1:n], ALU.mult, ALU.add,
                                     r=[("ps", bb), tk, "vecF"], w=[tk])
                            self.stt(t_[:, 0:n - 1], pa[:, 1:n], cv[:, ch, 2:3], t_[:, 0:n - 1], ALU.mult, ALU.add,
                                     r=[("ps", bb), tk, "vecF"], w=[tk])
                    self.act(ta, ta, AF.Silu, r=[("ta", ts_)], w=[("ta", ts_)])
                    o0, no, oc0 = ft["o0"], ft["no"], ft["oc0"]
                    self.tt(self.actT[:, hi, oc0:oc0 + no], ta[:, o0:o0 + no], tg[:, o0:o0 + no], ALU.mult,
                            r=[("ta", ts_), ("tg", ts_)], w=[("actT", hi)])
        self.mod_rows(cond, 1)
        wd = self.w_down.rearrange("(kc p) c -> p kc c", p=128)
        ykeys = ["ybuf%d" % i for i in range(6)]
        P.alias(ykeys, ["pT", "T1", "T2", ("ta", 0), ("ta", 1), ("tg", 0), ("tg", 1)])
        yslot = 0
        f2i = [0]
        pfs = list(G.get("prefetch_steps", []))
        if pfs:
            pfs.pop(0)()
        for j in range(2):
            banks = [self.bank("main") for _ in own]
            for u in range(3):
                nk_ = 8 if u < 2 else 6
                f2i[0] += 1
                su = self.unit([(lambda R, nk_=nk_: R[:, 0:nk_, :], wd[:, 8 * u:8 * u + nk_, j * 512:(j + 1) * 512])])
                for oi in range(len(own)):
                    b = banks[oi]
                    for kk in range(nk_):
                        k = 8 * u + kk
                        self.mm(self.ps[b][:, :], self.actT[:, k, oi * 128:(oi + 1) * 128], self.R[su][:, kk, :], k == 0, k == 21,
                                r=[("R", su), ("actT", k)], w=[("ps", b)], inc=(kk == nk_ - 1))
                if pfs:
                    pfs.pop(0)()
            for oi, xi in enumerate(own):
                b = banks[oi]
                ys = yslot % 6
                yslot += 1
                yb = self.ybuf[:, ys, :]
                self.tt(yb, self.ps[b][:, :], self.GF[:, j * 512:(j + 1) * 512], ALU.mult, r=[("ps", b), "GF"], w=["ybuf%d" % ys])
                self.tt(yb, yb, self.xres[:, xi, j * 512:(j + 1) * 512], ALU.add, r=["ybuf%d" % ys, ("x", xi)], w=["ybuf%d" % ys])
                if kind == "p":
                    dst = self.yp[G["g"] * 512 + oi * 128: G["g"] * 512 + (oi + 1) * 128, j * 512:(j + 1) * 512]
                else:
                    dst = self.ys[oi * 128:(oi + 1) * 128, j * 512:(j + 1) * 512]
                self.st("yb%d" % ys, dst, yb, r=["ybuf%d" % ys])
        if G.get("post_f2") is not None:
            G["post_f2"]()
        P.alias([("KT", c) for c in range(9)] + [("QT", c) for c in range(6)] + [("PTr", c) for c in range(4)], actkeys)
        P.alias(["pT", "T1", "T2", ("ta", 0), ("ta", 1), ("tg", 0), ("tg", 1)], ykeys + [("ta", 0), ("ta", 1), ("tg", 0), ("tg", 1)])

    def _norm2(self, i, xc, m, G2, S2):
        keys = [("hT", c) for c in range(xc // 128, (xc + m - 1) // 128 + 1)]
        x_ap = self.xres[0:m, i, :]
        si = self.stat(1)
        st = self.small[0:m, si:si + 1]
        skey = ("small", si)
        slot = i % 2
        self.act(self.junk[0:m, :], x_ap, AF.Square, r=[("x", i)], w=["junk", skey], accum=st)
        self.rstd_from_ssq(st, st, float(D), m, 1, [skey])
        xh = self.xh[0:m, slot, :]
        hk = ("xh", slot)
        self.ts(xh, x_ap, st, None, ALU.mult, None, r=[skey, ("x", i)], w=[hk])
        for fc in range(8):
            b = self.bank("tr")
            self.tr(self.ps[b][:, 0:m], xh[:, fc * 128:(fc + 1) * 128], self.identF[0:m, 0:m], r=[hk, "identF"], w=[("ps", b)])
            if fc % 2 == 0:
                self.act(self.hT[:, fc, xc:xc + m], self.ps[b][:, 0:m], AF.Identity, r=[("ps", b), "G2F", "modF"], w=keys,
                         scale=G2[:, fc:fc + 1], bias=S2[:, fc:fc + 1])
            else:
                self.ts(self.hT[:, fc, xc:xc + m], self.ps[b][:, 0:m], G2[:, fc:fc + 1], S2[:, fc:fc + 1], ALU.mult, ALU.add,
                        r=[("ps", b), "G2F", "modF"], w=keys)

    def attention(self, G, xch, ntiles, inter=()):
        kind = G["kind"]
        inter = list(inter)
        units, flat = [], []
        for h in range(8):
            for ti, T in enumerate(ntiles):
                chunks = [(c, True) for c in T["kv"]] if T["local"] else [(c, False) for c in T["kv"]]
                chunks += [(c, False) for c in T["ctx"]]
                u = dict(h=h, T=T, chunks=chunks, nck=len(chunks), first=True, bO=None, last_tile=(ti == len(ntiles) - 1))
                units.append(u)
                for ci in range(len(chunks)):
                    flat.append((u, ci))
        eb_loaded = set()
        slots = {}

        def emit_S(k):
            u, ci = flat[k]
            h, T = u["h"], u["T"]
            j = h // 2
            x0, n = T["x0"], T["n"]
            qkeys = [("QT", i) for i in T["ch"]]
            c, loc = u["chunks"][ci]
            if kind == "s" and h not in eb_loaded:
                eb_loaded.add(h)
                es_ = h % 2
                self.ld("EB%d" % es_, self.EB[:, es_, 4 * 64:22 * 64], self.t3_d[h][:, 4 * 64:22 * 64], w=[("EB", es_)], q="pool")
                self.act(self.EB[:, es_, 4 * 64:22 * 64], self.EB[:, es_, 4 * 64:22 * 64], AF.Exp, r=[("EB", es_)], w=[("EB", es_)])
            b = self.bank("S")
            lo, hi = T.get("lo", 0), T.get("hi", n)
            if loc and c == 8:
                lo = 256
            self.mm(self.ps[b][:, lo:hi], self.KT[:, j, c * 128:(c + 1) * 128], self.Qz[:, h, x0 + lo:x0 + hi],
                    True, not loc, r=[("KT", c)] + qkeys, w=[("ps", b)], skip=loc)
            if loc:
                r0 = T["r0"]
                rhs = self.negm[:, c * 10 + r0: c * 10 + r0 + 5].unsqueeze(2).broadcast_to([128, 5, 64])
                self.mm(self.ps[b][:, 0:n], self.half[:, :], rhs, False, True, r=["half", "negm"], w=[("ps", b)], skip=True)
            slot = self.pt_i % 4
            self.pt_i += 1
            pt = self.PTr[:, slot, lo:hi]
            self.act(pt, self.ps[b][:, lo:hi], AF.Exp, r=[("ps", b)], w=[("PTr", slot)], scale=0.125)
            if loc:
                e0 = T["r0"] - 2 * c + 16
                es_ = h % 2
                self.tt(pt, pt, self.EB[:, es_, e0 * 64 + lo:e0 * 64 + hi], ALU.mult, r=[("PTr", slot), ("EB", es_)], w=[("PTr", slot)])
            slots[k] = slot

        def emit_PV(k):
            u, ci = flat[k]
            h, T = u["h"], u["T"]
            x0 = T["x0"]
            c, loc = u["chunks"][ci]
            slot = slots.pop(k)
            if u["bO"] is None:
                u["bO"] = self.bank("O")
            bO = u["bO"]
            order = sorted(range(len(T["ch"])), key=lambda q: -xch[T["ch"][q]][1])
            if loc and c == 8:
                order = [q for q in order if T["ch"][q] == 5]
            for oi_, qi in enumerate(order):
                xc, m = xch[T["ch"][qi]]
                last = oi_ == len(order) - 1
                self.mm(self.ps[bO][:, qi * 65:(qi + 1) * 65], self.PTr[:, slot, xc - x0: xc - x0 + 128], self.Vaug[:, c, h, :],
                        u["first"], (ci == u["nck"] - 1),
                        r=[("PTr", slot), ("V", c)], w=[("ps", bO)], inc=last, skip=True)
                u["first"] = False
            if ci == u["nck"] - 1:
                nch = len(T["ch"])
                ch0 = T["ch"][0]
                si = self.stat(nch)
                rc = self.small[:, si:si + nch].unsqueeze(2)
                O3 = self.ps[bO][:, 0:nch * 65].rearrange("p (c d) -> p c d", d=65)
                self.recip(rc, O3[:, :, 64:65], r=[("ps", bO)], w=[("small", si)])
                self.tt(self.na[:, ch0:ch0 + nch, h * 64:(h + 1) * 64], O3[:, :, 0:64], rc.broadcast_to([128, nch, 64]), ALU.mult,
                        r=[("ps", bO), ("small", si)], w=[("na", xi) for xi in T["ch"]])

        LA = 3
        nf = len(flat)
        rate = len(inter) / max(1.0, nf - 6.0)
        acc = 0.0
        for k in range(min(LA, nf)):
            emit_S(k)
        for k in range(nf):
            if k + LA < nf:
                emit_S(k + LA)
            emit_PV(k)
            acc += rate
            while acc >= 1.0 and inter:
                inter.pop(0)()
                acc -= 1.0
        while inter:
            inter.pop(0)()

    def pool_stage(self, G, s, XOFF, NX, hkey):
        kind = G["kind"]
        if kind == "p":
            nseg, L, LP = 2, 256, 272
            tabL, tabR = 0, 1
        else:
            nseg, L, LP = 1, 640, 656
            tabL, tabR = 2, 3
        pT4 = self.PF[:, 0:4 * nseg * LP].rearrange("p (g s c) -> p g s c", g=4, s=nseg)
        if kind == "p":
            for g in range(4):
                self.memset(pT4[:, g, :, 0:8], 0.0, w=["pT"])
                self.memset(pT4[:, g, :, 8 + L:LP], 0.0, w=["pT"])
        for g in range(4):
            if kind == "p":
                b = self.bank("main")
                for kc in range(8):
                    self.mm(self.ps[b][:, :], self.R[s][:, kc, g * 128:(g + 1) * 128], self.hT[:, kc, 0:512], kc == 0, kc == 7,
                            r=[("R", s)] + hkey(0, 512), w=[("ps", b)])
                self.act(pT4[:, g, :, 8:8 + L], self.ps[b][:, :].rearrange("p (s l) -> p s l", s=2), AF.Identity, r=[("ps", b)], w=["pT"])
            else:
                for nt in range(2):
                    b = self.bank("main")
                    c0 = XOFF - 8 + nt * 328
                    for kc in range(8):
                        self.mm(self.ps[b][:, 0:328], self.R[s][:, kc, g * 128:(g + 1) * 128], self.hT[:, kc, c0:c0 + 328], kc == 0, kc == 7,
                                r=[("R", s)] + hkey(c0, 328), w=[("ps", b)])
                    self.tt(pT4[:, g, 0, nt * 328:(nt + 1) * 328], self.ps[b][:, 0:328], self.pmask[:, nt * 328:(nt + 1) * 328], ALU.mult,
                            r=[("ps", b), "pmask"], w=["pT"])
        T1 = self.PF[:, 2624:2624 + nseg * LP].rearrange("p (s c) -> p s c", s=nseg)
        T2 = self.PF[:, 3280:3280 + nseg * LP].rearrange("p (s c) -> p s c", s=nseg)
        tabv = self.ptab[:, :].rearrange("p (t g d) -> p t g d", t=4, g=4)

        from functools import partial
        ops = []

        def chain(g):
            p = pT4[:, g]
            w = POOL_W[g]
            ops.append(partial(self.tt, T1[:, :, 1:LP], p[:, :, 0:LP - 1], p[:, :, 1:LP], ALU.add, r=["pT"], w=["T1"]))
            W = T1
            wk = "T1"
            if g >= 1:
                ops.append(partial(self.tt, T2[:, :, 2:LP - 1], T1[:, :, 1:LP - 2], T1[:, :, 3:LP], ALU.add, r=["T1"], w=["T2"]))
                W, wk = T2, "T2"
            if g >= 2:
                ops.append(partial(self.tt, T1[:, :, 4:LP - 3], T2[:, :, 2:LP - 5], T2[:, :, 6:LP - 1], ALU.add, r=["T2"], w=["T1"]))
                W, wk = T1, "T1"
            if g >= 3:
                ops.append(partial(self.tt, T2[:, :, 8:LP - 7], T1[:, :, 4:LP - 11], T1[:, :, 12:LP - 3], ALU.add, r=["T1"], w=["T2"]))
                W, wk = T2, "T2"
            po = self.pooledT[:, g, 0:nseg * L].rearrange("p (s l) -> p s l", s=nseg)
            ops.append(partial(self.stt, po, W[:, :, 8:8 + L], 1.0 / w, p[:, :, 8:8 + L], ALU.mult, ALU.subtract,
                               r=[wk, "pT"], w=[("pl", g, 0), ("pl", g, 1)]))
            if kind == "p":
                edges = [(0, tabL), (L - 8, tabR)]
            else:
                edges = [(64, tabL), (568, tabR)]
            for (lc, tb) in edges:
                te = self.tmpE[:, 0:nseg * 8].rearrange("p (s d) -> p s d", s=nseg)
                tv = tabv[:, tb, g, :].unsqueeze(1).broadcast_to([128, nseg, 8])
                ops.append(partial(self.tt, te, W[:, :, 8 + lc:8 + lc + 8], tv, ALU.mult, r=[wk, "ptab"], w=["tmpE"]))
                ops.append(partial(self.tt, po[:, :, lc:lc + 8], te, p[:, :, 8 + lc:8 + lc + 8], ALU.subtract,
                                   r=["tmpE", "pT"], w=[("pl", g, 0), ("pl", g, 1)]))

        for g in range(4):
            chain(g)
        return ops

    def pool_mix(self, G):
        kind = G["kind"]
        mt = [(0, 512)] if kind == "p" else [(63, 257), (320, 257)]
        for g in range(4):
            for ti, (x0, n) in enumerate(mt):
                b = self.bank("main")
                self.mm(self.ps[b][:, 0:n], self.poolw[:, g, :], self.pooledT[:, g, x0:x0 + n], True, True, r=["poolw", ("pl", g, ti)], w=[("ps", b)])
                self.act(self.pooloutT[:, g, x0:x0 + n], self.ps[b][:, 0:n], AF.Identity, r=[("ps", b), "vecF"], w=[("pl", g, ti)],
                         scale=self.vecF[:, 64 + g:65 + g])


Builder.pt_i = 0


def _feat(v):
    v = np.asarray(v, np.float32)
    return np.ascontiguousarray(v.reshape(-1, 128).T)


def _geometry(j):
    R0 = 8 * j
    negm = np.full((2, 9, 10), NEG, np.float32)
    for c in range(9):
        for hf in range(2):
            ak = R0 - 5 + 2 * c + hf
            for r in range(10):
                aq = R0 - 1 + r
                if not (0 <= aq < 32) or not (0 <= ak < 32):
                    continue
                st = min(max(aq - 4, 0), 24)
                if st <= ak < st + 8:
                    negm[hf, c, r] = 0.0
    ptab = np.zeros((4, 4, 8), np.float32)
    for g, w in enumerate(POOL_W):
        for d in range(8):
            ptab[0, g, d] = 1.0 / (w // 2 + min(d, w // 2))
            dd = 7 - d
            ptab[1, g, d] = 1.0 / (w // 2 + min(w // 2, dd + 1))
            ptab[2, g, d] = ptab[0, g, d] if j == 0 else 1.0 / w
            ptab[3, g, d] = ptab[1, g, d] if j == 3 else 1.0 / w
    tok = (R0 - 5) * 64 + 248 + np.arange(656)
    pmask = ((tok >= 0) & (tok < 2048)).astype(np.float32)
    hval = np.array([1.0 if j > 0 else 0.0, 1.0 if j < 3 else 0.0], np.float32)
    return negm, ptab, pmask, hval


def _t3(rpb):
    t3 = np.full((8, 2, 64, NE, 64), NEG, np.float32)
    kc = np.arange(64)[:, None]
    qc = np.arange(64)[None, :]
    ws = np.clip(qc - 8, 0, 48)
    colv = (kc >= ws) & (kc < ws + 16)
    dc = np.clip(kc - qc + 15, 0, 30)
    for hf in range(2):
        for e in range(NE):
            i = hf + 19 - e
            if 0 <= i <= 14:
                vals = rpb[:, i][:, dc]
                t3[:, hf, :, e, :] = np.where(colv[None], vals, NEG)
    return np.ascontiguousarray(t3.reshape(8, 128, NE * 64))


_NC_CACHE = {}


def make_in_maps(x_prompt, x_sample, cache_k, cache_v, c, c_ctx, norm_mix_g, norm_ffn_g, w_mod, b_mod,
                 w_in, q_norm_g, k_norm_g, pool_w, pool_scale, na_rpb, w_pool_proj, w_na_proj, w_o,
                 w_up, ffn_conv_w, ffn_conv_b, w_down):
    f = lambda a: np.ascontiguousarray(np.asarray(a, np.float32))
    x_prompt, x_sample, cache_k, cache_v = f(x_prompt), f(x_sample), f(cache_k), f(cache_v)
    c, c_ctx = f(c), f(c_ctx)
    shared = dict(w_mod=f(w_mod[0]), w_in=f(w_in[0]), pool_w=f(pool_w[0]), w_pp=f(w_pool_proj[0]), w_np=f(w_na_proj[0]),
                  w_o=f(w_o[0]), w_up=f(w_up[0]), w_down=f(w_down[0]))
    convF = np.stack([_feat(ffn_conv_w[0, 0]), _feat(ffn_conv_w[0, 1]), _feat(ffn_conv_w[0, 2]), _feat(ffn_conv_b[0])], axis=2)
    gqk = np.ascontiguousarray(np.broadcast_to(np.concatenate([f(q_norm_g[0]), f(k_norm_g[0])])[None, :], (128, 128)))
    bm = f(b_mod[0])
    bgagf = np.ascontiguousarray(np.broadcast_to(np.concatenate([bm[2048:3072], bm[5120:6144]])[None, :], (2, 2048)))
    ident = np.eye(128, dtype=np.float32)
    half = np.zeros((128, 128), np.float32)
    half[0:2] = np.repeat(np.eye(2, dtype=np.float32), 64, axis=1)
    t3 = _t3(f(na_rpb[0]))
    ones1 = np.ones((1, 128), np.float32)
    in_maps = []
    for i in range(8):
        b, j = i // 4, i % 4
        R0 = 8 * j
        lo, hi = (R0 - 5) * 64, (R0 + 13) * 64
        xs = np.zeros((1152, D), np.float32)
        a, e = max(lo, 0), min(hi, 2048)
        xs[a - lo:e - lo] = x_sample[b, a:e]
        negm, ptab, pmask, hval = _geometry(j)
        condF = np.stack([_feat(c_ctx), _feat(c[b])], axis=2)
        vecF = np.concatenate([_feat(norm_mix_g[0]), _feat(norm_ffn_g[0]), _feat(bm), _feat(pool_scale[0]),
                               convF.reshape(128, 176), condF.reshape(128, 16)], axis=1)
        m = dict(shared)
        m.update(xp=np.ascontiguousarray(x_prompt[4 * i:4 * i + 4].reshape(1024, D)), xs=xs,
                 ck=cache_k[b, 0], cv=cache_v[b, 0], vecF=np.ascontiguousarray(vecF, dtype=np.float32), gqk=gqk, bgagf=bgagf,
                 ident=ident, half=half, t3=t3, negm=np.ascontiguousarray(np.concatenate([negm.reshape(2, 90), np.zeros((126, 90), np.float32)], axis=0)),
                 ptab=np.ascontiguousarray(np.broadcast_to(ptab.reshape(1, 128), (128, 128))),
                 pmask=np.ascontiguousarray(np.broadcast_to(pmask[None, :], (128, 656))),
                 hval=np.ascontiguousarray(np.broadcast_to(hval[None, :], (128, 2))), ones1=ones1)
        in_maps.append(m)
    return in_maps


def kernel(**inputs):
    in_maps = make_in_maps(**inputs)
    if "nc" not in _NC_CACHE:
        _NC_CACHE["nc"] = Builder().build()
    nc = _NC_CACHE["nc"]
    res = run_bass_kernel_spmd(nc, in_maps, core_ids=list(range(8)))
    rs = res.results
    y_prompt = np.concatenate([r["yp"].reshape(4, 256, D) for r in rs], axis=0)
    y_sample = np.stack([np.concatenate([rs[4 * b + j]["ys"] for j in range(4)], axis=0) for b in range(2)], axis=0)
    new_k = np.concatenate([r["nk"] for r in rs], axis=0)[:, None]
    new_v = np.concatenate([r["nv"] for r in rs], axis=0)[:, None]
    return (y_prompt.astype(np.float32), y_sample.astype(np.float32),
            np.ascontiguousarray(new_k, dtype=np.float32), np.ascontiguousarray(new_v, dtype=np.float32))
```

```python
import numpy as np
from contextlib import ExitStack
import concourse.bass as bass
import concourse.mybir as mybir
from concourse.bass_utils import run_bass_kernel_spmd

F32 = mybir.dt.float32
BF16 = mybir.dt.bfloat16
AF = mybir.ActivationFunctionType
ALU = mybir.AluOpType
AX = mybir.AxisListType

D = 1024
DFF = 2816
NEG = -30000.0
EPS = 1e-6
NRING = 5
NE = 26
POOL_W = (2, 4, 8, 16)
DEBUG_STOP = None
DEBUG_LVL = 99


class Prog:
    ENG = ("pe", "act", "dve", "pool", "sp")

    def __init__(self, nc, es):
        self.nc = nc
        self.es = es
        self.ops = {e: [] for e in self.ENG}
        self.cnt = {e: 0 for e in self.ENG}
        self.sem = {e: es.enter_context(nc.semaphore("s_" + e)) for e in self.ENG}
        self.res = {}
        self.seen = {e: {} for e in self.ENG}
        self.dsem = {}
        self.dcnt = {}

    def _deps(self, eng, r, w):
        deps = {}

        def add(d):
            if d is None:
                return
            s, v = d
            if s == "pe" and eng == "pe":
                return
            if deps.get(s, 0) < v:
                deps[s] = v

        for k in r:
            st = self.res.get(k)
            if st:
                add(st["w"])
        for k in w:
            st = self.res.get(k)
            if st:
                add(st["w"])
                for d in st["r"]:
                    add(d)
        waits = []
        for s, v in deps.items():
            if self.seen[eng].get(s, 0) >= v:
                continue
            self.seen[eng][s] = v
            waits.append((s, v))
        return waits

    def _mark(self, tok, r, w):
        for k in r:
            st = self.res.setdefault(k, {"w": None, "r": []})
            if len(st["r"]) > 64:
                best = {}
                for s, v in st["r"]:
                    if best.get(s, 0) < v:
                        best[s] = v
                st["r"] = list(best.items())
            st["r"].append(tok)
        for k in w:
            self.res[k] = {"w": tok, "r": []}

    def op(self, eng, fn, r=(), w=(), inc=True):
        waits = self._deps(eng, r, w)
        val = self.cnt[eng] + 1
        if inc:
            self.cnt[eng] = val
        self.ops[eng].append((waits, fn, ("e", eng) if inc else None))
        self._mark((eng, val), r, w)

    def dma(self, q, skey, fn, r=(), w=()):
        if skey not in self.dsem:
            self.dsem[skey] = self.es.enter_context(self.nc.semaphore("d_" + skey))
            self.dcnt[skey] = 0
        waits = self._deps(q, r, w)
        self.dcnt[skey] += 16
        self.ops[q].append((waits, fn, ("d", skey)))
        self._mark(("d:" + skey, self.dcnt[skey]), r, w)

    def alias(self, newkeys, oldkeys):
        acc = []
        for k in oldkeys:
            st = self.res.get(k)
            if st:
                if st["w"]:
                    acc.append(st["w"])
                acc.extend(st["r"])
        for k in newkeys:
            st = self.res.setdefault(k, {"w": None, "r": []})
            st["r"].extend(acc)

    def _semof(self, s):
        if s.startswith("d:"):
            return self.dsem[s[2:]]
        return self.sem[s]

    def finish_waits(self, eng):
        waits = [("d:" + k, c) for k, c in self.dcnt.items()]
        self.ops[eng].append((waits, None, None))

    def emit(self):
        nc = self.nc
        hw = {"pe": "tensor", "act": "scalar", "dve": "vector", "pool": "gpsimd", "sp": "sync"}
        with nc.Block() as block:
            for e in self.ENG:
                lst = self.ops[e]

                def body(engine, lst=lst):
                    for waits, fn, inc in lst:
                        for s, v in waits:
                            engine.wait_ge(self._semof(s), v)
                        if fn is None:
                            continue
                        ins = fn(engine)
                        if inc is not None:
                            if inc[0] == "e":
                                ins.then_inc(self.sem[inc[1]], 1)
                            else:
                                ins.then_inc(self.dsem[inc[1]], 16)

                getattr(block, hw[e])(body)


class Builder:
    def __init__(self):
        self.nc = bass.Bass("TRN2", target_bir_lowering=False)
        self.es = ExitStack()

    def mm(self, out, lhsT, rhs, start, stop, r, w, inc=None, skip=False):
        inc = stop if inc is None else inc
        self.P.op("pe", lambda e, o=out, l=lhsT, rr=rhs, s=start, t=stop, sk=skip:
                  e.matmul(o, lhsT=l, rhs=rr, start=s, stop=t, skip_group_check=sk), r=r, w=w, inc=inc)

    def tr(self, out, in_, ident, r, w):
        self.P.op("pe", lambda e, o=out, i=in_, d=ident: e.transpose(o, i, d), r=r, w=w)

    def act(self, out, in_, func, r, w, scale=None, bias=None, accum=None):
        kw = {}
        if scale is not None:
            kw["scale"] = scale
        if bias is not None:
            kw["bias"] = bias
        if accum is not None:
            kw["accum_out"] = accum
        self.P.op("act", lambda e, o=out, i=in_, f=func, kw=kw: e.activation(out=o, in_=i, func=f, **kw),
                  r=r, w=w)

    def tt(self, out, a, b, op, r, w, eng="dve"):
        self.P.op(eng, lambda e, o=out, a=a, b=b, op=op: e.tensor_tensor(out=o, in0=a, in1=b, op=op), r=r, w=w)

    def ts(self, out, a, s1, s2, op0, op1, r, w, eng="dve"):
        if s2 is None:
            self.P.op(eng, lambda e, o=out, a=a, s1=s1, op0=op0:
                      e.tensor_scalar(out=o, in0=a, scalar1=s1, scalar2=None, op0=op0), r=r, w=w)
        else:
            self.P.op(eng, lambda e, o=out, a=a, s1=s1, s2=s2, op0=op0, op1=op1:
                      e.tensor_scalar(out=o, in0=a, scalar1=s1, scalar2=s2, op0=op0, op1=op1), r=r, w=w)

    def stt(self, out, a, s, b, op0, op1, r, w):
        self.P.op("dve", lambda e, o=out, a=a, s=s, b=b, op0=op0, op1=op1:
                  e.scalar_tensor_tensor(out=o, in0=a, scalar=s, in1=b, op0=op0, op1=op1), r=r, w=w)

    def cp(self, out, in_, r, w, eng="dve"):
        self.P.op(eng, lambda e, o=out, i=in_: e.tensor_copy(out=o, in_=i), r=r, w=w)

    def recip(self, out, in_, r, w):
        self.P.op("dve", lambda e, o=out, i=in_: e.reciprocal(out=o, in_=i), r=r, w=w)

    def memset(self, ap, val, w, eng="dve"):
        self.P.op(eng, lambda e, a=ap, v=val: e.memset(a, v), w=w)

    def ld(self, skey, out, in_, w, r=(), q="sp"):
        if skey in ("c", "cp"):
            self._cn = getattr(self, "_cn", 0) + 1
            skey = "%s%d" % (skey, self._cn)
        self.P.dma(q, skey, lambda e, o=out, i=in_: e.dma_start(out=o, in_=i), r=r, w=w)

    def st(self, skey, out, in_, r, q="sp"):
        self.P.dma(q, skey, lambda e, o=out, i=in_: e.dma_start(out=o, in_=i), r=r, w=())

    def bank(self, pool):
        lst, idx = self.pools[pool]
        b = lst[idx % len(lst)]
        self.pools[pool][1] = idx + 1
        return b

    def unit(self, loads, slot=None):
        if slot is None:
            s = self.uidx % NRING
            self.uidx += 1
        else:
            s = slot
        R = self.R[s]
        for dfn, src in loads:
            self.ld("R%d" % s, dfn(R), src, w=[("R", s)], q="pool")
        return s

    def sb(self, name, shape, dt):
        return self.es.enter_context(self.nc.sbuf_tensor("sb_" + name, shape, dt))

    def build(self):
        nc, es = self.nc, self.es
        self.P = P = Prog(nc, es)
        din = lambda n, s: nc.dram_tensor(n, s, F32, kind="ExternalInput").ap()
        dout = lambda n, s: nc.dram_tensor(n, s, F32, kind="ExternalOutput").ap()
        self.xp = din("xp", [1024, D])
        self.xs = din("xs", [1152, D])
        self.ck = din("ck", [8, 512, 64])
        self.cv = din("cv", [8, 512, 64])
        self.w_mod = din("w_mod", [D, 6 * D])
        self.w_in = din("w_in", [D, 4096])
        self.pool_w = din("pool_w", [4, 128, 128])
        self.w_pp = din("w_pp", [512, D])
        self.w_np = din("w_np", [512, D])
        self.w_o = din("w_o", [D, D])
        self.w_up = din("w_up", [D, 2 * DFF])
        self.w_down = din("w_down", [DFF, D])
        self.vecF_d = din("vecF", [128, 260])
        self.gqk_d = din("gqk", [128, 128])
        self.bgagf_d = din("bgagf", [2, 2048])
        self.mscr = nc.dram_tensor("mscr", [1, 2048], F32, kind="Internal").ap()
        self.ident_d = din("ident", [128, 128])
        self.half_d = din("half", [128, 128])
        self.t3_d = din("t3", [8, 128, NE * 64])
        self.negm_d = din("negm", [128, 90])
        self.ptab_d = din("ptab", [128, 128])
        self.pmask_d = din("pmask", [128, 656])
        self.hval_d = din("hval", [128, 2])
        self.ones_d = din("ones1", [1, 128])
        self.yp = dout("yp", [1024, D])
        self.ys = dout("ys", [512, D])
        self.nk = dout("nk", [4, 8, 256, 64])
        self.nv = dout("nv", [4, 8, 256, 64])

        sb = self.sb
        self.R = [sb("ring%d" % i, [128, 8, 512], BF16) for i in range(NRING)]
        self.xres = sb("xres", [128, 6, D], F32)
        self.xh = sb("xh", [128, 2, D], BF16)
        self.kbq = sb("kbq", [128, 2, 512], BF16)
        self.hT = sb("hT", [128, 8, 1152], BF16)
        self.ATT = sb("ATT", [128, 13312], BF16)
        self.KT = self.ATT[:, 0:6656].rearrange("p (j c) -> p j c", j=4)
        self.Qz = self.ATT[:, 6656:11776].rearrange("p (h c) -> p h c", h=8)
        self.PTr = self.ATT[:, 11776:13312].rearrange("p (j c) -> p j c", j=4)
        self.actT = self.ATT[:, 0:11264].rearrange("p (j c) -> p j c", j=22)
        self.Vaug = sb("Vaug", [128, 13, 8, 65], BF16)
        self.sq = sb("sq", [128, 3, 512], F32)
        self.kf = sb("kf", [128, 4, 512], F32)
        self.kb = sb("kb", [128, 2, 512], BF16)
        self.EB = sb("EB", [128, 2, NE * 64], BF16)
        self.na = sb("na", [128, 6, 512], BF16)
        self.naT = sb("naT", [128, 4, 640], BF16)
        self.PF = sb("PF", [128, 3936], F32)
        self.pT = self.PF[:, 0:2624].rearrange("p (g c) -> p g c", g=4)
        self.T1 = self.PF[:, 2624:3280]
        self.T2 = self.PF[:, 3280:3936]
        self.tab = self.PF[:, 0:2048].rearrange("p (s c) -> p s c", s=4)
        self.ybuf = self.PF[:, 0:3072].rearrange("p (s c) -> p s c", s=6)
        self.pmask = sb("pmask", [128, 656], BF16)
        self.pooledT = sb("pooledT", [128, 4, 640], BF16)
        self.pooloutT = self.pooledT
        self.mergedT = sb("mergedT", [128, 8, 640], BF16)
        self.GA = sb("GA", [128, D], F32)
        self.GF = sb("GF", [128, D], F32)
        self.identF = sb("identF", [128, 128], F32)
        self.identB = sb("identB", [128, 128], BF16)
        self.vecF = sb("vecF", [128, 260], F32)
        self.gqk = sb("gqk", [128, 128], F32)
        self.scb = sb("scb", [128, 8, 2], BF16)
        self.modF = sb("modF", [128, 2, 48], F32)
        self.G1F = sb("G1F", [128, 2, 8], F32)
        self.G2F = sb("G2F", [128, 2, 8], F32)
        self.poolw = sb("poolw", [128, 4, 128], BF16)
        self.half = sb("halfb", [128, 128], BF16)
        self.negm = sb("negmb", [128, 90], BF16)
        self.ptab = sb("ptab", [128, 128], F32)
        self.hval = sb("hval", [128, 2], F32)
        self.ones1 = sb("ones1s", [1, 128], F32)
        self.mrow = self.sq[0:1, 2, :]
        self.brow = self.kf[0:1, 3, :]
        self.small = sb("small", [128, 64], F32)
        self.tmpE = sb("tmpE", [128, 64], F32)
        self.ps = [es.enter_context(nc.psum_tensor("ps%d" % i, [128, 512], F32)) for i in range(8)]
        self.psb = [p.bitcast(BF16) for p in self.ps]
        self.pools = {"main": [[0, 1, 2, 3, 4, 5], 0], "tr": [[6, 7], 0], "S": [[0, 1, 2, 3], 0],
                      "O": [[4, 5], 0], "all8": [[0, 1, 2, 3, 4, 5, 6, 7], 0]}
        self.uidx = 0
        self.small_i = 0

        self.row_cond = [None, None]
        self.prologue()
        def pf_steps(kind, cond):
            G1n = self.G1F[:, cond, :]
            S1n = self.modF[:, cond, 0:8]

            def land(c):
                if kind == "p":
                    ls = 4 + c % 2
                    return self.xres[:, ls, :], [("x", ls)], "x%d" % ls, self.xp[512 + c * 128: 512 + (c + 1) * 128, :]
                sl = c % 2
                xl = self.kf[:, 2 * sl:2 * sl + 2, :].rearrange("p a c -> p (a c)")
                return xl, [("kf", 2 * sl), ("kf", 2 * sl + 1)], "kf%d" % (2 * sl), self.xs[c * 128:(c + 1) * 128, :]

            def load(c):
                ap, keys, sem, src = land(c)
                self.ld(sem, ap, src, w=keys)

            def n1(c):
                ap, keys, sem, src = land(c)
                self.norm_A(ap, keys, 128, c % 2, False)

            def n2(c):
                self.norm_B(128, c % 2, G1n, S1n, c * 128, [("hT", c)])

            return [lambda: (load(0), load(1), n1(0)),
                    lambda: (n1(1), n2(0), load(2)),
                    lambda: (n1(2), n2(1), load(3)),
                    lambda: (n1(3), n2(2)),
                    lambda: n2(3)]

        def pre_units():
            win = self.w_in.rearrange("(kc p) c -> p kc c", p=128)
            return (self.unit([(lambda R: R[:, :, :], win[:, :, 1024:1536])]),
                    self.unit([(lambda R: R[:, :, :], win[:, :, 1536:2048])]),
                    self.unit([(lambda R: R[:, :, :], win[:, :, 512:1024])]))

        G0 = dict(kind="p", g=0, cond=0)
        G1 = dict(kind="p", g=1, cond=0, prefetched=True)
        G2 = dict(kind="s", cond=1, prefetched=True)
        G0["prefetch_steps"] = pf_steps("p", 0)
        G1["prefetch_steps"] = pf_steps("s", 1)
        G0["post_f2"] = lambda: G1.__setitem__("pre_units", pre_units())
        G1["post_f2"] = lambda: G2.__setitem__("pre_units", pre_units())
        self.group(G0)
        self.group(G1)
        self.group(G2)
        P.finish_waits("sp")
        P.emit()
        return nc

    def dbg_stop(self, tag):
        if DEBUG_STOP != tag:
            return False
        dbg = self.nc.dram_tensor("dbg", [4, 128, D], F32, kind="ExternalOutput").ap()
        for i in range(4):
            self.st("dbg", dbg[i], self.xres[:, i, :], r=[("x", i)])
        return True

    def stat(self, n=1):
        i = self.small_i
        if i + n > 64:
            i = 0
        self.small_i = i + n
        return i

    def prologue(self):
        P = self.P
        ld = self.ld
        ld("c", self.vecF[:], self.vecF_d, w=["vecF"])
        ld("c", self.identF[:], self.ident_d, w=["identF"])
        ld("c", self.gqk[:], self.gqk_d, w=["gqk"])
        ld("c", self.ptab[:], self.ptab_d, w=["ptab"])
        ld("cp", self.pmask[:], self.pmask_d, w=["pmask"], q="pool")
        ld("c", self.hval[:], self.hval_d, w=["hval"])
        ld("c", self.ones1[:], self.ones_d, w=["ones1"])
        ld("cp", self.half[:], self.half_d, w=["half"], q="pool")
        ld("cp", self.negm[:], self.negm_d, w=["negm"], q="pool")
        ld("cp", self.poolw[:], self.pool_w.rearrange("g c d -> c g d"), w=["poolw"], q="pool")
        self.cp(self.identB[:], self.identF[:], r=["identF"], w=["identB"])
        self.memset(self.PF[:], 0.0, w=["pT", "T1", "T2", "tab", "ybuf"])
        self.memset(self.Vaug[:].rearrange("p a b c -> p (a b c)"), 1.0, w=[("V", c) for c in range(13)])
        self.memset(self.ATT[:, 11776:13312], 1.0, w=[("PTr", c) for c in range(4)])
        self.memset(self.mergedT[:].rearrange("p a b -> p (a b)"), 0.0, w=[("mT", c) for c in range(8)])
        self.memset(self.tmpE[:, 63:64], EPS, w=["epsT"])
        condF = self.vecF[:, 244:260].rearrange("p (k c) -> p k c", c=2)
        self.act(self.scb[:], condF, AF.Silu, r=["vecF"], w=["scb"])
        self.mod_late_done = False
        self.mod_feat((0, 1, 2, 3))
        self.mod_G(1)
        cvv = self.cv.rearrange("h k d -> k h d")
        ckk = self.ck.rearrange("h k d -> k h d")
        for cc in range(4):
            self.ld("cp", self.Vaug[:, 9 + cc, :, 0:64], cvv[cc * 128:(cc + 1) * 128], w=[("V", 9 + cc)], q="pool")

    def mod_feat(self, units):
        wm = self.w_mod.rearrange("(kc p) c -> p kc c", p=128)
        for u in units:
            s = self.unit([(lambda R: R[:, :, :], wm[:, :, u * 512:(u + 1) * 512])])
            b = self.bank("main")
            for cc in range(4):
                for kc in range(8):
                    self.mm(self.ps[b][:, cc * 2:cc * 2 + 2], self.R[s][:, kc, cc * 128:(cc + 1) * 128],
                            self.scb[:, kc, :], kc == 0, kc == 7, r=[("R", s), "scb"], w=[("ps", b)])
            for cond in range(2):
                pv = self.ps[b][:, 0:8].rearrange("p (c k) -> p c k", k=2)[:, :, cond]
                self.tt(self.modF[:, cond, 4 * u:4 * u + 4], pv, self.vecF[:, 16 + 4 * u:16 + 4 * u + 4], ALU.add,
                        r=[("ps", b), "vecF"], w=["modF"])

    def mod_G(self, which):
        for cond in range(2):
            if which == 1:
                self.ts(self.G1F[:, cond, :], self.modF[:, cond, 8:16], 1.0, None, ALU.add, None, r=["modF"], w=["G1F"])
                self.tt(self.G1F[:, cond, :], self.G1F[:, cond, :], self.vecF[:, 0:8], ALU.mult, r=["G1F", "vecF"], w=["G1F"])
            else:
                self.ts(self.G2F[:, cond, :], self.modF[:, cond, 32:40], 1.0, None, ALU.add, None, r=["modF"], w=["G2F"])
                self.tt(self.G2F[:, cond, :], self.G2F[:, cond, :], self.vecF[:, 8:16], ALU.mult, r=["G2F", "vecF"], w=["G2F"])

    def k_transposes(self, src, m, col0, srckey, kchunk, dst="KT", dst_ap=None, wkeys=None):
        b = self.bank("tr")
        pb = self.psb[b][:, 0:512].rearrange("p (j c) -> p j c", j=4)
        for j in range(4):
            self.tr(pb[:, j, 0:m], src[0:m, j * 128:(j + 1) * 128], self.identB[0:m, 0:m],
                    r=[srckey, "identB"], w=[("ps", b)])
        tgt = self.KT if dst == "KT" else (self.QT if dst == "QT" else self.naT)
        if dst_ap is None:
            dst_ap = tgt[:, :, col0:col0 + m]
        self.act(dst_ap, pb[:, :, 0:m], AF.Identity, r=[("ps", b)], w=wkeys or [(dst, kchunk)])

    def mod_rows(self, cond, only):
        wm = self.w_mod.rearrange("(kc p) c -> p kc c", p=128)
        for which, u0, dst, key in ((0, 4, self.GA, "GA"), (1, 10, self.GF, "GF")):
            if which != only:
                continue
            if self.row_cond[which] == cond:
                continue
            first = self.row_cond[which] is None
            self.row_cond[which] = cond
            mrow2 = self.sq[0:2, 2, :]
            brow2 = self.kf[0:2, 3, :]
            for uu in range(2):
                o_ = which * 1024 + uu * 512
                if first:
                    u = u0 + uu
                    s = self.unit([(lambda R: R[:, :, :], wm[:, :, u * 512:(u + 1) * 512])])
                    b = self.bank("main")
                    for kc in range(8):
                        self.mm(self.ps[b][0:2, :], self.scb[:, kc, 0:2], self.R[s][:, kc, :],
                                kc == 0, kc == 7, r=[("R", s), "scb"], w=[("ps", b)])
                    self.ld("kf3", brow2, self.bgagf_d[:, o_:o_ + 512], w=[("kf", 3)])
                    self.tt(mrow2, self.ps[b][0:2, :], brow2, ALU.add, r=[("ps", b), ("kf", 3)], w=[("sq", 2)])
                    oc_ = 1 - cond
                    self.P.dma("sp", "mscr", lambda e, o=self.mscr[0:1, o_:o_ + 512], i=self.sq[oc_:oc_ + 1, 2, :]: e.dma_start(out=o, in_=i),
                               r=[("sq", 2)], w=[("mscr", which, uu)])
                    src_row = self.sq[cond:cond + 1, 2, :]
                    if cond != 0:
                        raise NotImplementedError
                else:
                    self.P.dma("sp", "mscr", lambda e, o=self.sq[0:1, 2, :], i=self.mscr[0:1, o_:o_ + 512]: e.dma_start(out=o, in_=i),
                               r=[("mscr", which, uu)], w=[("sq", 2)])
                    src_row = self.sq[0:1, 2, :]
                b2 = self.bank("main")
                self.mm(self.ps[b2][:, :], self.ones1[:, :], src_row, True, True, r=["ones1", ("sq", 2)], w=[("ps", b2)])
                self.cp(dst[:, uu * 512:(uu + 1) * 512], self.ps[b2][:, :], r=[("ps", b2)], w=[key])

    def rstd_from_ssq(self, ssq_ap, out_ap, n, m, width, keys):
        self.act(out_ap, ssq_ap, AF.Sqrt, r=keys + ["epsT"], w=keys, scale=1.0 / n, bias=self.tmpE[0:m, 63:64])
        self.recip(out_ap, out_ap, r=keys, w=keys)

    def pipeline(self, n, stages, lag=1):
        ns = len(stages)
        for t in range(n + (ns - 1) * lag):
            for si, f in enumerate(stages):
                c = t - si * lag
                if 0 <= c < n:
                    f(c)

    def schedule(self, items):
        T = max(st + len(fs) for st, fs in items)
        for t in range(T):
            for st, fs in items:
                k = t - st
                if 0 <= k < len(fs):
                    fs[k]()

    def norm_A(self, x_ap, xkey, m, slot, inplace):
        xkeys = xkey if isinstance(xkey, list) else [xkey]
        si = self.stat(1)
        st = self.small[0:m, si:si + 1]
        skey = ("small", si)
        xh = self.xh[0:m, slot, :]
        hkey = ("xh", slot)
        self.act(xh, x_ap, AF.Square, r=xkeys, w=[skey, hkey], accum=st)
        self.rstd_from_ssq(st, st, float(D), m, 1, [skey])
        if inplace:
            self.ts(xh, xh, st, None, ALU.mult, None, r=[skey, hkey], w=[hkey])
        else:
            self.ts(xh, x_ap, st, None, ALU.mult, None, r=[skey] + xkeys, w=[hkey])

    def norm_B(self, m, slot, GF_, SF_, dcol, dkeys):
        xh = self.xh[0:m, slot, :]
        hkey = ("xh", slot)
        for f4 in range(2):
            b = self.bank("tr")
            pb = self.psb[b][:, 0:512].rearrange("p (j c) -> p j c", j=4)
            for q in range(4):
                fc = 4 * f4 + q
                self.tr(pb[:, q, 0:m], xh[:, fc * 128:(fc + 1) * 128], self.identB[0:m, 0:m],
                        r=[hkey, "identB"], w=[("ps", b)])
            for q in range(4):
                fc = 4 * f4 + q
                if fc % 2 == 0:
                    self.act(self.hT[:, fc, dcol:dcol + m], pb[:, q, 0:m], AF.Identity, r=[("ps", b), "G1F", "G2F", "modF"],
                             w=dkeys, scale=GF_[:, fc:fc + 1], bias=SF_[:, fc:fc + 1])
                else:
                    self.ts(self.hT[:, fc, dcol:dcol + m], pb[:, q, 0:m], GF_[:, fc:fc + 1], SF_[:, fc:fc + 1],
                            ALU.mult, ALU.add, r=[("ps", b), "G1F", "G2F", "modF"], w=dkeys)

    def norm_to_T(self, x_ap, xkey, m, slot, GF_, SF_, dstT, dcol, dkey, inplace):
        si = self.stat(1)
        st = self.small[0:m, si:si + 1]
        skey = ("small", si)
        self.act(self.junk[0:m, :], x_ap, AF.Square, r=[xkey], w=["junk", skey], accum=st)
        self.rstd_from_ssq(st, st, float(D), m, 1, [skey])
        xh = self.xh[0:m, slot, :]
        hkey = ("xh", slot)
        if inplace:
            self.ts(xh, xh, st, None, ALU.mult, None, r=[skey, hkey], w=[hkey])
        else:
            self.ts(xh, x_ap, st, None, ALU.mult, None, r=[skey, xkey], w=[hkey])
        for fc in range(8):
            b = self.bank("tr")
            self.tr(self.ps[b][:, 0:m], xh[:, fc * 128:(fc + 1) * 128], self.identF[0:m, 0:m],
                    r=[hkey, "identF"], w=[("ps", b)])
            if fc % 2 == 0:
                self.act(dstT[:, fc, dcol:dcol + m], self.ps[b][:, 0:m], AF.Identity, r=[("ps", b), "G1F", "G2F", "modF"],
                         w=[dkey], scale=GF_[:, fc:fc + 1], bias=SF_[:, fc:fc + 1])
            else:
                self.ts(dstT[:, fc, dcol:dcol + m], self.ps[b][:, 0:m], GF_[:, fc:fc + 1], SF_[:, fc:fc + 1],
                        ALU.mult, ALU.add, r=[("ps", b), "G1F", "G2F", "modF"], w=[dkey])

    def qk_A(self, s, m, hcol, hkeys, slot3):
        b = self.bank("main")
        for kc in range(8):
            self.mm(self.ps[b][0:m, :], self.hT[:, kc, hcol:hcol + m], self.R[s][:, kc, :], kc == 0, kc == 7,
                    r=[("R", s)] + hkeys, w=[("ps", b)])
        pk = self.ps[b][0:m, :]
        si = self.stat(8)
        st = self.small[0:m, si:si + 8]
        skey = ("small", si)
        sq = self.sq[0:m, slot3, :]
        self.act(sq, pk, AF.Square, r=[("ps", b)], w=[("sq", slot3)])
        self.P.op("dve", lambda e, o=st, i=sq.rearrange("p (h d) -> p h d", h=8): e.tensor_reduce(out=o, in_=i, axis=AX.X, op=ALU.add),
                  r=[("sq", slot3)], w=[skey])
        self.rstd_from_ssq(st, st, 64.0, m, 8, [skey])
        return dict(b=b, si=si, slot3=slot3, m=m)

    def qk_B(self, cx, gcol, kslot, out_dram=None, fslot=0, kbuf=None, kname="kb"):
        b, si, slot3, m = cx["b"], cx["si"], cx["slot3"], cx["m"]
        kbuf = self.kb if kbuf is None else kbuf
        pk = self.ps[b][0:m, :]
        st = self.small[0:m, si:si + 8]
        skey = ("small", si)
        pk3 = pk.rearrange("p (h d) -> p h d", h=8)
        sq3 = self.sq[0:m, slot3, :].rearrange("p (h d) -> p h d", h=8)
        self.tt(sq3, pk3, st.unsqueeze(2).broadcast_to([m, 8, 64]), ALU.mult, r=[("ps", b), skey], w=[("sq", slot3)])
        gb = self.gqk[0:m, gcol:gcol + 64].unsqueeze(1).broadcast_to([m, 8, 64])
        kb = kbuf[0:m, kslot, :]
        if out_dram is not None:
            kf = self.kf[0:m, fslot, :]
            self.tt(kf.rearrange("p (h d) -> p h d", h=8), sq3, gb, ALU.mult, r=[("sq", slot3), "gqk"], w=[("kf", fslot)])
            self.st("kf%d" % fslot, out_dram, kf.rearrange("p (h d) -> p h d", h=8), r=[("kf", fslot)])
            self.act(kb, kf, AF.Identity, r=[("kf", fslot)], w=[(kname, kslot)])
        else:
            self.tt(kb.rearrange("p (h d) -> p h d", h=8), sq3, gb, ALU.mult, r=[("sq", slot3), "gqk"], w=[(kname, kslot)])
        return kb

    def qk_block(self, s, m, hcol, hkeys, gcol, slot, out_dram=None):
        b = self.bank("main")
        for kc in range(8):
            self.mm(self.ps[b][0:m, :], self.hT[:, kc, hcol:hcol + m], self.R[s][:, kc, :], kc == 0, kc == 7,
                    r=[("R", s)] + hkeys, w=[("ps", b)])
        pk = self.ps[b][0:m, :]
        sq = self.sq[0:m, slot, :]
        self.act(sq, pk, AF.Square, r=[("ps", b)], w=[("sq", slot)])
        if DEBUG_LVL <= 0:
            return None
        si = self.stat(8)
        st = self.small[0:m, si:si + 8]
        skey = ("small", si)
        self.P.op("dve", lambda e, o=st, i=sq.rearrange("p (h d) -> p h d", h=8): e.tensor_reduce(out=o, in_=i, axis=AX.X, op=ALU.add),
                  r=[("sq", slot)], w=[skey])
        if DEBUG_LVL <= 1:
            return None
        self.rstd_from_ssq(st, st, 64.0, m, 8, [skey])
        if DEBUG_LVL <= 2:
            return None
        pk3 = pk.rearrange("p (h d) -> p h d", h=8)
        sq3 = sq.rearrange("p (h d) -> p h d", h=8)
        self.tt(sq3, pk3, st.unsqueeze(2).broadcast_to([m, 8, 64]), ALU.mult, r=[("ps", b), skey], w=[("sq", slot)])
        if DEBUG_LVL <= 3:
            return None
        gb = self.gqk[0:m, gcol:gcol + 64].unsqueeze(1).broadcast_to([m, 8, 64])
        kb = self.kb[0:m, slot, :]
        if out_dram is not None:
            kf = self.kf[0:m, slot, :]
            self.tt(kf.rearrange("p (h d) -> p h d", h=8), sq3, gb, ALU.mult, r=[("sq", slot), "gqk"], w=[("kf", slot)])
            if DEBUG_LVL <= 4:
                return None
            self.st("kf%d" % slot, out_dram, kf.rearrange("p (h d) -> p h d", h=8), r=[("kf", slot)])
            if DEBUG_LVL <= 5:
                return None
            self.act(kb, kf, AF.Identity, r=[("kf", slot)], w=[("kb", slot)])
        else:
            self.tt(kb.rearrange("p (h d) -> p h d", h=8), sq3, gb, ALU.mult, r=[("sq", slot), "gqk"], w=[("kb", slot)])
        return kb

    def group(self, G):
        P = self.P
        kind, cond = G["kind"], G["cond"]
        win = self.w_in.rearrange("(kc p) c -> p kc c", p=128)
        if kind == "p":
            g = G["g"]
            nkv = 4
            XOFF = 0
            NX = 512
            xch = [(i * 128, 128) for i in range(4)]
            own = [0, 1, 2, 3]
            ntiles = [dict(x0=0, n=256, ch=[0, 1], kv=[0, 1], ctx=[], local=False),
                      dict(x0=256, n=256, ch=[2, 3], kv=[2, 3], ctx=[], local=False)]
            mtiles = [(0, 512)]
        else:
            nkv = 9
            XOFF = 256
            NX = 640
            xch = [(0, 64), (64, 128), (192, 128), (320, 128), (448, 128), (576, 64)]
            own = [1, 2, 3, 4]
            ntiles = [dict(x0=0, n=320, ch=[0, 1, 2], kv=list(range(0, 7)), ctx=[9, 10, 11, 12], local=True, r0=0, lo=63, hi=320),
                      dict(x0=320, n=320, ch=[3, 4, 5], kv=list(range(2, 9)), ctx=[9, 10, 11, 12], local=True, r0=5, lo=0, hi=257)]
            mtiles = [(63, 257), (320, 257)]
        G1 = self.G1F[:, cond, :]
        S1 = self.modF[:, cond, 0:8]
        G2 = self.G2F[:, cond, :]
        S2 = self.modF[:, cond, 24:32]
        hkey = lambda c0, n: [("hT", c) for c in range(c0 // 128, (c0 + n - 1) // 128 + 1)]

        if kind == "s":
            for cc in range(4):
                self.k_transposes(self.na[:, cc, :], 128, 1152 + cc * 128, ("na", cc), 9 + cc)
        if G.get("pre_units") is not None:
            s_k, s_v, s_q = G["pre_units"]
        else:
            s_k = self.unit([(lambda R: R[:, :, :], win[:, :, 1024:1536])])
            s_v = self.unit([(lambda R: R[:, :, :], win[:, :, 1536:2048])])
            s_q = self.unit([(lambda R: R[:, :, :], win[:, :, 512:1024])])
        s_p = self.unit([(lambda R: R[:, :, :], win[:, :, 0:512])])
        kctx = {}

        NPF = 4 if G.get("prefetched", False) else 0

        def N1(c):
            slot = c % 2
            if kind == "p":
                self.ld("x%d" % c, self.xres[:, c, :], self.xp[g * 512 + c * 128: g * 512 + (c + 1) * 128, :], w=[("x", c)])
                if c >= NPF:
                    self.norm_A(self.xres[:, c, :], ("x", c), 128, slot, False)
            else:
                xl = self.kf[:, 2 * slot:2 * slot + 2, :].rearrange("p a c -> p (a c)")
                if c >= NPF:
                    self.ld("kf%d" % (2 * slot), xl, self.xs[c * 128:(c + 1) * 128, :], w=[("kf", 2 * slot), ("kf", 2 * slot + 1)])
                    self.norm_A(xl, [("kf", 2 * slot), ("kf", 2 * slot + 1)], 128, slot, False)

        def N2(c):
            self.norm_B(128, c % 2, G1, S1, c * 128, [("hT", c)])

        def K1(c):
            kctx[c] = self.qk_A(s_k, 128, c * 128, [("hT", c)], c % 3)

        def V1(c):
            b = self.bank("main")
            for kc in range(8):
                self.mm(self.ps[b][:, :], self.hT[:, kc, c * 128:(c + 1) * 128], self.R[s_v][:, kc, :], kc == 0, kc == 7,
                        r=[("R", s_v), ("hT", c)], w=[("ps", b)])
            pv3 = self.ps[b][:, :].rearrange("p (h d) -> p h d", h=8)
            self.cp(self.Vaug[:, c, :, 0:64], pv3, r=[("ps", b)], w=[("V", c)])
            if kind == "p":
                fs = 2 + c % 2
                bl = 2 * g + c // 2
                od = self.nv[bl].rearrange("h s d -> s h d")[(c % 2) * 128:(c % 2) * 128 + 128]
                self.cp(self.kf[:, fs, :], self.ps[b][:, :], r=[("ps", b)], w=[("kf", fs)])
                self.st("kf%d" % fs, od, self.kf[:, fs, :].rearrange("p (h d) -> p h d", h=8), r=[("kf", fs)])

        def K2(c):
            od = None
            if kind == "p":
                bl = 2 * g + c // 2
                od = self.nk[bl].rearrange("h s d -> s h d")[(c % 2) * 128:(c % 2) * 128 + 128]
            self.qk_B(kctx[c], 64, c % 2, od, c % 2)

        def K3(c):
            self.k_transposes(self.kb[:, c % 2, :], 128, c * 128, ("kb", c % 2), c)

        qctx = {}

        def Q1(i):
            xc, m = xch[i]
            qctx[i] = self.qk_A(s_q, m, XOFF + xc, hkey(XOFF + xc, m), i % 3)

        def Q2(i):
            self.qk_B(qctx[i], 0, i % 2, None, kbuf=self.kbq, kname="kbq")

        def Q3(i):
            xc, m = xch[i]
            bq = self.bank("tr")
            pb = self.psb[bq][:, 0:512].rearrange("p (j c) -> p j c", j=4)
            for j in range(4):
                self.tr(pb[:, j, 0:m], self.kbq[0:m, i % 2, j * 128:(j + 1) * 128], self.identB[0:m, 0:m],
                        r=[("kbq", i % 2), "identB"], w=[("ps", bq)])
            if kind == "p":
                bb, blk = i // 2, i % 2
                csl = slice(bb * 256 + blk, bb * 256 + 256, 2)
                wk = [("QT", 2 * bb), ("QT", 2 * bb + 1)]
            else:
                csl = slice(xc, xc + m)
                wk = [("QT", i)]
            self.cp(self.Qz[0:64, 0:8:2, csl], pb[0:64, :, 0:m], r=[("ps", bq)], w=wk)
            self.act(self.Qz[64:128, 1:8:2, csl], pb[64:128, :, 0:m], AF.Identity, r=[("ps", bq)], w=wk)

        items = []
        for c in range(nkv):
            if c < NPF:
                items.append((c, [lambda c=c: (N1(c), K1(c), V1(c)), lambda c=c: K2(c), lambda c=c: K3(c)]))
            else:
                items.append((c - 2 if NPF else c,
                              [lambda c=c: N1(c), lambda c=c: N2(c), lambda c=c: (K1(c), V1(c)), lambda c=c: K2(c), lambda c=c: K3(c)]))
        for i, (xc, m) in enumerate(xch):
            cl = (XOFF + xc + m - 1) // 128
            items.append((cl + (0 if NPF else 2), [lambda i=i: Q1(i), lambda i=i: Q2(i), lambda i=i: Q3(i)]))
        def zero_q():
            wk_ = [("QT", i) for i in range(6)]
            self.P.op("act", lambda e, a=self.Qz[64:128, 0:8:2, 0:NX]: e.memzero(a), w=wk_)
            self.P.op("act", lambda e, a=self.Qz[0:64, 1:8:2, 0:NX]: e.memzero(a), w=wk_)
        items.append((1 if kind == "s" else 0, [zero_q]))
        items.sort(key=lambda it: it[0])
        self.schedule(items)
        if kind == "s":
            for i, (xc, m) in enumerate(xch):
                self.ld("x%d" % i, self.xres[0:m, i, :], self.xs[XOFF + xc: XOFF + xc + m, :], w=[("x", i)])
        if self.dbg_stop("S3"):
            return
        chains = self.pool_stage(G, s_p, XOFF, NX, hkey)
        self.attention(G, xch, ntiles, chains)
        for i, (xc, m) in enumerate(xch):
            self.k_transposes(self.na[:, i, :], m, xc, ("na", i), i, dst="naT")
        if self.dbg_stop("S4"):
            return
        self.pool_mix(G)
        if self.dbg_stop("S5"):
            return
        wpp = self.w_pp.rearrange("(g p) c -> p g c", p=128)
        wnp = self.w_np.rearrange("(g p) c -> p g c", p=128)
        for j in range(2):
            s_pn = self.unit([(lambda R: R[:, 0:4, :], wpp[:, :, j * 512:(j + 1) * 512]),
                              (lambda R: R[:, 4:8, :], wnp[:, :, j * 512:(j + 1) * 512])])
            s_gp = self.unit([(lambda R: R[:, :, :], win[:, :, 2048 + j * 512: 2048 + (j + 1) * 512])])
            s_gn = self.unit([(lambda R: R[:, :, :], win[:, :, 3072 + j * 512: 3072 + (j + 1) * 512])])
            for oc4 in range(4):
                oc = 4 * j + oc4
                osl = slice(oc4 * 128, (oc4 + 1) * 128)
                for (x0, n) in mtiles:
                    bA, bB, bC, bD = [self.bank("all8") for _ in range(4)]
                    ti_ = mtiles.index((x0, n))
                    for gg in range(4):
                        self.mm(self.ps[bA][:, 0:n], self.R[s_pn][:, gg, osl], self.pooloutT[:, gg, x0:x0 + n], gg == 0, gg == 3,
                                r=[("R", s_pn), ("pl", gg, ti_)], w=[("ps", bA)])
                    for kc in range(8):
                        self.mm(self.ps[bB][:, 0:n], self.R[s_gp][:, kc, osl], self.hT[:, kc, XOFF + x0:XOFF + x0 + n], kc == 0, kc == 7,
                                r=[("R", s_gp)] + hkey(XOFF + x0, n), w=[("ps", bB)])
                    for gg in range(4):
                        self.mm(self.ps[bC][:, 0:n], self.R[s_pn][:, 4 + gg, osl], self.naT[:, gg, x0:x0 + n], gg == 0, gg == 3,
                                r=[("R", s_pn)] + [("naT", i) for i in range(len(xch))], w=[("ps", bC)])
                    for kc in range(8):
                        self.mm(self.ps[bD][:, 0:n], self.R[s_gn][:, kc, osl], self.hT[:, kc, XOFF + x0:XOFF + x0 + n], kc == 0, kc == 7,
                                r=[("R", s_gn)] + hkey(XOFF + x0, n), w=[("ps", bD)])
                    t1 = self.sq[:, 0, 0:n]
                    t2 = self.sq[:, 1, 0:n]
                    self.act(t1, self.ps[bB][:, 0:n], AF.Sigmoid, r=[("ps", bB)], w=[("sq", 0)])
                    self.act(t2, self.ps[bD][:, 0:n], AF.Sigmoid, r=[("ps", bD)], w=[("sq", 1)])
                    self.tt(t1, self.ps[bA][:, 0:n], t1, ALU.mult, r=[("ps", bA), ("sq", 0)], w=[("sq", 0)])
                    self.tt(t2, self.ps[bC][:, 0:n], t2, ALU.mult, r=[("ps", bC), ("sq", 1)], w=[("sq", 1)])
                    self.tt(self.mergedT[:, oc, x0:x0 + n], t1, t2, ALU.add, r=[("sq", 0), ("sq", 1)], w=[("mT", oc)])
        if not self.mod_late_done:
            self.mod_late_done = True
            self.mod_feat((6, 7, 8, 9))
            self.mod_G(2)
        self.mod_rows(cond, 0)
        wo = self.w_o.rearrange("(kc p) c -> p kc c", p=128)
        s_o = [self.unit([(lambda R: R[:, :, :], wo[:, :, j * 512:(j + 1) * 512])]) for j in range(2)]
        h2T = self.hT
        tcnt = [0]

        def O1(i):
            xc, m = xch[i]
            for j in range(2):
                b = self.bank("main")
                for kc in range(8):
                    self.mm(self.ps[b][0:m, :], self.mergedT[:, kc, xc:xc + m], self.R[s_o[j]][:, kc, :], kc == 0, kc == 7,
                            r=[("R", s_o[j]), ("mT", kc)], w=[("ps", b)])
                sl = tcnt[0] % 3
                tcnt[0] += 1
                tmp = self.sq[0:m, sl, :]
                self.tt(tmp, self.ps[b][0:m, :], self.GA[0:m, j * 512:(j + 1) * 512], ALU.mult, r=[("ps", b), "GA"], w=[("sq", sl)])
                xr = self.xres[0:m, i, j * 512:(j + 1) * 512]
                self.tt(xr, xr, tmp, ALU.add, r=[("x", i), ("sq", sl)], w=[("x", i)])

        def N1b(i):
            xc, m = xch[i]
            self.norm_A(self.xres[0:m, i, :], ("x", i), m, i % 2, False)

        def N2b(i):
            xc, m = xch[i]
            self.norm_B(m, i % 2, G2, S2, xc, [("hT", c) for c in range(xc // 128, (xc + m - 1) // 128 + 1)])

        self.pipeline(len(xch), [O1, N1b, N2b])
        if kind == "s":
            for col, hv in ((63, 0), (576, 1)):
                ap = self.hT[:, :, col:col + 1]
                self.ts(ap, ap, self.hval[:, hv:hv + 1], None, ALU.mult, None, r=[("hT", col // 128), "hval"], w=[("hT", col // 128)])
        if kind == "p" and G["g"] == 1:
            ckk = self.ck.rearrange("h k d -> k h d")
            for cc in range(4):
                kc_t = self.na[:, cc, :].rearrange("p (h d) -> p h d", h=8)
                self.ld("cp", kc_t, ckk[cc * 128:(cc + 1) * 128], w=[("na", cc)], q="pool")
        actkeys = [("actT", i) for i in range(22)]
        P.alias([("ta", 0), ("ta", 1), ("tg", 0), ("tg", 1)], ["pT", "T1", "T2"])
        P.alias(actkeys, [("KT", c) for c in range(13)] + [("QT", c) for c in range(6)] + [("PTr", c) for c in range(4)])
        if kind == "p":
            ftiles = [dict(c0=0, n=512, segs=[(0, 256), (256, 256)], o0=0, no=512, oc0=0)]
        else:
            ftiles = [dict(c0=63, n=258, segs=[(0, 258)], o0=1, no=256, oc0=0),
                      dict(c0=319, n=258, segs=[(0, 258)], o0=1, no=256, oc0=256)]
        wup = self.w_up.rearrange("(kc p) c -> p kc c", p=128)
        cv = self.vecF[:, 68:244].rearrange("p (c k) -> p c k", k=4)
        tslot = 0
        for i2 in range(11):
            s = self.unit([(lambda R: R[:, :, 0:256], wup[:, :, i2 * 256:(i2 + 1) * 256]),
                           (lambda R: R[:, :, 256:512], wup[:, :, DFF + i2 * 256: DFF + (i2 + 1) * 256])])
            for ii in range(2):
                hi = 2 * i2 + ii
                for ft in ftiles:
                    c0, n = ft["c0"], ft["n"]
                    hk = hkey(c0, n)
                    bA = self.bank("main")
                    bG = self.bank("main")
                    for kc in range(8):
                        self.mm(self.ps[bA][:, 0:n], self.R[s][:, kc, ii * 128:(ii + 1) * 128], h2T[:, kc, c0:c0 + n], kc == 0, kc == 7,
                                r=[("R", s)] + hk, w=[("ps", bA)])
                    for kc in range(8):
                        self.mm(self.ps[bG][:, 0:n], self.R[s][:, kc, 256 + ii * 128: 256 + (ii + 1) * 128], h2T[:, kc, c0:c0 + n], kc == 0, kc == 7,
                                r=[("R", s)] + hk, w=[("ps", bG)])
                    ts_ = tslot % 2
                    tslot += 1
                    ta = self.tab[:, ts_, 0:n]
                    tg = self.tab[:, 2 + ts_, 0:n]
                    for (t_, bb, ch, tk) in ((ta, bA, hi, ("ta", ts_)), (tg, bG, 22 + hi, ("tg", ts_))):
                        pa = self.ps[bb][:, 0:n]
                        self.act(t_, pa, AF.Identity, r=[("ps", bb), "vecF"], w=[tk], scale=cv[:, ch, 1:2], bias=cv[:, ch, 3:4])
                        if len(ft["segs"]) == 2:
                            L = 256
                            p3 = pa.rearrange("p (s l) -> p s l", s=2)
                            t3 = t_.rearrange("p (s l) -> p s l", s=2)
                            self.stt(t3[:, :, 1:L], p3[:, :, 0:L - 1], cv[:, ch, 0:1], t3[:, :, 1:L], ALU.mult, ALU.add,
                                     r=[("ps", bb), tk, "vecF"], w=[tk])
                            self.stt(t3[:, :, 0:L - 1], p3[:, :, 1:L], cv[:, ch, 2:3], t3[:, :, 0:L - 1], ALU.mult, ALU.add,
                                     r=[("ps", bb), tk, "vecF"], w=[tk])
                        else:
                            self.stt(t_[:, 1:n], pa[:, 0:n - 1], cv[:, ch, 0:1], t_[:, 1:n], ALU.mult, ALU.add,
                                     r=[("ps", bb), tk, "vecF"], w=[tk])
                            self.stt(t_[:, 0:n - 1], pa[:, 1:n], cv[:, ch, 2:3], t_[:, 0:n - 1], ALU.mult, ALU.add,
                                     r=[("ps", bb), tk, "vecF"], w=[tk])
                    self.act(ta, ta, AF.Silu, r=[("ta", ts_)], w=[("ta", ts_)])
                    o0, no, oc0 = ft["o0"], ft["no"], ft["oc0"]
                    self.tt(self.actT[:, hi, oc0:oc0 + no], ta[:, o0:o0 + no], tg[:, o0:o0 + no], ALU.mult,
                            r=[("ta", ts_), ("tg", ts_)], w=[("actT", hi)])
        self.mod_rows(cond, 1)
        wd = self.w_down.rearrange("(kc p) c -> p kc c", p=128)
        ykeys = ["ybuf%d" % i for i in range(6)]
        P.alias(ykeys, ["pT", "T1", "T2", ("ta", 0), ("ta", 1), ("tg", 0), ("tg", 1)])
        yslot = 0
        f2i = [0]
        pfs = list(G.get("prefetch_steps", []))
        if pfs:
            pfs.pop(0)()
        for j in range(2):
            banks = [self.bank("main") for _ in own]
            for u in range(3):
                nk_ = 8 if u < 2 else 6
                f2i[0] += 1
                su = self.unit([(lambda R, nk_=nk_: R[:, 0:nk_, :], wd[:, 8 * u:8 * u + nk_, j * 512:(j + 1) * 512])])
                for oi in range(len(own)):
                    b = banks[oi]
                    for kk in range(nk_):
                        k = 8 * u + kk
                        self.mm(self.ps[b][:, :], self.actT[:, k, oi * 128:(oi + 1) * 128], self.R[su][:, kk, :], k == 0, k == 21,
                                r=[("R", su), ("actT", k)], w=[("ps", b)], inc=(kk == nk_ - 1))
                if pfs:
                    pfs.pop(0)()
            for oi, xi in enumerate(own):
                b = banks[oi]
                ys = yslot % 6
                yslot += 1
                yb = self.ybuf[:, ys, :]
                self.tt(yb, self.ps[b][:, :], self.GF[:, j * 512:(j + 1) * 512], ALU.mult, r=[("ps", b), "GF"], w=["ybuf%d" % ys])
                self.tt(yb, yb, self.xres[:, xi, j * 512:(j + 1) * 512], ALU.add, r=["ybuf%d" % ys, ("x", xi)], w=["ybuf%d" % ys])
                if kind == "p":
                    dst = self.yp[G["g"] * 512 + oi * 128: G["g"] * 512 + (oi + 1) * 128, j * 512:(j + 1) * 512]
                else:
                    dst = self.ys[oi * 128:(oi + 1) * 128, j * 512:(j + 1) * 512]
                self.st("yb%d" % ys, dst, yb, r=["ybuf%d" % ys])
        if G.get("post_f2") is not None:
            G["post_f2"]()
        P.alias([("KT", c) for c in range(9)] + [("QT", c) for c in range(6)] + [("PTr", c) for c in range(4)], actkeys)
        P.alias(["pT", "T1", "T2", ("ta", 0), ("ta", 1), ("tg", 0), ("tg", 1)], ykeys + [("ta", 0), ("ta", 1), ("tg", 0), ("tg", 1)])

    def _norm2(self, i, xc, m, G2, S2):
        keys = [("hT", c) for c in range(xc // 128, (xc + m - 1) // 128 + 1)]
        x_ap = self.xres[0:m, i, :]
        si = self.stat(1)
        st = self.small[0:m, si:si + 1]
        skey = ("small", si)
        slot = i % 2
        self.act(self.junk[0:m, :], x_ap, AF.Square, r=[("x", i)], w=["junk", skey], accum=st)
        self.rstd_from_ssq(st, st, float(D), m, 1, [skey])
        xh = self.xh[0:m, slot, :]
        hk = ("xh", slot)
        self.ts(xh, x_ap, st, None, ALU.mult, None, r=[skey, ("x", i)], w=[hk])
        for fc in range(8):
            b = self.bank("tr")
            self.tr(self.ps[b][:, 0:m], xh[:, fc * 128:(fc + 1) * 128], self.identF[0:m, 0:m], r=[hk, "identF"], w=[("ps", b)])
            if fc % 2 == 0:
                self.act(self.hT[:, fc, xc:xc + m], self.ps[b][:, 0:m], AF.Identity, r=[("ps", b), "G2F", "modF"], w=keys,
                         scale=G2[:, fc:fc + 1], bias=S2[:, fc:fc + 1])
            else:
                self.ts(self.hT[:, fc, xc:xc + m], self.ps[b][:, 0:m], G2[:, fc:fc + 1], S2[:, fc:fc + 1], ALU.mult, ALU.add,
                        r=[("ps", b), "G2F", "modF"], w=keys)

    def attention(self, G, xch, ntiles, inter=()):
        kind = G["kind"]
        inter = list(inter)
        units, flat = [], []
        for h in range(8):
            for ti, T in enumerate(ntiles):
                chunks = [(c, True) for c in T["kv"]] if T["local"] else [(c, False) for c in T["kv"]]
                chunks += [(c, False) for c in T["ctx"]]
                u = dict(h=h, T=T, chunks=chunks, nck=len(chunks), first=True, bO=None, last_tile=(ti == len(ntiles) - 1))
                units.append(u)
                for ci in range(len(chunks)):
                    flat.append((u, ci))
        eb_loaded = set()
        slots = {}

        def emit_S(k):
            u, ci = flat[k]
            h, T = u["h"], u["T"]
            j = h // 2
            x0, n = T["x0"], T["n"]
            qkeys = [("QT", i) for i in T["ch"]]
            c, loc = u["chunks"][ci]
            if kind == "s" and h not in eb_loaded:
                eb_loaded.add(h)
                es_ = h % 2
                self.ld("EB%d" % es_, self.EB[:, es_, 4 * 64:22 * 64], self.t3_d[h][:, 4 * 64:22 * 64], w=[("EB", es_)], q="pool")
                self.act(self.EB[:, es_, 4 * 64:22 * 64], self.EB[:, es_, 4 * 64:22 * 64], AF.Exp, r=[("EB", es_)], w=[("EB", es_)])
            b = self.bank("S")
            lo, hi = T.get("lo", 0), T.get("hi", n)
            if loc and c == 8:
                lo = 256
            self.mm(self.ps[b][:, lo:hi], self.KT[:, j, c * 128:(c + 1) * 128], self.Qz[:, h, x0 + lo:x0 + hi],
                    True, not loc, r=[("KT", c)] + qkeys, w=[("ps", b)], skip=loc)
            if loc:
                r0 = T["r0"]
                rhs = self.negm[:, c * 10 + r0: c * 10 + r0 + 5].unsqueeze(2).broadcast_to([128, 5, 64])
                self.mm(self.ps[b][:, 0:n], self.half[:, :], rhs, False, True, r=["half", "negm"], w=[("ps", b)], skip=True)
            slot = self.pt_i % 4
            self.pt_i += 1
            pt = self.PTr[:, slot, lo:hi]
            self.act(pt, self.ps[b][:, lo:hi], AF.Exp, r=[("ps", b)], w=[("PTr", slot)], scale=0.125)
            if loc:
                e0 = T["r0"] - 2 * c + 16
                es_ = h % 2
                self.tt(pt, pt, self.EB[:, es_, e0 * 64 + lo:e0 * 64 + hi], ALU.mult, r=[("PTr", slot), ("EB", es_)], w=[("PTr", slot)])
            slots[k] = slot

        def emit_PV(k):
            u, ci = flat[k]
            h, T = u["h"], u["T"]
            x0 = T["x0"]
            c, loc = u["chunks"][ci]
            slot = slots.pop(k)
            if u["bO"] is None:
                u["bO"] = self.bank("O")
            bO = u["bO"]
            order = sorted(range(len(T["ch"])), key=lambda q: -xch[T["ch"][q]][1])
            if loc and c == 8:
                order = [q for q in order if T["ch"][q] == 5]
            for oi_, qi in enumerate(order):
                xc, m = xch[T["ch"][qi]]
                last = oi_ == len(order) - 1
                self.mm(self.ps[bO][:, qi * 65:(qi + 1) * 65], self.PTr[:, slot, xc - x0: xc - x0 + 128], self.Vaug[:, c, h, :],
                        u["first"], (ci == u["nck"] - 1),
                        r=[("PTr", slot), ("V", c)], w=[("ps", bO)], inc=last, skip=True)
                u["first"] = False
            if ci == u["nck"] - 1:
                nch = len(T["ch"])
                ch0 = T["ch"][0]
                si = self.stat(nch)
                rc = self.small[:, si:si + nch].unsqueeze(2)
                O3 = self.ps[bO][:, 0:nch * 65].rearrange("p (c d) -> p c d", d=65)
                self.recip(rc, O3[:, :, 64:65], r=[("ps", bO)], w=[("small", si)])
                self.tt(self.na[:, ch0:ch0 + nch, h * 64:(h + 1) * 64], O3[:, :, 0:64], rc.broadcast_to([128, nch, 64]), ALU.mult,
                        r=[("ps", bO), ("small", si)], w=[("na", xi) for xi in T["ch"]])

        LA = 3
        nf = len(flat)
        rate = len(inter) / max(1.0, nf - 6.0)
        acc = 0.0
        for k in range(min(LA, nf)):
            emit_S(k)
        for k in range(nf):
            if k + LA < nf:
                emit_S(k + LA)
            emit_PV(k)
            acc += rate
            while acc >= 1.0 and inter:
                inter.pop(0)()
                acc -= 1.0
        while inter:
            inter.pop(0)()

    def pool_stage(self, G, s, XOFF, NX, hkey):
        kind = G["kind"]
        if kind == "p":
            nseg, L, LP = 2, 256, 272
            tabL, tabR = 0, 1
        else:
            nseg, L, LP = 1, 640, 656
            tabL, tabR = 2, 3
        pT4 = self.PF[:, 0:4 * nseg * LP].rearrange("p (g s c) -> p g s c", g=4, s=nseg)
        if kind == "p":
            for g in range(4):
                self.memset(pT4[:, g, :, 0:8], 0.0, w=["pT"])
                self.memset(pT4[:, g, :, 8 + L:LP], 0.0, w=["pT"])
        for g in range(4):
            if kind == "p":
                b = self.bank("main")
                for kc in range(8):
                    self.mm(self.ps[b][:, :], self.R[s][:, kc, g * 128:(g + 1) * 128], self.hT[:, kc, 0:512], kc == 0, kc == 7,
                            r=[("R", s)] + hkey(0, 512), w=[("ps", b)])
                self.act(pT4[:, g, :, 8:8 + L], self.ps[b][:, :].rearrange("p (s l) -> p s l", s=2), AF.Identity, r=[("ps", b)], w=["pT"])
            else:
                for nt in range(2):
                    b = self.bank("main")
                    c0 = XOFF - 8 + nt * 328
                    for kc in range(8):
                        self.mm(self.ps[b][:, 0:328], self.R[s][:, kc, g * 128:(g + 1) * 128], self.hT[:, kc, c0:c0 + 328], kc == 0, kc == 7,
                                r=[("R", s)] + hkey(c0, 328), w=[("ps", b)])
                    self.tt(pT4[:, g, 0, nt * 328:(nt + 1) * 328], self.ps[b][:, 0:328], self.pmask[:, nt * 328:(nt + 1) * 328], ALU.mult,
                            r=[("ps", b), "pmask"], w=["pT"])
        T1 = self.PF[:, 2624:2624 + nseg * LP].rearrange("p (s c) -> p s c", s=nseg)
        T2 = self.PF[:, 3280:3280 + nseg * LP].rearrange("p (s c) -> p s c", s=nseg)
        tabv = self.ptab[:, :].rearrange("p (t g d) -> p t g d", t=4, g=4)

        from functools import partial
        ops = []

        def chain(g):
            p = pT4[:, g]
            w = POOL_W[g]
            ops.append(partial(self.tt, T1[:, :, 1:LP], p[:, :, 0:LP - 1], p[:, :, 1:LP], ALU.add, r=["pT"], w=["T1"]))
            W = T1
            wk = "T1"
            if g >= 1:
                ops.append(partial(self.tt, T2[:, :, 2:LP - 1], T1[:, :, 1:LP - 2], T1[:, :, 3:LP], ALU.add, r=["T1"], w=["T2"]))
                W, wk = T2, "T2"
            if g >= 2:
                ops.append(partial(self.tt, T1[:, :, 4:LP - 3], T2[:, :, 2:LP - 5], T2[:, :, 6:LP - 1], ALU.add, r=["T2"], w=["T1"]))
                W, wk = T1, "T1"
            if g >= 3:
                ops.append(partial(self.tt, T2[:, :, 8:LP - 7], T1[:, :, 4:LP - 11], T1[:, :, 12:LP - 3], ALU.add, r=["T1"], w=["T2"]))
                W, wk = T2, "T2"
            po = self.pooledT[:, g, 0:nseg * L].rearrange("p (s l) -> p s l", s=nseg)
            ops.append(partial(self.stt, po, W[:, :, 8:8 + L], 1.0 / w, p[:, :, 8:8 + L], ALU.mult, ALU.subtract,
                               r=[wk, "pT"], w=[("pl", g, 0), ("pl", g, 1)]))
            if kind == "p":
                edges = [(0, tabL), (L - 8, tabR)]
            else:
                edges = [(64, tabL), (568, tabR)]
            for (lc, tb) in edges:
                te = self.tmpE[:, 0:nseg * 8].rearrange("p (s d) -> p s d", s=nseg)
                tv = tabv[:, tb, g, :].unsqueeze(1).broadcast_to([128, nseg, 8])
                ops.append(partial(self.tt, te, W[:, :, 8 + lc:8 + lc + 8], tv, ALU.mult, r=[wk, "ptab"], w=["tmpE"]))
                ops.append(partial(self.tt, po[:, :, lc:lc + 8], te, p[:, :, 8 + lc:8 + lc + 8], ALU.subtract,
                                   r=["tmpE", "pT"], w=[("pl", g, 0), ("pl", g, 1)]))

        for g in range(4):
            chain(g)
        return ops

    def pool_mix(self, G):
        kind = G["kind"]
        mt = [(0, 512)] if kind == "p" else [(63, 257), (320, 257)]
        for g in range(4):
            for ti, (x0, n) in enumerate(mt):
                b = self.bank("main")
                self.mm(self.ps[b][:, 0:n], self.poolw[:, g, :], self.pooledT[:, g, x0:x0 + n], True, True, r=["poolw", ("pl", g, ti)], w=[("ps", b)])
                self.act(self.pooloutT[:, g, x0:x0 + n], self.ps[b][:, 0:n], AF.Identity, r=[("ps", b), "vecF"], w=[("pl", g, ti)],
                         scale=self.vecF[:, 64 + g:65 + g])


Builder.pt_i = 0


def _feat(v):
    v = np.asarray(v, np.float32)
    return np.ascontiguousarray(v.reshape(-1, 128).T)


def _geometry(j):
    R0 = 8 * j
    negm = np.full((2, 9, 10), NEG, np.float32)
    for c in range(9):
        for hf in range(2):
            ak = R0 - 5 + 2 * c + hf
            for r in range(10):
                aq = R0 - 1 + r
                if not (0 <= aq < 32) or not (0 <= ak < 32):
                    continue
                st = min(max(aq - 4, 0), 24)
                if st <= ak < st + 8:
                    negm[hf, c, r] = 0.0
    ptab = np.zeros((4, 4, 8), np.float32)
    for g, w in enumerate(POOL_W):
        for d in range(8):
            ptab[0, g, d] = 1.0 / (w // 2 + min(d, w // 2))
            dd = 7 - d
            ptab[1, g, d] = 1.0 / (w // 2 + min(w // 2, dd + 1))
            ptab[2, g, d] = ptab[0, g, d] if j == 0 else 1.0 / w
            ptab[3, g, d] = ptab[1, g, d] if j == 3 else 1.0 / w
    tok = (R0 - 5) * 64 + 248 + np.arange(656)
    pmask = ((tok >= 0) & (tok < 2048)).astype(np.float32)
    hval = np.array([1.0 if j > 0 else 0.0, 1.0 if j < 3 else 0.0], np.float32)
    return negm, ptab, pmask, hval


def _t3(rpb):
    t3 = np.full((8, 2, 64, NE, 64), NEG, np.float32)
    kc = np.arange(64)[:, None]
    qc = np.arange(64)[None, :]
    ws = np.clip(qc - 8, 0, 48)
    colv = (kc >= ws) & (kc < ws + 16)
    dc = np.clip(kc - qc + 15, 0, 30)
    for hf in range(2):
        for e in range(NE):
            i = hf + 19 - e
            if 0 <= i <= 14:
                vals = rpb[:, i][:, dc]
                t3[:, hf, :, e, :] = np.where(colv[None], vals, NEG)
    return np.ascontiguousarray(t3.reshape(8, 128, NE * 64))


_NC_CACHE = {}


def make_in_maps(x_prompt, x_sample, cache_k, cache_v, c, c_ctx, norm_mix_g, norm_ffn_g, w_mod, b_mod,
                 w_in, q_norm_g, k_norm_g, pool_w, pool_scale, na_rpb, w_pool_proj, w_na_proj, w_o,
                 w_up, ffn_conv_w, ffn_conv_b, w_down):
    f = lambda a: np.ascontiguousarray(np.asarray(a, np.float32))
    x_prompt, x_sample, cache_k, cache_v = f(x_prompt), f(x_sample), f(cache_k), f(cache_v)
    c, c_ctx = f(c), f(c_ctx)
    shared = dict(w_mod=f(w_mod[0]), w_in=f(w_in[0]), pool_w=f(pool_w[0]), w_pp=f(w_pool_proj[0]), w_np=f(w_na_proj[0]),
                  w_o=f(w_o[0]), w_up=f(w_up[0]), w_down=f(w_down[0]))
    convF = np.stack([_feat(ffn_conv_w[0, 0]), _feat(ffn_conv_w[0, 1]), _feat(ffn_conv_w[0, 2]), _feat(ffn_conv_b[0])], axis=2)
    gqk = np.ascontiguousarray(np.broadcast_to(np.concatenate([f(q_norm_g[0]), f(k_norm_g[0])])[None, :], (128, 128)))
    bm = f(b_mod[0])
    bgagf = np.ascontiguousarray(np.broadcast_to(np.concatenate([bm[2048:3072], bm[5120:6144]])[None, :], (2, 2048)))
    ident = np.eye(128, dtype=np.float32)
    half = np.zeros((128, 128), np.float32)
    half[0:2] = np.repeat(np.eye(2, dtype=np.float32), 64, axis=1)
    t3 = _t3(f(na_rpb[0]))
    ones1 = np.ones((1, 128), np.float32)
    in_maps = []
    for i in range(8):
        b, j = i // 4, i % 4
        R0 = 8 * j
        lo, hi = (R0 - 5) * 64, (R0 + 13) * 64
        xs = np.zeros((1152, D), np.float32)
        a, e = max(lo, 0), min(hi, 2048)
        xs[a - lo:e - lo] = x_sample[b, a:e]
        negm, ptab, pmask, hval = _geometry(j)
        condF = np.stack([_feat(c_ctx), _feat(c[b])], axis=2)
        vecF = np.concatenate([_feat(norm_mix_g[0]), _feat(norm_ffn_g[0]), _feat(bm), _feat(pool_scale[0]),
                               convF.reshape(128, 176), condF.reshape(128, 16)], axis=1)
        m = dict(shared)
        m.update(xp=np.ascontiguousarray(x_prompt[4 * i:4 * i + 4].reshape(1024, D)), xs=xs,
                 ck=cache_k[b, 0], cv=cache_v[b, 0], vecF=np.ascontiguousarray(vecF, dtype=np.float32), gqk=gqk, bgagf=bgagf,
                 ident=ident, half=half, t3=t3, negm=np.ascontiguousarray(np.concatenate([negm.reshape(2, 90), np.zeros((126, 90), np.float32)], axis=0)),
                 ptab=np.ascontiguousarray(np.broadcast_to(ptab.reshape(1, 128), (128, 128))),
                 pmask=np.ascontiguousarray(np.broadcast_to(pmask[None, :], (128, 656))),
                 hval=np.ascontiguousarray(np.broadcast_to(hval[None, :], (128, 2))), ones1=ones1)
        in_maps.append(m)
    return in_maps


def kernel(**inputs):
    in_maps = make_in_maps(**inputs)
    if "nc" not in _NC_CACHE:
        _NC_CACHE["nc"] = Builder().build()
    nc = _NC_CACHE["nc"]
    res = run_bass_kernel_spmd(nc, in_maps, core_ids=list(range(8)))
    rs = res.results
    y_prompt = np.concatenate([r["yp"].reshape(4, 256, D) for r in rs], axis=0)
    y_sample = np.stack([np.concatenate([rs[4 * b + j]["ys"] for j in range(4)], axis=0) for b in range(2)], axis=0)
    new_k = np.concatenate([r["nk"] for r in rs], axis=0)[:, None]
    new_v = np.concatenate([r["nv"] for r in rs], axis=0)[:, None]
    return (y_prompt.astype(np.float32), y_sample.astype(np.float32),
            np.ascontiguousarray(new_k, dtype=np.float32), np.ascontiguousarray(new_v, dtype=np.float32))
```
